# Optimizing a Trainium2 kernel written in Bass

```python
import math
import jax, jax.numpy as jnp
from jax import lax
import numpy as np

D_MODEL = 1024
BATCH = 8
SEQ = 4096
DEPTH = 1

DSWA_CONFIGS = ((128, 1), (512, 4), (2048, 16))
N_GROUPS = len(DSWA_CONFIGS)
DSWA_HEADS = 4
DSWA_HEAD_DIM = 128
DSWA_WIDTH = N_GROUPS * DSWA_HEADS * DSWA_HEAD_DIM
DSWA_OUT = DSWA_HEADS * DSWA_HEAD_DIM
DSWA_BLK = 128
D_RNN = 1024
LRU_BLOCKS = 16
LRU_BW = D_RNN // LRU_BLOCKS
CONV_W = 4
LRU_C = 8.0
MEM_LEN = 256
MEM_HEADS = 4
MEM_HEAD_DIM = 128
MEM_WIDTH = MEM_HEADS * MEM_HEAD_DIM
N_BRANCHES = 3
IN_SPLITS = (DSWA_WIDTH, DSWA_WIDTH, DSWA_WIDTH, D_RNN, D_RNN, MEM_WIDTH, N_BRANCHES * D_MODEL)
D_IN = sum(IN_SPLITS)
PEER_HEADS = 8
PEER_KEY_DIM = 256
PEER_HALF = PEER_KEY_DIM // 2
N_KEYS = 128
N_EXPERTS = N_KEYS * N_KEYS
PEER_TOPK = 16
PEER_CHUNK = 128
ALPHA = (2.0 * DEPTH) ** 0.25
BETA = (8.0 * DEPTH) ** -0.25
LN_EPS = 1e-5
NEG_INF = -1e30

kernel_name = "hybrid_dswa_rglru_mem_peer_deepnorm"


def _layernorm(h, g, b):
    h = h.astype(jnp.float32)
    mu = jnp.mean(h, axis=-1, keepdims=True)
    var = jnp.mean(jnp.square(h - mu), axis=-1, keepdims=True)
    return (h - mu) * lax.rsqrt(var + LN_EPS) * g.astype(jnp.float32) + b.astype(jnp.float32)


def _dilated_window_attention(q, k, v, window, dilation):
    B, S, H, hd = q.shape
    steps = window // dilation
    assert steps <= DSWA_BLK
    span = dilation * DSWA_BLK
    sp = -(-S // span) * span
    m_len = sp // dilation
    nb = m_len // DSWA_BLK

    def to_sub(t):
        t = jnp.pad(t.astype(jnp.float32), ((0, 0), (0, sp - S), (0, 0), (0, 0)))
        t = t.reshape(B, m_len, dilation, H, hd).transpose(0, 2, 1, 3, 4)
        return t.reshape(B, dilation, nb, DSWA_BLK, H, hd)

    def with_prev(t):
        prev = jnp.pad(t[:, :, :-1], ((0, 0), (0, 0), (1, 0), (0, 0), (0, 0), (0, 0)))
        return jnp.concatenate([prev, t], axis=3)

    qb = to_sub(q)
    kw = with_prev(to_sub(k))
    vw = with_prev(to_sub(v))
    s = jnp.einsum('brnqhd,brnkhd->brnhqk', qb, kw) * (1.0 / math.sqrt(hd))
    qi = jnp.arange(DSWA_BLK)[:, None]
    kj = jnp.arange(2 * DSWA_BLK)[None, :]
    dist = DSWA_BLK + qi - kj
    band = (dist >= 0) & (dist <= steps)
    has_prev = (jnp.arange(nb) > 0)[:, None, None] | (kj >= DSWA_BLK)[None]
    valid = band[None] & has_prev
    s = jnp.where(valid[None, None, :, None], s, NEG_INF)
    lse = jax.nn.logsumexp(s, axis=-1)
    p = jnp.exp(s - lse[..., None])
    o = jnp.einsum('brnhqk,brnkhd->brnqhd', p, vw)
    o = o.reshape(B, dilation, m_len, H, hd).transpose(0, 2, 1, 3, 4).reshape(B, sp, H, hd)[:, :S]
    lse = lse.transpose(0, 1, 2, 4, 3).reshape(B, dilation, m_len, H)
    lse = lse.transpose(0, 2, 1, 3).reshape(B, sp, H)[:, :S]
    return o, lse


def _linear_recurrence(a, b):
    def comb(left, right):
        al, bl = left
        ar, br = right
        return al * ar, ar * bl + br
    _, h = lax.associative_scan(comb, (a, b), axis=1)
    return h


def _rglru_branch(xr, yg, conv_w, conv_b, wa, ba, wx, bx, lam):
    B, S, _ = xr.shape
    xc = lax.conv_general_dilated(
        xr.astype(jnp.float32), conv_w.astype(jnp.float32)[:, None, :],
        window_strides=(1,), padding=[(CONV_W - 1, 0)],
        dimension_numbers=('NWC', 'WIO', 'NWC'), feature_group_count=D_RNN,
    ) + conv_b.astype(jnp.float32)
    xh = xc.reshape(B, S, LRU_BLOCKS, LRU_BW)
    r = jax.nn.sigmoid(jnp.einsum('bsni,nij->bsnj', xh, wa.astype(jnp.float32)).reshape(B, S, D_RNN) + ba)
    i = jax.nn.sigmoid(jnp.einsum('bsni,nij->bsnj', xh, wx.astype(jnp.float32)).reshape(B, S, D_RNN) + bx)
    log_a = -LRU_C * r * jax.nn.softplus(-lam.astype(jnp.float32))
    a = jnp.exp(log_a)
    mult = jnp.sqrt(-jnp.expm1(2.0 * log_a))
    h = _linear_recurrence(a, mult * i * xc)
    return h * jax.nn.gelu(yg.astype(jnp.float32))


def _memory_attention(mq, mem, w_mem_kv):
    B, S, _ = mq.shape
    q = mq.astype(jnp.float32).reshape(B, S, MEM_HEADS, MEM_HEAD_DIM)
    kv = jnp.einsum('bmd,de->bme', mem.astype(jnp.float32), w_mem_kv.astype(jnp.float32))
    k, v = jnp.split(kv.reshape(B, MEM_LEN, 2, MEM_HEADS, MEM_HEAD_DIM), 2, axis=2)
    k, v = k[:, :, 0], v[:, :, 0]
    s = jnp.einsum('bshd,bmhd->bhsm', q, k) * (1.0 / math.sqrt(MEM_HEAD_DIM))
    p = jax.nn.softmax(s, axis=-1)
    return jnp.einsum('bhsm,bmhd->bshd', p, v).reshape(B, S, MEM_WIDTH)


def _peer(x, wq, keys, u, v):
    B, S, D = x.shape
    q = jnp.einsum('bsd,de->bse', x, wq.astype(jnp.float32)).reshape(B, S, PEER_HEADS, 2, PEER_HALF)
    sc = jnp.einsum('bshpc,hpnc->bshpn', q, keys.astype(jnp.float32))
    s_top, i_top = lax.top_k(sc, PEER_TOPK)
    cand = s_top[..., 0, :, None] + s_top[..., 1, None, :]
    c_s, c_i = lax.top_k(cand.reshape(B, S, PEER_HEADS, PEER_TOPK * PEER_TOPK), PEER_TOPK)
    ia = jnp.take_along_axis(i_top[..., 0, :], c_i // PEER_TOPK, axis=-1)
    ib = jnp.take_along_axis(i_top[..., 1, :], c_i % PEER_TOPK, axis=-1)
    ids = ia * N_KEYS + ib
    g = jax.nn.softmax(c_s, axis=-1)
    n_chunks = (B * S) // PEER_CHUNK
    xs = x.reshape(n_chunks, PEER_CHUNK, D)
    ids = ids.reshape(n_chunks, PEER_CHUNK, PEER_HEADS, PEER_TOPK)
    g = g.reshape(n_chunks, PEER_CHUNK, PEER_HEADS, PEER_TOPK)

    def block(args):
        xc, idc, gc = args
        act = jax.nn.gelu(jnp.einsum('thkd,td->thk', u[idc].astype(jnp.float32), xc), approximate=False)
        return jnp.einsum('thk,thkd->td', gc * act, v[idc].astype(jnp.float32))

    return lax.map(block, (xs, ids, g)).reshape(B, S, D)


def _hybrid_layer(x, mem, w_in, b_gate, conv_w, conv_b, lru_wa, lru_ba, lru_wx, lru_bx, lru_lambda,
                  w_mem_kv, w_br_attn, w_br_lru, w_br_mem, w_out, ln1_g, ln1_b,
                  peer_wq, peer_keys, peer_u, peer_v, ln2_g, ln2_b):
    B, S, D = x.shape
    z = jnp.einsum('bsd,de->bse', x, w_in.astype(jnp.float32))
    offs = [0]
    for w in IN_SPLITS:
        offs.append(offs[-1] + w)
    q, k, v, xr, yg, mq, gl = [z[..., offs[j]:offs[j + 1]] for j in range(len(IN_SPLITS))]

    hs = (B, S, N_GROUPS, DSWA_HEADS, DSWA_HEAD_DIM)
    q, k, v = q.reshape(hs), k.reshape(hs), v.reshape(hs)
    outs, lses = [], []
    for gi, (win, dil) in enumerate(DSWA_CONFIGS):
        o_g, l_g = _dilated_window_attention(q[:, :, gi], k[:, :, gi], v[:, :, gi], win, dil)
        outs.append(o_g)
        lses.append(l_g)
    wgt = jax.nn.softmax(jnp.stack(lses, axis=0), axis=0)
    attn = jnp.sum(wgt[..., None] * jnp.stack(outs, axis=0), axis=0).reshape(B, S, DSWA_OUT)

    rec = _rglru_branch(xr, yg, conv_w, conv_b, lru_wa, lru_ba, lru_wx, lru_bx, lru_lambda)

    memo = _memory_attention(mq, mem, w_mem_kv)

    gates = jax.nn.sigmoid(gl.reshape(B, S, N_BRANCHES, D) + b_gate.astype(jnp.float32))
    merged = (gates[:, :, 0] * (attn @ w_br_attn.astype(jnp.float32))
              + gates[:, :, 1] * (rec @ w_br_lru.astype(jnp.float32))
              + gates[:, :, 2] * (memo @ w_br_mem.astype(jnp.float32)))
    mix = merged @ w_out.astype(jnp.float32)
    x1 = _layernorm(ALPHA * x + mix, ln1_g, ln1_b)

    ffn = _peer(x1, peer_wq, peer_keys, peer_u, peer_v)
    return _layernorm(ALPHA * x1 + ffn, ln2_g, ln2_b)


def setup_inputs(seed: int = 0) -> dict:
    key = jax.random.key(seed)
    ks = jax.random.split(key, 26)
    f32 = jnp.float32
    L, D = DEPTH, D_MODEL

    def nrm(k, shape, scale):
        return jax.random.normal(k, shape, f32) * scale

    a0 = jax.random.uniform(ks[10], (L, D_RNN), f32, 0.9, 0.999)
    s0 = a0 ** (1.0 / LRU_C)
    lam = jnp.log(s0) - jnp.log1p(-s0)
    return {
        'x': nrm(ks[0], (BATCH, SEQ, D), 1.0),
        'mem': nrm(ks[1], (BATCH, MEM_LEN, D), 1.0),
        'w_in': nrm(ks[2], (L, D, D_IN), D ** -0.5),
        'b_gate': nrm(ks[3], (L, N_BRANCHES, D), 0.1),
        'conv_w': nrm(ks[4], (L, CONV_W, D_RNN), CONV_W ** -0.5),
        'conv_b': nrm(ks[5], (L, D_RNN), 0.01),
        'lru_wa': nrm(ks[6], (L, LRU_BLOCKS, LRU_BW, LRU_BW), LRU_BW ** -0.5),
        'lru_ba': nrm(ks[7], (L, D_RNN), 0.01),
        'lru_wx': nrm(ks[8], (L, LRU_BLOCKS, LRU_BW, LRU_BW), LRU_BW ** -0.5),
        'lru_bx': nrm(ks[9], (L, D_RNN), 0.01),
        'lru_lambda': lam,
        'w_mem_kv': nrm(ks[11], (L, D, 2 * MEM_WIDTH), D ** -0.5),
        'w_br_attn': nrm(ks[12], (L, DSWA_OUT, D), BETA * DSWA_OUT ** -0.5),
        'w_br_lru': nrm(ks[13], (L, D_RNN, D), BETA * D_RNN ** -0.5),
        'w_br_mem': nrm(ks[14], (L, MEM_WIDTH, D), BETA * MEM_WIDTH ** -0.5),
        'w_out': nrm(ks[15], (L, D, D), BETA * D ** -0.5),
        'ln1_g': 1.0 + nrm(ks[16], (L, D), 0.02),
        'ln1_b': nrm(ks[17], (L, D), 0.02),
        'peer_wq': nrm(ks[18], (L, D, PEER_HEADS * PEER_KEY_DIM), D ** -0.5),
        'peer_keys': nrm(ks[19], (L, PEER_HEADS, 2, N_KEYS, PEER_HALF), PEER_HALF ** -0.5),
        'peer_u': nrm(ks[20], (L, N_EXPERTS, D), D ** -0.5),
        'peer_v': nrm(ks[21], (L, N_EXPERTS, D), BETA * PEER_HEADS ** -0.5),
        'ln2_g': 1.0 + nrm(ks[22], (L, D), 0.02),
        'ln2_b': nrm(ks[23], (L, D), 0.02),
    }


def reference(x, mem, w_in, b_gate, conv_w, conv_b, lru_wa, lru_ba, lru_wx, lru_bx, lru_lambda,
              w_mem_kv, w_br_attn, w_br_lru, w_br_mem, w_out, ln1_g, ln1_b,
              peer_wq, peer_keys, peer_u, peer_v, ln2_g, ln2_b):
    h = x.astype(jnp.float32)
    for l in range(DEPTH):
        h = _hybrid_layer(h, mem, w_in[l], b_gate[l], conv_w[l], conv_b[l], lru_wa[l], lru_ba[l],
                          lru_wx[l], lru_bx[l], lru_lambda[l], w_mem_kv[l], w_br_attn[l], w_br_lru[l],
                          w_br_mem[l], w_out[l], ln1_g[l], ln1_b[l], peer_wq[l], peer_keys[l],
                          peer_u[l], peer_v[l], ln2_g[l], ln2_b[l])
    return h.astype(x.dtype)
```

```python
import math
import os
DBG_P1 = int(os.environ.get('DBG_P1', '99'))
from contextlib import ExitStack

import numpy as np
import concourse.bass as bass
import concourse.mybir as mybir
from concourse.bass_utils import run_bass_kernel_spmd

F32 = mybir.dt.float32
BF16 = mybir.dt.bfloat16
U32 = mybir.dt.uint32
AF = mybir.ActivationFunctionType
ALU = mybir.AluOpType
AX = mybir.AxisListType

S_LEN = 4096
D = 1024
NT = S_LEN // 128
ALPHA = 2.0 ** 0.25
LN_EPS = 1e-5
QOFF, KOFF, VOFF, XROFF, YGOFF, MQOFF, GLOFF = 0, 1536, 3072, 4608, 5632, 6656, 7168
DILS = (1, 4, 16)
ENGS = ("pe", "act", "dve", "pool", "sp")


class Trk:
    __slots__ = ("w", "r")

    def __init__(self):
        self.w = None
        self.r = {}


def trks(n):
    return [Trk() for _ in range(n)]


class Sched:
    def __init__(self, nc, es):
        self.nc = nc
        self.semh = {}
        for e in ("pe", "act", "dve", "pool"):
            self.semh[e] = es.enter_context(nc.semaphore("s_" + e))
        self.dq = {"sp": [], "pool": [], "act": []}
        nd = {"sp": 10, "pool": 8, "act": 2}
        for q, n in nd.items():
            for i in range(n):
                k = "d_%s%d" % (q, i)
                self.semh[k] = es.enter_context(nc.semaphore(k))
                self.dq[q].append(k)
        self.cnt = {k: 0 for k in self.semh}
        self.rr = {q: 0 for q in self.dq}
        self.seen = {e: {} for e in ENGS}
        self.prog = {e: [] for e in ENGS}
        self.pending = {e: [] for e in ENGS}

    def _waits(self, eng, reads, writes, extra=(), strict=False):
        deps = {}

        def need(tok):
            if tok is None:
                return
            k, v = tok
            if deps.get(k, 0) < v:
                deps[k] = v
        for b in reads:
            need(b.w)
        for b in writes:
            if b.w is not None and (strict or b.w[0] != eng):
                need(b.w)
            for k, v in b.r.items():
                if strict or k != eng:
                    need((k, v))
        for tok in extra:
            need(tok)
        out = []
        sn = self.seen[eng]
        for k, v in deps.items():
            if sn.get(k, 0) < v:
                sn[k] = v
                out.append((k, v))
        return out

    def emit(self, eng, fn, reads=(), writes=(), sig=True):
        waits = self._waits(eng, reads, writes, strict=(eng != 'pe'))
        ticket = self.cnt[eng] + 1
        if sig:
            self.cnt[eng] = ticket
        self.prog[eng].append((waits, fn, (eng, 1) if sig else None))
        for b in reads:
            if b.r.get(eng, 0) < ticket:
                b.r[eng] = ticket
        for b in writes:
            b.w = (eng, ticket)
            b.r = {}

    def dma(self, q, out, in_, reads=(), writes=()):
        ks = self.dq[q]
        k = ks[self.rr[q] % len(ks)]
        self.rr[q] += 1
        extra = [(k, self.cnt[k])] if self.cnt[k] > 0 else []
        waits = self._waits(q, reads, writes, extra, strict=True)
        self.cnt[k] += 16
        v = self.cnt[k]
        self.prog[q].append((waits, lambda e: e.dma_start(out=out, in_=in_), (k, 16)))
        for b in reads:
            if b.r.get(k, 0) < v:
                b.r[k] = v
        for b in writes:
            b.w = (k, v)
            b.r = {}

    def barrier(self):
        for e in ENGS:
            waits = []
            for k, v in self.cnt.items():
                if v > 0 and k != e and self.seen[e].get(k, 0) < v:
                    self.seen[e][k] = v
                    waits.append((k, v))
            if waits:
                self.prog[e].append((waits, None, None))

    def flush(self):
        nc = self.nc
        semh = self.semh
        prog = self.prog

        def replay(name, e):
            for waits, fn, inc in prog[name]:
                for k, v in waits:
                    e.wait_ge(semh[k], v)
                if fn is None:
                    continue
                ins = fn(e)
                if inc is not None:
                    ins.then_inc(semh[inc[0]], inc[1])

        with nc.Block() as block:
            @block.tensor
            def _(e):
                replay("pe", e)

            @block.scalar
            def _(e):
                replay("act", e)

            @block.vector
            def _(e):
                replay("dve", e)

            @block.gpsimd
            def _(e):
                replay("pool", e)

            @block.sync
            def _(e):
                replay("sp", e)
        self.prog = {e: [] for e in ENGS}

    def mm(self, out, lhsT, rhs, start, stop, reads=(), writes=(), sig=None):
        if sig is None:
            sig = stop
        self.emit("pe", lambda e: e.matmul(out, lhsT, rhs, start=start, stop=stop), reads, writes, sig)

    def tr(self, out, in_, ident, reads=(), writes=()):
        self.emit("pe", lambda e: e.transpose(out, in_, ident), reads, writes)

    def actf(self, out, in_, func, reads=(), writes=(), bias=0.0, scale=1.0, accum=None, eng="act"):
        if accum is None:
            self.emit("act", lambda e: e.activation(out, in_, func, bias=bias, scale=scale), reads, writes)
        else:
            self.emit("act", lambda e: e.activation(out, in_, func, bias=bias, scale=scale, accum_out=accum),
                      reads, writes)

    def tt(self, eng, out, in0, in1, op, reads=(), writes=()):
        self.emit(eng, lambda e: e.tensor_tensor(out, in0, in1, op), reads, writes)

    def ts(self, eng, out, in0, s1, s2, op0, op1=None, reads=(), writes=()):
        if op1 is None:
            self.emit(eng, lambda e: e.tensor_scalar(out, in0, s1, None, op0), reads, writes)
        else:
            self.emit(eng, lambda e: e.tensor_scalar(out, in0, s1, s2, op0, op1), reads, writes)

    def stt(self, eng, out, in0, scalar, in1, op0, op1, reads=(), writes=()):
        self.emit(eng, lambda e: e.scalar_tensor_tensor(out, in0, scalar, in1, op0, op1), reads, writes)

    def cp(self, eng, out, in_, reads=(), writes=()):
        if eng == "act":
            self.emit("act", lambda e: e.copy(out, in_), reads, writes)
        else:
            self.emit(eng, lambda e: e.tensor_copy(out, in_), reads, writes)

    def memset(self, eng, ap, val, writes=()):
        self.emit(eng, lambda e: e.memset(ap, val), (), writes)


class Rec:
    def __init__(self, S):
        self.S = S
        self.ops = []

    def __getattr__(self, name):
        f = getattr(self.S, name)

        def wrap(*a, **k):
            self.ops.append(lambda: f(*a, **k))
        return wrap


class Ring:
    def __init__(self, items):
        self.items = items
        self.i = 0

    def next(self):
        it = self.items[self.i % len(self.items)]
        self.i += 1
        return it


def build_program(stop_after=99, debug=False):
    nc = bass.Bass("TRN2", target_bir_lowering=False)
    es = ExitStack()

    def din(name, shape, dt=F32):
        return nc.dram_tensor(name, list(shape), dt, kind="ExternalInput").ap()

    skind = "ExternalOutput" if debug else "Internal"
    xT_d = din("xT", [D, S_LEN])
    x_d = din("x", [S_LEN, D])
    memT_d = din("memT", [D, 256])
    w_in_d = din("w_in", [D, 10240])
    cpar_d = din("cpar", [128, 8, 8])
    wa_d = din("lru_wa", [16, 64, 64])
    wx_d = din("lru_wx", [16, 64, 64])
    wkv_d = din("w_mem_kv", [D, 1024])
    wbra_d = din("w_br_attn", [512, D])
    wbrl_d = din("w_br_lru", [1024, D])
    wbrm_d = din("w_br_mem", [512, D])
    wout_d = din("w_out", [D, D])
    bgate_d = din("b_gate", [1, 3072])
    lnp_d = din("lnp", [4, D])
    wqT_d = din("peer_wqT", [2048, D])
    keysT_d = din("keysT", [128, 16, 128])
    ul_d = din("u_l", [16384, 1024])
    v_d = din("peer_v", [16384, 1024])
    out_d = nc.dram_tensor("out", [S_LEN, D], F32, kind="ExternalOutput").ap()
    br_s = nc.dram_tensor("br_s", [2048, S_LEN], BF16, kind=skind).ap()
    x1_s = nc.dram_tensor("x1_s", [S_LEN, D], F32, kind=skind).ap()
    uv_s = nc.dram_tensor("uv_s", [16384, 2048], BF16, kind="Internal").ap()

    S = Sched(nc, es)

    def sb(st, name, shape, dt):
        return st.enter_context(nc.sbuf_tensor("sb_" + name, list(shape), dt))

    def psb(st, name, shape=(128, 512), dt=F32):
        return st.enter_context(nc.psum_tensor("ps_" + name, list(shape), dt))

    ident = sb(es, "ident", [128, 128], F32)
    ones_bf = sb(es, "ones_bf", [128, 128], BF16)
    iota_i = sb(es, "iota_i", [128, 128], mybir.dt.int32)
    iota_f = sb(es, "iota_f", [128, 128], F32)
    iota_p = sb(es, "iota_p", [128, 1], F32)
    t_const = Trk()
    S.emit("pool", lambda e: e.iota(iota_i[:], pattern=[[1, 128]], base=0, channel_multiplier=0), (), [t_const])
    S.cp("dve", iota_f[:], iota_i[:], [t_const], [t_const])
    S.emit("pool", lambda e: e.iota(iota_i[:, 0:1], pattern=[[1, 1]], base=0, channel_multiplier=1), [t_const], [t_const])
    S.cp("dve", iota_p[:], iota_i[:, 0:1], [t_const], [t_const])
    S.ts("dve", ident[:], iota_f[:], iota_p[:, 0:1], None, ALU.is_equal, reads=[t_const], writes=[t_const])
    S.memset("dve", ones_bf[:], 1.0, [t_const])
    iota_b = sb(es, "iota_b", [128, 128], BF16)
    S.cp("dve", iota_b[:], iota_f[:], [t_const], [t_const])

    t_us, t_vs = Trk(), Trk()
    NCH = 32
    rows = 16384 // NCH

    cast_i = [0]

    def next_cast(n=1):
        for _ in range(n):
            i = cast_i[0]
            if stop_after < 4 or i >= 2 * NCH:
                return
            cast_i[0] += 1
            j = i // 2
            if i % 2 == 0:
                S.dma("pool", uv_s[j * rows:(j + 1) * rows, 0:1024], ul_d[j * rows:(j + 1) * rows, :], (), ())
            else:
                S.dma("pool", uv_s[j * rows:(j + 1) * rows, 1024:2048], v_d[j * rows:(j + 1) * rows, :], (), ())

    with ExitStack() as p12:
        xT = sb(p12, "xT", [128, 8, S_LEN], BF16)
        t_xT = trks(8)
        xT_v = xT_d.rearrange("(k p) t -> p k t", p=128)
        for k in range(8):
            S.dma("pool", xT[:, k, :], xT_v[:, k, :], (), [t_xT[k]])
        w_in_v = w_in_d.rearrange("(k p) n -> p k n", p=128)

        with ExitStack() as p1:
            cpar = sb(p1, "cpar", [128, 8, 8], F32)
            cneg = sb(p1, "cneg", [128, 8], F32)
            cneg2 = sb(p1, "cneg2", [128, 8], F32)
            wa_bd = sb(p1, "wa_bd", [128, 8, 128], BF16)
            wx_bd = sb(p1, "wx_bd", [128, 8, 128], BF16)
            t_cp, t_wbd = Trk(), Trk()
            S.dma("sp", cpar[:], cpar_d, (), [t_cp])
            S.actf(cneg[:], cpar[:, :, 7], AF.Exp, [t_cp], [t_cp], scale=-1.0)
            S.actf(cneg[:], cneg[:], AF.Ln, [t_cp], [t_cp], bias=1.0)
            S.ts("dve", cneg2[:], cneg[:], -16.0, None, ALU.mult, reads=[t_cp], writes=[t_cp])
            S.ts("dve", cneg[:], cneg[:], -8.0, None, ALU.mult, reads=[t_cp], writes=[t_cp])
            S.memset("dve", wa_bd[:], 0.0, [t_wbd])
            S.memset("dve", wx_bd[:], 0.0, [t_wbd])
            for (wd, wsb) in ((wa_d, wa_bd), (wx_d, wx_bd)):
                wv = wd.rearrange("(c two) i j -> two i c j", two=2)
                S.dma("pool", wsb[0:64, :, 0:64], wv[0], (), [t_wbd])
                S.dma("pool", wsb[64:128, :, 64:128], wv[1], (), [t_wbd])

            HL = 1024
            NBC = S_LEN // HL
            NSET = 4
            XR = [sb(p1, "XR%d" % i, [128, HL + 4], F32) for i in range(NSET)]
            XC = [sb(p1, "XC%d" % i, [128, HL], F32) for i in range(NSET)]
            XCb = [sb(p1, "XCb%d" % i, [128, HL], BF16) for i in range(NSET)]
            RA = [sb(p1, "RA%d" % i, [128, HL], F32) for i in range(NSET)]
            IB = [sb(p1, "IB%d" % i, [128, HL], F32) for i in range(NSET)]
            MU = [sb(p1, "MU%d" % i, [128, HL], F32) for i in range(NSET)]
            YG = [sb(p1, "YG%d" % i, [128, HL], F32) for i in range(NSET)]
            REC = [sb(p1, "REC%d" % i, [128, HL], BF16) for i in range(NSET)]
            wxy = [sb(p1, "wxy%d" % i, [128, 8, 256], BF16) for i in range(2)]
            t_wxy = trks(2)
            t_XR, t_XC, t_XCb, t_RA, t_IB, t_MU, t_YG, t_REC = [trks(NSET) for _ in range(8)]
            pp = [psb(p1, "p1ps%d" % i) for i in range(8)]
            t_pp = trks(8)
            ppr = Ring(list(zip(pp, t_pp)))

            def block_ops(c, hh, it):
                r1, r2, r3 = Rec(S), Rec(S), Rec(S)
                wt, t_w = wxy[c % 2], t_wxy[c % 2]
                if hh == 0:
                    for cn in ([0, 1] if c == 0 else [c + 1]):
                        if cn < 8:
                            r1.dma("pool", wxy[cn % 2][:, :, 0:128],
                                   w_in_v[:, :, XROFF + cn * 128:XROFF + (cn + 1) * 128], (), [t_wxy[cn % 2]])
                            r1.dma("pool", wxy[cn % 2][:, :, 128:256],
                                   w_in_v[:, :, YGOFF + cn * 128:YGOFF + (cn + 1) * 128], (), [t_wxy[cn % 2]])
                pb = it % NSET
                pv = (it - 1) % NSET
                xr, xc, xcb, ra, ib, mu, yg, rec = XR[pb], XC[pb], XCb[pb], RA[pb], IB[pb], MU[pb], YG[pb], REC[pb]
                for which, (dst, t_dst, off) in enumerate(((xr, t_XR[pb], 4), (yg, t_YG[pb], 0))):
                    for tb in range(HL // 512):
                        ps, t_ps = ppr.next()
                        t0 = hh * HL + tb * 512
                        for k in range(8):
                            r1.mm(ps[:], wt[:, k, which * 128:(which + 1) * 128], xT[:, k, t0:t0 + 512],
                                  k == 0, k == 7, [t_w, t_xT[k]], [t_ps])
                        eng = "act" if tb % 2 == 0 else "dve"
                        r1.cp(eng, dst[:, off + tb * 512: off + (tb + 1) * 512], ps[:], [t_ps], [t_dst])
                if hh == 0:
                    r2.memset("dve", xr[:, 0:4], 0.0, [t_XR[pb]])
                else:
                    r2.cp("dve", xr[:, 0:4], XR[pv][:, HL:HL + 4], [t_XR[pv]], [t_XR[pb]])
                r2.actf(xc[:], xr[:, 4:4 + HL], AF.Identity, [t_XR[pb], t_cp], [t_XC[pb]],
                        bias=cpar[:, c, 4:5], scale=cpar[:, c, 3:4])
                for j in range(3):
                    r2.stt("dve", xc[:], xr[:, 1 + j:1 + j + HL], cpar[:, c, j:j + 1], xc[:],
                           ALU.mult, ALU.add, [t_XR[pb], t_cp, t_XC[pb]], [t_XC[pb]])
                r2.cp("pool", xcb[:], xc[:], [t_XC[pb]], [t_XCb[pb]])
                for (wsb, dst, t_dst, bcol) in ((wa_bd, ra, t_RA[pb], 5), (wx_bd, ib, t_IB[pb], 6)):
                    for tb in range(HL // 512):
                        ps, t_ps = ppr.next()
                        r2.mm(ps[:], wsb[:, c, :], xcb[:, tb * 512:(tb + 1) * 512], True, True,
                              [t_wbd, t_XCb[pb]], [t_ps])
                        r2.actf(dst[:, tb * 512:(tb + 1) * 512], ps[:], AF.Sigmoid, [t_ps, t_cp], [t_dst],
                                bias=cpar[:, c, bcol:bcol + 1])
                r3.actf(mu[:], ra[:], AF.Exp, [t_RA[pb], t_cp], [t_MU[pb]], scale=cneg2[:, c:c + 1])
                r3.actf(ra[:], ra[:], AF.Exp, [t_RA[pb], t_cp], [t_RA[pb]], scale=cneg[:, c:c + 1])
                r3.actf(mu[:], mu[:], AF.Sqrt, [t_MU[pb]], [t_MU[pb]], bias=1.0, scale=-1.0)
                r3.tt("dve", ib[:], ib[:], xc[:], ALU.mult, [t_IB[pb], t_XC[pb]], [t_IB[pb]])
                r3.tt("pool", ib[:], ib[:], mu[:], ALU.mult, [t_IB[pb], t_MU[pb]], [t_IB[pb]])
                if hh == 0:
                    r3.emit("dve", (lambda e, mu=mu, ra=ra, ib=ib: e.tensor_tensor_scan(
                        mu[:], ra[:], ib[:], 0.0, ALU.mult, ALU.add)),
                        [t_RA[pb], t_IB[pb], t_MU[pb]], [t_MU[pb]])
                else:
                    r3.emit("dve", (lambda e, mu=mu, ra=ra, ib=ib, pm=MU[pv]: e.tensor_tensor_scan(
                        mu[:], ra[:], ib[:], pm[:, HL - 1:HL], ALU.mult, ALU.add)),
                        [t_RA[pb], t_IB[pb], t_MU[pb], t_MU[pv]], [t_MU[pb]])
                r3.tt("pool", xc[:], yg[:], yg[:], ALU.mult, [t_YG[pb]], [t_XC[pb]])
                r3.ts("dve", xc[:], xc[:], 0.044715, 1.0, ALU.mult, ALU.add, reads=[t_XC[pb]], writes=[t_XC[pb]])
                r3.tt("pool", xc[:], xc[:], yg[:], ALU.mult, [t_XC[pb], t_YG[pb]], [t_XC[pb]])
                r3.actf(xc[:], xc[:], AF.Sigmoid, [t_XC[pb]], [t_XC[pb]], scale=1.5957691216057308)
                r3.tt("pool", xc[:], xc[:], yg[:], ALU.mult, [t_XC[pb], t_YG[pb]], [t_XC[pb]])
                r3.tt("dve", rec[:], mu[:], xc[:], ALU.mult, [t_MU[pb], t_XC[pb]], [t_REC[pb]])
                r3.dma("sp", br_s[512 + c * 128:512 + (c + 1) * 128, hh * HL:(hh + 1) * HL], rec[:], [t_REC[pb]], ())
                if it % 2 == 0:
                    r3.ops.append(lambda: next_cast(1))
                return r1.ops, r2.ops, r3.ops

            blocks = [(c, hh) for c in range(8 if stop_after >= 1 else 0) for hh in range(NBC)]
            stages = []
            for it in range(len(blocks) + 2):
                if it < len(blocks):
                    stages.append(block_ops(blocks[it][0], blocks[it][1], it))
                lists = []
                if it < len(blocks):
                    lists.append(stages[it][0])
                if 0 <= it - 1 < len(blocks):
                    lists.append(stages[it - 1][1])
                if 0 <= it - 2 < len(blocks):
                    lists.append(stages[it - 2][2])
                n = max(len(l) for l in lists)
                pos = [0] * len(lists)
                for step in range(1, n + 1):
                    for li, l in enumerate(lists):
                        want = (step * len(l)) // n
                        while pos[li] < want:
                            l[pos[li]]()
                            pos[li] += 1
            S.barrier()
            S.flush()

        with ExitStack() as p2:
            mask2 = sb(p2, "mask2", [128, 512], BF16)
            mtmp = sb(p2, "mtmp", [128, 128], F32)
            t_mask = Trk()
            pp = [psb(p2, "p2ps%d" % i) for i in range(8)]
            t_pp = trks(8)
            ppr = Ring(list(zip(pp, t_pp)))
            S.ts("dve", mtmp[:], iota_f[:], iota_p[:, 0:1], None, ALU.is_ge, reads=[t_const], writes=[t_mask])
            S.cp("dve", mask2[:, 0:128], mtmp[:], [t_mask], [t_mask])
            S.cp("dve", mask2[:, 256:384], mtmp[:], [t_mask], [t_mask])
            S.ts("dve", mtmp[:], iota_f[:], iota_p[:, 0:1], None, ALU.is_le, reads=[t_const, t_mask], writes=[t_mask])
            S.cp("dve", mask2[:, 128:256], mtmp[:], [t_mask], [t_mask])
            S.cp("dve", mask2[:, 384:512], mtmp[:], [t_mask], [t_mask])
            inv_sqrt = 1.0 / math.sqrt(128.0)
            with ExitStack() as p2a:
                QTr = [sb(p2a, "QT%d" % i, [128, S_LEN], BF16) for i in range(2)]
                KTr = [sb(p2a, "KT%d" % i, [128, S_LEN], BF16) for i in range(2)]
                VVr = [sb(p2a, "VV%d" % i, [128, 32, 128], BF16) for i in range(2)]
                ACCN = sb(p2a, "ACCN", [128, S_LEN], F32)
                ACCD = sb(p2a, "ACCD", [128, S_LEN], F32)
                ATT = sb(p2a, "ATT", [128, S_LEN], BF16)
                wqkv = [sb(p2a, "wqkv%d" % i, [128, 8, 384], BF16) for i in range(2)]
                PT = [sb(p2a, "PT%d" % i, [128, 512], BF16) for i in range(3)]
                t_QTr, t_KTr, t_VVr = trks(2), trks(2), trks(2)
                t_ACCN, t_ACCD, t_ATT = trks(3)
                t_wqkv = trks(2)
                t_PT = trks(3)
                pprA = Ring(list(zip(pp[0:3], t_pp[0:3])))
                pprB = Ring(list(zip(pp[3:8], t_pp[3:8])))

                def head_ops(hi):
                    h, g = hi // 3, hi % 3
                    ra_, rb_ = Rec(S), Rec(S)
                    d = DILS[g]
                    L = S_LEN // d
                    nb = L // 128
                    hb = hi % 2
                    QT, KT, VV = QTr[hb], KTr[hb], VVr[hb]
                    t_QT, t_KT, t_VV = t_QTr[hb], t_KTr[hb], t_VVr[hb]
                    wt, t_w = wqkv[hb], t_wqkv[hb]
                    for hn in ([0, 1] if hi == 0 else [hi + 1]):
                        if hn < 12:
                            coln = (hn % 3) * 512 + (hn // 3) * 128
                            for j, off in enumerate((QOFF, KOFF, VOFF)):
                                ra_.dma("pool", wqkv[hn % 2][:, :, j * 128:(j + 1) * 128],
                                        w_in_v[:, :, off + coln:off + coln + 128], (), [t_wqkv[hn % 2]])
                    ra_.ops.append(lambda: next_cast(2))
                    for j, (dst, t_dst) in enumerate(((QT, t_QT), (KT, t_KT))):
                        dv = dst[:].rearrange("p (r m) -> p m r", r=d)
                        for tb in range(8):
                            ps, t_ps = pprA.next()
                            for k in range(8):
                                ra_.mm(ps[:], wt[:, k, j * 128:(j + 1) * 128], xT[:, k, tb * 512:(tb + 1) * 512],
                                       k == 0, k == 7, [t_w, t_xT[k]], [t_ps])
                            m0 = tb * (512 // d)
                            o_ap = dv[:, m0:m0 + 512 // d, :]
                            i_ap = ps[:].rearrange("p (m r) -> p m r", r=d)
                            if j == 0:
                                ra_.actf(o_ap, i_ap, AF.Copy, [t_ps], [t_dst], scale=inv_sqrt)
                            else:
                                ra_.cp("dve", o_ap, i_ap, [t_ps], [t_dst])
                    xTg = [xT[:, k, :].rearrange("p (m r) -> p r m", r=d) for k in range(8)]
                    for jb4 in range(8):
                        ps, t_ps = pprA.next()
                        for q4 in range(4):
                            jb = jb4 * 4 + q4
                            r, n = jb // nb, jb % nb
                            for k in range(8):
                                ra_.mm(ps[:, q4 * 128:(q4 + 1) * 128], xTg[k][:, r, n * 128:(n + 1) * 128],
                                       wt[:, k, 256:384], k == 0, k == 7, [t_w, t_xT[k]], [t_ps],
                                       sig=(k == 7 and q4 == 3))
                        ra_.cp("act" if jb4 % 2 else "dve", VV[:, jb4 * 4:(jb4 + 1) * 4, :],
                               ps[:].rearrange("p (a b) -> p a b", b=128), [t_ps], [t_VV])
                    accn_v = ACCN[:].rearrange("p (m r) -> p r m", r=d)
                    accd_v = ACCD[:].rearrange("p (m r) -> p r m", r=d)
                    prevPT = None
                    for jp in range(16):
                        ps, t_ps = pprB.next()
                        for u in range(2):
                            jb = jp * 2 + u
                            n = jb % nb
                            ncol = 256 if n < nb - 1 else 128
                            rb_.mm(ps[:, u * 256:u * 256 + ncol], KT[:, jb * 128:(jb + 1) * 128],
                                   QT[:, jb * 128:jb * 128 + ncol], True, True, [t_KT, t_QT], [t_ps], sig=(u == 1))
                        pt, t_pt = PT[jp % 3], t_PT[jp % 3]
                        wv = 512 if ((jp * 2 + 1) % nb) < nb - 1 else 384
                        rb_.actf(pt[:, 0:wv], ps[:, 0:wv], AF.Exp, [t_ps], [t_pt])
                        rb_.tt("pool" if jp % 2 else "dve", pt[:, 0:wv], pt[:, 0:wv], mask2[:, 0:wv], ALU.mult,
                               [t_pt, t_mask], [t_pt])
                        pso, t_pso = pprB.next()
                        for u in range(2):
                            jb = jp * 2 + u
                            n = jb % nb
                            srcs = []
                            if n > 0:
                                if u == 0:
                                    srcs.append((jb - 1, prevPT[0][:, 384:512], prevPT[1]))
                                else:
                                    srcs.append((jb - 1, pt[:, 128:256], t_pt))
                            srcs.append((jb, pt[:, u * 256:u * 256 + 128], t_pt))
                            for half, use_ones in ((0, False), (1, True)):
                                oc = half * 256 + u * 128
                                for si, (kb, p_ap, t_p) in enumerate(srcs):
                                    lhs = ones_bf[:] if use_ones else VV[:, kb, :]
                                    rb_.mm(pso[:, oc:oc + 128], lhs, p_ap, si == 0, si == len(srcs) - 1,
                                           [t_p, t_VV, t_const], [t_pso],
                                           sig=(si == len(srcs) - 1 and half == 1 and u == 1))
                        prevPT = (pt, t_pt)
                        c0 = jp * 256
                        assert L >= 256
                        r0, m0 = c0 // L, c0 % L
                        on = accn_v[:, r0, m0:m0 + 256]
                        od = accd_v[:, r0, m0:m0 + 256]
                        inn, ind = pso[:, 0:256], pso[:, 256:512]
                        if g == 0:
                            rb_.cp("act", on, inn, [t_pso], [t_ACCN])
                            rb_.cp("act", od, ind, [t_pso], [t_ACCD])
                        else:
                            rb_.tt("dve", on, on, inn, ALU.add, [t_pso, t_ACCN], [t_ACCN])
                            rb_.tt("dve", od, od, ind, ALU.add, [t_pso, t_ACCD], [t_ACCD])
                    if g == 2:
                        rb_.emit("dve", (lambda e: e.reciprocal(ACCD[:], ACCD[:])), [t_ACCD], [t_ACCD])
                        rb_.tt("pool", ATT[:], ACCN[:], ACCD[:], ALU.mult, [t_ACCN, t_ACCD], [t_ATT])
                        rb_.dma("sp", br_s[h * 128:(h + 1) * 128, :], ATT[:], [t_ATT], ())
                    return ra_.ops, rb_.ops

                nh = 12 if stop_after >= 2 else 0
                prevB2 = []
                for hi in range(nh + 1):
                    A2 = []
                    B2 = []
                    if hi < nh:
                        A2, B2 = head_ops(hi)
                    lists = [l for l in (A2, prevB2) if l]
                    if lists:
                        n = max(len(l) for l in lists)
                        pos = [0] * len(lists)
                        for step in range(1, n + 1):
                            for li, l in enumerate(lists):
                                want = (step * len(l)) // n
                                while pos[li] < want:
                                    l[pos[li]]()
                                    pos[li] += 1
                    prevB2 = B2
                S.barrier()
                S.flush()

            if stop_after >= 2:
                memT = sb(p2, "memT", [128, 8, 256], BF16)
                wkv = sb(p2, "wkv", [128, 8, 1024], BF16)
                wmq = sb(p2, "wmq", [128, 8, 512], BF16)
                KmT = sb(p2, "KmT", [128, 4, 256], BF16)
                Vm = sb(p2, "Vm", [128, 2, 512], BF16)
                MQ = [sb(p2, "MQ%d" % i, [128, 512], BF16) for i in range(2)]
                PM = [sb(p2, "PM%d" % i, [128, 2, 512], BF16) for i in range(2)]
                MO = [sb(p2, "MO%d" % i, [128, 512], F32) for i in range(2)]
                MOb = [sb(p2, "MOb%d" % i, [128, 512], BF16) for i in range(2)]
                t_memT, t_wkv, t_wmq, t_KmT, t_Vm = trks(5)
                t_MQ, t_PM, t_MO, t_MOb = trks(2), trks(2), trks(2), trks(2)
                S.dma("pool", memT[:], memT_d.rearrange("(k p) m -> p k m", p=128), (), [t_memT])
                S.dma("pool", wkv[:], wkv_d.rearrange("(k p) n -> p k n", p=128), (), [t_wkv])
                S.dma("pool", wmq[:], w_in_v[:, :, MQOFF:MQOFF + 512], (), [t_wmq])
                for hh in range(4):
                    ps, t_ps = ppr.next()
                    for k in range(8):
                        S.mm(ps[:, 0:256], wkv[:, k, hh * 128:(hh + 1) * 128], memT[:, k, :], k == 0, k == 7,
                             [t_wkv, t_memT], [t_ps])
                    S.cp("dve", KmT[:, hh, :], ps[:, 0:256], [t_ps], [t_KmT])
                for mc in range(2):
                    ps, t_ps = ppr.next()
                    for k in range(8):
                        S.mm(ps[:], memT[:, k, mc * 128:(mc + 1) * 128], wkv[:, k, 512:1024], k == 0, k == 7,
                             [t_wkv, t_memT], [t_ps])
                    S.cp("dve", Vm[:, mc, :], ps[:], [t_ps], [t_Vm])
                it = 0
                for tb in range(8):
                    for hh in range(4):
                        b2 = it % 2
                        it += 1
                        ps, t_ps = ppr.next()
                        for k in range(8):
                            S.mm(ps[:], wmq[:, k, hh * 128:(hh + 1) * 128], xT[:, k, tb * 512:(tb + 1) * 512],
                                 k == 0, k == 7, [t_wmq, t_xT[k]], [t_ps])
                        S.actf(MQ[b2][:], ps[:], AF.Copy, [t_ps], [t_MQ[b2]], scale=inv_sqrt)
                        for mc in range(2):
                            ps, t_ps = ppr.next()
                            S.mm(ps[:], KmT[:, hh, mc * 128:(mc + 1) * 128], MQ[b2][:], True, True,
                                 [t_KmT, t_MQ[b2]], [t_ps])
                            S.actf(PM[b2][:, mc, :], ps[:], AF.Exp, [t_ps], [t_PM[b2]])
                        pn, t_pn = ppr.next()
                        pd, t_pd = ppr.next()
                        for mc in range(2):
                            S.mm(pn[:], Vm[:, mc, hh * 128:(hh + 1) * 128], PM[b2][:, mc, :], mc == 0, mc == 1,
                                 [t_Vm, t_PM[b2]], [t_pn])
                        for mc in range(2):
                            S.mm(pd[:], ones_bf[:], PM[b2][:, mc, :], mc == 0, mc == 1, [t_const, t_PM[b2]], [t_pd])
                        S.emit("dve", lambda e, o=MO[b2], i=pd: e.reciprocal(o[:], i[:]), [t_pd], [t_MO[b2]])
                        S.tt("dve", MOb[b2][:], MO[b2][:], pn[:], ALU.mult, [t_MO[b2], t_pn], [t_MOb[b2]])
                        S.dma("sp", br_s[1536 + hh * 128:1536 + (hh + 1) * 128, tb * 512:(tb + 1) * 512], MOb[b2][:],
                              [t_MOb[b2]], ())
            S.barrier()
            S.flush()

    t_br = Trk()

    with ExitStack() as p3:
        wg = sb(p3, "wg", [128, 8, 3072], BF16)
        wbr = sb(p3, "wbr", [128, 16, 1024], BF16)
        wout = sb(p3, "wout", [128, 8, 1024], BF16)
        bg = sb(p3, "bg", [1, 3072], F32)
        ones_f = sb(p3, "ones_f", [1, 128], F32)
        lng = sb(p3, "lng", [128, 1024], F32)
        lnb = sb(p3, "lnb", [128, 1024], F32)
        t_wg, t_wbr, t_wout, t_bg, t_ln = trks(5)
        if stop_after >= 3:
            for k in range(8):
                S.dma("pool", wg[:, k, :], w_in_v[:, k, GLOFF:GLOFF + 3072], (), [t_wg])
            S.dma("pool", wbr[:, 0:4, :], wbra_d.rearrange("(k p) n -> p k n", p=128), (), [t_wbr])
            S.dma("pool", wbr[:, 4:12, :], wbrl_d.rearrange("(k p) n -> p k n", p=128), (), [t_wbr])
            S.dma("pool", wbr[:, 12:16, :], wbrm_d.rearrange("(k p) n -> p k n", p=128), (), [t_wbr])
            S.dma("pool", wout[:], wout_d.rearrange("(k p) n -> p k n", p=128), (), [t_wout])
            S.dma("sp", bg[:], bgate_d, (), [t_bg])
            S.memset("dve", ones_f[:], 1.0, [t_bg])
            S.dma("sp", lng[:], lnp_d[0:1, :].to_broadcast([128, 1024]), (), [t_ln])
            S.dma("sp", lnb[:], lnp_d[1:2, :].to_broadcast([128, 1024]), (), [t_ln])
        xTb = [sb(p3, "xTb%d" % i, [128, 8, 512], BF16) for i in range(2)]
        brT = [sb(p3, "brT%d" % i, [128, 16, 512], BF16) for i in range(2)]
        xres = [sb(p3, "xres%d" % i, [128, 1024], F32) for i in range(2)]
        gate = sb(p3, "gate", [128, 1024], F32)
        macc = sb(p3, "macc", [128, 1024], F32)
        macc2 = sb(p3, "macc2", [128, 1024], F32)
        t_macc2 = Trk()
        mtmp3 = sb(p3, "mtmp3", [128, 1024], F32)
        mT = sb(p3, "mT", [128, 8, 128], BF16)
        yb = [sb(p3, "yb%d" % i, [128, 1024], F32) for i in range(2)]
        junk = sb(p3, "junk3", [128, 1024], F32)
        st = [sb(p3, "st%d" % i, [128, 8], F32) for i in range(2)]
        t_xTb, t_brT, t_xres, t_yb, t_st = trks(2), trks(2), trks(2), trks(2), trks(2)
        t_gate, t_macc, t_mtmp, t_mT, t_junk = trks(5)
        pg = [psb(p3, "p3g%d" % i) for i in range(2)]
        pj = [psb(p3, "p3j%d" % i) for i in range(2)]
        ptr = [psb(p3, "p3t%d" % i) for i in range(2)]
        pm = [psb(p3, "p3m%d" % i) for i in range(2)]
        t_pg, t_pj, t_ptr, t_pm = trks(2), trks(2), trks(2), trks(2)
        xT_v2 = xT_d.rearrange("(k p) t -> p k t", p=128)
        br_v = br_s.rearrange("(k p) t -> p k t", p=128)
        brch = ((0, 4), (4, 12), (12, 16))
        maccr = [macc, macc2]
        t_maccr = [t_macc, t_macc2]

        def tile_ops(ti):
            tb, tl = ti // 4, ti % 4
            b2 = tb % 2
            b1 = ti % 2
            mac, t_mac = maccr[b1], t_maccr[b1]
            RA_, RB_ = Rec(S), Rec(S)
            if tl == 0:
                for tbn in ([0, 1] if tb == 0 else [tb + 1]):
                    if tbn < 8:
                        RA_.dma("pool", xTb[tbn % 2][:], xT_v2[:, :, tbn * 512:(tbn + 1) * 512], (), [t_xTb[tbn % 2]])
                        RA_.dma("sp", brT[tbn % 2][:], br_v[:, :, tbn * 512:(tbn + 1) * 512], [t_br], [t_brT[tbn % 2]])
            tsl = slice(tl * 128, (tl + 1) * 128)
            RA_.dma("sp", xres[b1][:], x_d[ti * 128:(ti + 1) * 128, :], (), [t_xres[b1]])
            RA_.ops.append(lambda: next_cast(1))
            for br in range(3):
                for hf in range(2):
                    cs = br * 1024 + hf * 512
                    RA_.mm(pg[hf][:], ones_f[0:1, :], bg[0:1, cs:cs + 512], True, False, [t_bg], [t_pg[hf]], sig=False)
                    for k in range(8):
                        RA_.mm(pg[hf][:], xTb[b2][:, k, tsl], wg[:, k, cs:cs + 512], False, k == 7,
                               [t_xTb[b2], t_wg], [t_pg[hf]])
                    RA_.actf(gate[:, hf * 512:(hf + 1) * 512], pg[hf][:], AF.Sigmoid, [t_pg[hf]], [t_gate])
                    c0, c1 = brch[br]
                    for c in range(c0, c1):
                        RA_.mm(pj[hf][:], brT[b2][:, c, tsl], wbr[:, c, hf * 512:(hf + 1) * 512], c == c0, c == c1 - 1,
                               [t_brT[b2], t_wbr], [t_pj[hf]])
                    hs = slice(hf * 512, (hf + 1) * 512)
                    if br == 0:
                        RA_.tt("dve", mac[:, hs], gate[:, hs], pj[hf][:], ALU.mult, [t_gate, t_pj[hf]], [t_mac])
                    else:
                        RA_.tt("dve", mtmp3[:, hs], gate[:, hs], pj[hf][:], ALU.mult, [t_gate, t_pj[hf]], [t_mtmp])
                        RA_.tt("pool", mac[:, hs], mac[:, hs], mtmp3[:, hs], ALU.add, [t_mtmp, t_mac], [t_mac])
            for k in range(8):
                RB_.tr(ptr[k // 4][:, (k % 4) * 128:(k % 4 + 1) * 128], mac[:, k * 128:(k + 1) * 128], ident[:],
                       [t_mac, t_const], [t_ptr[k // 4]])
            for hf in range(2):
                RB_.cp("act", mT[:, hf * 4:(hf + 1) * 4, :], ptr[hf][:].rearrange("p (a b) -> p a b", b=128),
                       [t_ptr[hf]], [t_mT])
            for hf in range(2):
                for k in range(8):
                    RB_.mm(pm[hf][:], mT[:, k, :], wout[:, k, hf * 512:(hf + 1) * 512], k == 0, k == 7,
                           [t_mT, t_wout], [t_pm[hf]])
                RB_.stt("dve", yb[b1][:, hf * 512:(hf + 1) * 512], xres[b1][:, hf * 512:(hf + 1) * 512], ALPHA,
                        pm[hf][:], ALU.mult, ALU.add, [t_xres[b1], t_pm[hf]], [t_yb[b1]])
            layer_norm(RB_, yb[b1], t_yb[b1], st[b1], t_st[b1], junk, t_junk, lng, lnb, t_ln)
            RB_.dma("sp", x1_s[ti * 128:(ti + 1) * 128, :], yb[b1][:], [t_yb[b1]], ())
            return RA_.ops, RB_.ops

        prevB3 = []
        for ti in range(32 if stop_after >= 3 else 0):
            A3, B3 = tile_ops(ti)
            nA, nB = len(A3), len(prevB3)
            jb = 0
            for ia, fa in enumerate(A3):
                fa()
                want = ((ia + 1) * nB) // nA
                while jb < want:
                    prevB3[jb]()
                    jb += 1
            while jb < nB:
                prevB3[jb]()
                jb += 1
            prevB3 = B3
        for fb in prevB3:
            fb()
        next_cast(2 * NCH)
        S.barrier()
        S.flush()

    if stop_after >= 4:
        with ExitStack() as p4:
            peer_phase(nc, S, p4, sb, psb, ident, iota_f, iota_b, t_const, x1_s, wqT_d, keysT_d, uv_s, lnp_d, out_d)
            S.barrier()
            S.flush()
    else:
        S.barrier()
        S.flush()
    es.close()
    return nc


def layer_norm(S, y, t_y, st, t_st, junk, t_junk, lng, lnb, t_ln):
    S.emit("dve", lambda e: e.tensor_reduce(st[:, 0:1], y[:], AX.X, ALU.add), [t_y], [t_st])
    S.ts("dve", st[:, 1:2], st[:, 0:1], -1.0 / D, None, ALU.mult, reads=[t_st], writes=[t_st])
    S.actf(y[:], y[:], AF.Identity, [t_y, t_st], [t_y], bias=st[:, 1:2])
    S.tt("pool", junk[:], y[:], y[:], ALU.mult, [t_y], [t_junk])
    S.emit("dve", lambda e: e.tensor_reduce(st[:, 2:3], junk[:], AX.X, ALU.add), [t_junk, t_st], [t_st])
    S.ts("dve", st[:, 3:4], st[:, 2:3], 1.0 / D, LN_EPS, ALU.mult, ALU.add, reads=[t_st], writes=[t_st])
    S.actf(st[:, 3:4], st[:, 3:4], AF.Sqrt, [t_st], [t_st])
    S.emit("dve", lambda e: e.reciprocal(st[:, 3:4], st[:, 3:4]), [t_st], [t_st])
    S.actf(y[:], y[:], AF.Copy, [t_y, t_st], [t_y], scale=st[:, 3:4])
    S.tt("pool", y[:], y[:], lng[:], ALU.mult, [t_y, t_ln], [t_y])
    S.tt("pool", y[:], y[:], lnb[:], ALU.add, [t_y, t_ln], [t_y])


def peer_phase(nc, S, p4, sb, psb, ident, iota_f, iota_b, t_const, x1_s, wqT_d, keysT_d, uv_s, lnp_d, out_d):
    TG = 256
    NG = S_LEN // TG
    Wsc = sb(p4, "Wsc", [128, 8, 2048], BF16)
    lng = sb(p4, "lng2", [128, 1024], F32)
    lnb = sb(p4, "lnb2", [128, 1024], F32)
    t_Wsc, t_ln = trks(2)
    S.dma("sp", lng[:], lnp_d[2:3, :].to_broadcast([128, 1024]), (), [t_ln])
    S.dma("sp", lnb[:], lnp_d[3:4, :].to_broadcast([128, 1024]), (), [t_ln])
    with ExitStack() as pre:
        wqT = sb(pre, "wqT", [128, 16, 1024], F32)
        keysT = sb(pre, "keysTf", [128, 16, 128], F32)
        t_wqT, t_keys = trks(2)
        wqT_v = wqT_d.rearrange("(e c) d -> c e d", c=128)
        for e4 in range(4):
            S.dma("sp", wqT[:, e4 * 4:(e4 + 1) * 4, :], wqT_v[:, e4 * 4:(e4 + 1) * 4, :], (), [t_wqT])
        S.dma("sp", keysT[:], keysT_d, (), [t_keys])
        pw = [psb(pre, "p4w%d" % i) for i in range(4)]
        t_pw = trks(4)
        i = 0
        for k in range(8):
            for e4 in range(4):
                ps, t_ps = pw[i % 4], t_pw[i % 4]
                i += 1
                for ee in range(4):
                    e_ = e4 * 4 + ee
                    S.mm(ps[:, ee * 128:(ee + 1) * 128], wqT[:, e_, k * 128:(k + 1) * 128], keysT[:, e_, :], True, True,
                         [t_wqT, t_keys], [t_ps], sig=(ee == 3))
                S.cp("act" if i % 2 else "dve", Wsc[:, k, e4 * 512:(e4 + 1) * 512], ps[:], [t_ps], [t_Wsc])
        S.barrier()
        S.flush()

    Gd = sb(p4, "Gd", [128, TG, 128], BF16)
    x1r = [sb(p4, "x1r%d" % i, [128, 1024], F32) for i in range(2)]
    x1Tr = [sb(p4, "x1Tr%d" % i, [128, 8, TG], BF16) for i in range(2)]
    SELTr = [sb(p4, "SELT%d" % i, [128, 3, 128], F32) for i in range(4)]
    BIG = sb(p4, "BIG", [128, 2048], F32)
    SC = BIG[:].rearrange("p (a b) -> p a b", b=128)
    CAND = BIG[:].rearrange("p (h c) -> p h c", c=256)
    OH = BIG[:].rearrange("p (h k i) -> p h k i", k=16, i=16)
    SC2 = sb(p4, "SC2", [128, 256], F32)
    M1 = sb(p4, "M1", [128, 16, 16], F32)
    IDX = sb(p4, "IDX", [128, 16, 16], U32)
    IDXF = sb(p4, "IDXF", [128, 16, 16], F32)
    CS = sb(p4, "CS", [128, 8, 16], F32)
    CI = sb(p4, "CI", [128, 8, 16], U32)
    II = sb(p4, "II", [128, 8, 16], U32)
    IIF = sb(p4, "IIF", [128, 2, 8, 16], F32)
    SELr = [sb(p4, "SEL%d" % i, [128, 3, 128], F32) for i in range(2)]
    sm = sb(p4, "sm", [128, 8, 4], F32)
    ABoh = [sb(p4, "ABoh%d" % i, [128, 2, 16, 128], BF16) for i in range(2)]
    Boh = [sb(p4, "Boh%d" % i, [128, 16, 128], BF16) for i in range(2)]
    NUV = 5
    UV = [sb(p4, "UV%d" % i, [128, 2048], BF16) for i in range(NUV)]
    HG = [sb(p4, "HG%d" % i, [128, TG], BF16) for i in range(4)]
    WW = [sb(p4, "WW%d" % i, [128, TG], BF16) for i in range(4)]
    ybr = [sb(p4, "yb4_%d" % i, [128, 1024], F32) for i in range(2)]
    junk = sb(p4, "junk4", [128, 1024], F32)
    st = sb(p4, "st4", [128, 8], F32)
    (t_Gd, t_big, t_SC2, t_M1, t_IDX, t_CS, t_CI, t_II, t_sm, t_junk, t_st) = trks(11)
    t_ybr = trks(2)
    t_SELr = trks(2)
    t_x1r, t_x1Tr, t_SELT = trks(2), trks(2), trks(4)
    t_A, t_B, t_Bt = trks(2), trks(2), trks(2)
    t_UV, t_HG, t_WW = trks(5), trks(4), trks(4)
    t_pBh = trks(4)
    pA = psb(p4, "p4A", [128, 2048])
    pB = [psb(p4, "p4B%d" % i) for i in range(2)]
    pC = [psb(p4, "p4C%d" % i) for i in range(2)]
    t_pA, = trks(1)
    t_pB, t_pC = trks(2), trks(2)
    uv_v = uv_s.rearrange("(a p) f -> p a f", p=128)

    def build_sel(g):
        ops = []
        late = []

        def op(f, *a, **k):
            ops.append(lambda: f(*a, **k))

        def op_late(f, *a, **k):
            late.append(lambda: f(*a, **k))
        gb = g % 2
        xtb, t_xtb = x1Tr[gb], t_x1Tr[gb]
        for tl in range(2):
            ti = g * 2 + tl
            xb, t_xb = x1r[tl], t_x1r[tl]
            SELT, t_selt = SELTr[ti % 4], t_SELT[ti % 4]
            SEL, t_SEL = SELr[tl], t_SELr[tl]
            op(S.dma, "sp", xb[:], x1_s[ti * 128:(ti + 1) * 128, :], (), [t_xb])
            for hf in range(2):
                for k4 in range(4):
                    k = hf * 4 + k4
                    op(S.tr, pC[1][:, k4 * 128:(k4 + 1) * 128], xb[:, k * 128:(k + 1) * 128], ident[:],
                       [t_xb, t_const], [t_pC[1]])
                op(S.cp, "dve", xtb[:, hf * 4:(hf + 1) * 4, tl * 128:(tl + 1) * 128],
                   pC[1][:].rearrange("p (a b) -> p a b", b=128), [t_pC[1]], [t_xtb])
            for pc in range(4):
                for k in range(8):
                    op(S.mm, pC[1][:], xtb[:, k, tl * 128:(tl + 1) * 128], Wsc[:, k, pc * 512:(pc + 1) * 512],
                       k == 0, k == 7, [t_xtb, t_Wsc], [t_pC[1]])
                op(S.cp, "dve", BIG[:, pc * 512:(pc + 1) * 512], pC[1][:], [t_pC[1]], [t_big])
            for e_ in range(16):
                op(S.emit, "dve", (lambda e, e_=e_: e.max(M1[:, e_, 0:8], SC[:, e_, :])), [t_big], [t_M1])
                op(S.emit, "dve", (lambda e, e_=e_: e.match_replace(SC2[:, 0:128], M1[:, e_, 0:8], SC[:, e_, :], -1e30)),
                   [t_big, t_M1], [t_SC2])
                op(S.emit, "dve", (lambda e, e_=e_: e.max(M1[:, e_, 8:16], SC2[:, 0:128])), [t_SC2], [t_M1])
                op(S.emit, "dve", (lambda e, e_=e_: e.max_index(IDX[:, e_, 0:8], M1[:, e_, 0:8], SC[:, e_, :])),
                   [t_big, t_M1], [t_IDX])
                op(S.emit, "dve", (lambda e, e_=e_: e.max_index(IDX[:, e_, 8:16], M1[:, e_, 8:16], SC[:, e_, :])),
                   [t_big, t_M1], [t_IDX])
            op(S.cp, "dve", IDXF[:], IDX[:], [t_IDX], [t_IDX])
            M1v = M1[:].rearrange("p (h two) k -> p h two k", two=2)
            op(S.tt, "dve", CAND.rearrange("p h (i j) -> p h i j", j=16),
               M1v[:, :, 0, :].unsqueeze(3).to_broadcast([128, 8, 16, 16]),
               M1v[:, :, 1, :].unsqueeze(2).to_broadcast([128, 8, 16, 16]), ALU.add, [t_M1], [t_big])
            for h in range(8):
                op(S.emit, "dve", (lambda e, h=h: e.max(CS[:, h, 0:8], CAND[:, h, :])), [t_big], [t_CS])
                op(S.emit, "dve", (lambda e, h=h: e.match_replace(SC2[:], CS[:, h, 0:8], CAND[:, h, :], -1e30)),
                   [t_big, t_CS], [t_SC2])
                op(S.emit, "dve", (lambda e, h=h: e.max(CS[:, h, 8:16], SC2[:])), [t_SC2], [t_CS])
                op(S.emit, "dve", (lambda e, h=h: e.max_index(CI[:, h, 0:8], CS[:, h, 0:8], CAND[:, h, :])),
                   [t_big, t_CS], [t_CI])
                op(S.emit, "dve", (lambda e, h=h: e.max_index(CI[:, h, 8:16], CS[:, h, 8:16], CAND[:, h, :])),
                   [t_big, t_CS], [t_CI])
            op(S.emit, "dve", (lambda e: e.tensor_single_scalar(II[:], CI[:], 4, ALU.logical_shift_right)), [t_CI], [t_II])
            op(S.cp, "dve", IIF[:, 0], II[:], [t_II], [t_II])
            op(S.emit, "dve", (lambda e: e.tensor_single_scalar(II[:], CI[:], 15, ALU.bitwise_and)), [t_CI, t_II], [t_II])
            op(S.cp, "dve", IIF[:, 1], II[:], [t_II], [t_II])
            IDXv = IDXF[:].rearrange("p (h two) k -> p h two k", two=2)
            for pp_ in range(2):
                op(S.tt, "dve", OH, IIF[:, pp_].unsqueeze(3).to_broadcast([128, 8, 16, 16]),
                   iota_f[:, 0:16].unsqueeze(1).unsqueeze(1).to_broadcast([128, 8, 16, 16]), ALU.is_equal,
                   [t_II, t_const, t_big], [t_big])
                op(S.tt, "dve", OH, OH, IDXv[:, :, pp_, :].unsqueeze(2).to_broadcast([128, 8, 16, 16]), ALU.mult,
                   [t_big, t_IDX], [t_big])
                op(S.emit, "dve", (lambda e, pp_=pp_, SEL=SEL: e.tensor_reduce(
                    SEL[:, pp_, :], OH.rearrange("p h k i -> p (h k) i"), AX.X, ALU.add)), [t_big], [t_SEL])
            op(S.emit, "dve", (lambda e: e.tensor_reduce(sm[:, :, 0], CS[:], AX.X, ALU.max)), [t_CS], [t_sm])
            op(S.tt, "dve", CS[:], CS[:], sm[:, :, 0:1].to_broadcast([128, 8, 16]), ALU.subtract, [t_CS, t_sm], [t_CS])
            op(S.actf, CS[:], CS[:], AF.Exp, [t_CS], [t_CS])
            op(S.emit, "dve", (lambda e: e.tensor_reduce(sm[:, :, 1], CS[:], AX.X, ALU.add)), [t_CS, t_sm], [t_sm])
            op(S.emit, "dve", (lambda e: e.reciprocal(sm[:, :, 2], sm[:, :, 1])), [t_sm], [t_sm])
            op(S.tt, "dve", SEL[:, 2, :].rearrange("p (h k) -> p h k", k=16), CS[:],
               sm[:, :, 2:3].to_broadcast([128, 8, 16]), ALU.mult, [t_CS, t_sm], [t_SEL])
            for j in range(3):
                op_late(S.tr, pC[1][:, j * 128:(j + 1) * 128], SEL[:, j, :], ident[:], [t_SEL, t_const], [t_pC[1]])
            op_late(S.cp, "dve", SELT[:].rearrange("p a b -> p (a b)"), pC[1][:, 0:384], [t_pC[1]], [t_selt])
        return ops, late

    def gd_build(g, pend):
        qi = 0
        pi = [0]

        def pop(n):
            for _ in range(n):
                if pi[0] < len(pend):
                    pend[pi[0]]()
                    pi[0] += 1
        for tl in range(2):
            ti = g * 2 + tl
            SELT, t_selt = SELTr[ti % 4], t_SELT[ti % 4]
            for q in range(8):
                b2 = qi % 2
                qi += 1
                tq = slice(q * 16, (q + 1) * 16)
                S.tt("dve", ABoh[b2][:], iota_f[:].unsqueeze(1).unsqueeze(1).to_broadcast([128, 2, 16, 128]),
                     SELT[:, 0:2, tq].unsqueeze(3).to_broadcast([128, 2, 16, 128]), ALU.is_equal,
                     [t_const, t_selt], [t_A[b2]])
                S.tt("pool", Boh[b2][:], ABoh[b2][:, 1],
                     SELT[:, 2, tq].unsqueeze(2).to_broadcast([128, 16, 128]), ALU.mult,
                     [t_A[b2], t_selt], [t_B[b2]])
                for tt_ in range(16):
                    S.mm(pA[:, tt_ * 128:(tt_ + 1) * 128], Boh[b2][:, tt_, :], ABoh[b2][:, 0, tt_, :], True, True,
                         [t_A[b2], t_B[b2]], [t_pA], sig=(tt_ == 15))
                c0 = tl * 128 + q * 16
                S.cp("act", Gd[:, c0:c0 + 16, :], pA[:].rearrange("p (t a) -> p t a", a=128), [t_pA], [t_Gd])
                pop(2)
        pop(len(pend))

    SKEW = 3
    hbank = [pB[0], pB[1], pC[0]]

    def dense(g, extra_ops, late_ops):
        gb = g % 2
        xtb, t_xtb = x1Tr[gb], t_x1Tr[gb]
        per = -(-len(extra_ops) // 112) if extra_ops else 0
        pos = 0

        def stage_h(a):
            r4 = a % 4
            r3 = a % 3
            r5 = a % NUV
            S.dma("sp", UV[r5][:], uv_v[:, a, :], (), [t_UV[r5]])
            hp = hbank[r3][:, 0:TG]
            for k in range(8):
                S.mm(hp, UV[r5][:, k * 128:(k + 1) * 128], xtb[:, k, :], k == 0, k == 7,
                     [t_UV[r5], t_xtb], [t_pBh[r3]])
            S.actf(HG[r4][:], hp, AF.Gelu, [t_pBh[r3]], [t_HG[r4]])
            S.tt("pool", WW[r4][:], HG[r4][:], Gd[:, :, a], ALU.mult, [t_HG[r4], t_Gd], [t_WW[r4]])

        def stage_o(a):
            r4 = a % 4
            r5 = a % NUV
            for tl in range(2):
                for hf in range(2):
                    S.mm(pA[:, (tl * 2 + hf) * 512:(tl * 2 + hf + 1) * 512], WW[r4][:, tl * 128:(tl + 1) * 128],
                         UV[r5][:, 1024 + hf * 512:1024 + (hf + 1) * 512], a == 0, a == 127, [t_WW[r4], t_UV[r5]], [t_pA],
                         sig=(tl == 1 and hf == 1))

        for a in range(128 + SKEW):
            if a < 128:
                stage_h(a)
            if a >= SKEW:
                stage_o(a - SKEW)
            for _ in range(per):
                if pos < len(extra_ops):
                    extra_ops[pos]()
                    pos += 1
        while pos < len(extra_ops):
            extra_ops[pos]()
            pos += 1
        for f in late_ops:
            f()

    def final(g):
        gb = g % 2
        th = []
        for tl in range(2):
            ti = g * 2 + tl
            S.dma("sp", ybr[tl][:], x1_s[ti * 128:(ti + 1) * 128, :], (), [t_ybr[tl]])
        for tl in range(2):
            S.stt("dve", ybr[tl][:], ybr[tl][:], ALPHA, pA[:, tl * 1024:(tl + 1) * 1024], ALU.mult, ALU.add,
                  [t_ybr[tl], t_pA], [t_ybr[tl]])
        for tl in range(2):
            ti = g * 2 + tl
            rec = Rec(S)
            layer_norm(rec, ybr[tl], t_ybr[tl], st, t_st, junk, t_junk, lng, lnb, t_ln)
            th.extend(rec.ops)
            th.append(lambda tl=tl, ti=ti: S.dma("sp", out_d[ti * 128:(ti + 1) * 128, :], ybr[tl][:], [t_ybr[tl]], ()))
        return th

    o0, l0 = build_sel(0)
    for f in o0 + l0:
        f()
    pend = []
    for g in range(NG):
        gd_build(g, pend)
        o1, l1 = build_sel(g + 1) if g + 1 < NG else ([], [])
        dense(g, o1, l1)
        pend = final(g)
    for f in pend:
        f()


def host_inputs(inputs, b):
    f = np.float32
    x = np.ascontiguousarray(inputs["x"][b], dtype=f)
    cp = np.stack([inputs["conv_w"][0][0], inputs["conv_w"][0][1], inputs["conv_w"][0][2], inputs["conv_w"][0][3],
                   inputs["conv_b"][0], inputs["lru_ba"][0], inputs["lru_bx"][0], inputs["lru_lambda"][0]], axis=-1)
    cpar = np.ascontiguousarray(cp.reshape(8, 128, 8).transpose(1, 0, 2), dtype=f)
    keys = inputs["peer_keys"][0]
    keysT = np.ascontiguousarray(keys.reshape(16, 128, 128).transpose(2, 0, 1), dtype=f)
    u = inputs["peer_u"][0]
    u_l = np.ascontiguousarray(u.reshape(128, 128, 8, 128).transpose(0, 3, 2, 1), dtype=f).reshape(16384, 1024)
    lnp = np.stack([inputs["ln1_g"][0], inputs["ln1_b"][0], inputs["ln2_g"][0], inputs["ln2_b"][0]], axis=0)
    return {
        "xT": np.ascontiguousarray(x.T),
        "x": x,
        "memT": np.ascontiguousarray(inputs["mem"][b].T, dtype=f),
        "w_in": np.ascontiguousarray(inputs["w_in"][0], dtype=f),
        "cpar": cpar,
        "lru_wa": np.ascontiguousarray(inputs["lru_wa"][0], dtype=f),
        "lru_wx": np.ascontiguousarray(inputs["lru_wx"][0], dtype=f),
        "w_mem_kv": np.ascontiguousarray(inputs["w_mem_kv"][0], dtype=f),
        "w_br_attn": np.ascontiguousarray(inputs["w_br_attn"][0], dtype=f),
        "w_br_lru": np.ascontiguousarray(inputs["w_br_lru"][0], dtype=f),
        "w_br_mem": np.ascontiguousarray(inputs["w_br_mem"][0], dtype=f),
        "w_out": np.ascontiguousarray(inputs["w_out"][0], dtype=f),
        "b_gate": np.ascontiguousarray(inputs["b_gate"][0].reshape(1, 3072), dtype=f),
        "lnp": np.ascontiguousarray(lnp, dtype=f),
        "peer_wqT": np.ascontiguousarray(inputs["peer_wq"][0].T, dtype=f),
        "keysT": keysT,
        "u_l": u_l,
        "peer_v": np.ascontiguousarray(inputs["peer_v"][0], dtype=f),
    }


def kernel(**inputs):
    inputs = {k: np.asarray(v) for k, v in inputs.items()}
    nc = build_program()
    shared = host_inputs(inputs, 0)
    in_maps = []
    for b in range(8):
        m = dict(shared)
        x = np.ascontiguousarray(inputs["x"][b], dtype=np.float32)
        m["x"] = x
        m["xT"] = np.ascontiguousarray(x.T)
        m["memT"] = np.ascontiguousarray(inputs["mem"][b].T, dtype=np.float32)
        in_maps.append(m)
    res = run_bass_kernel_spmd(nc, in_maps, core_ids=list(range(8)))
    out = np.stack([np.asarray(r["out"]) for r in res.results], axis=0)
    return out.astype(np.float32)
```

```python
import math
import os
DBG_P1 = int(os.environ.get('DBG_P1', '99'))
from contextlib import ExitStack

import numpy as np
import concourse.bass as bass
import concourse.mybir as mybir
from concourse.bass_utils import run_bass_kernel_spmd

F32 = mybir.dt.float32
BF16 = mybir.dt.bfloat16
U32 = mybir.dt.uint32
AF = mybir.ActivationFunctionType
ALU = mybir.AluOpType
AX = mybir.AxisListType

S_LEN = 4096
D = 1024
NT = S_LEN // 128
ALPHA = 2.0 ** 0.25
LN_EPS = 1e-5
QOFF, KOFF, VOFF, XROFF, YGOFF, MQOFF, GLOFF = 0, 1536, 3072, 4608, 5632, 6656, 7168
DILS = (1, 4, 16)
ENGS = ("pe", "act", "dve", "pool", "sp")


class Trk:
    __slots__ = ("w", "r")

    def __init__(self):
        self.w = None
        self.r = {}


def trks(n):
    return [Trk() for _ in range(n)]


class Sched:
    def __init__(self, nc, es):
        self.nc = nc
        self.semh = {}
        for e in ("pe", "act", "dve", "pool"):
            self.semh[e] = es.enter_context(nc.semaphore("s_" + e))
        self.dq = {"sp": [], "pool": [], "act": []}
        nd = {"sp": 10, "pool": 8, "act": 2}
        for q, n in nd.items():
            for i in range(n):
                k = "d_%s%d" % (q, i)
                self.semh[k] = es.enter_context(nc.semaphore(k))
                self.dq[q].append(k)
        self.cnt = {k: 0 for k in self.semh}
        self.rr = {q: 0 for q in self.dq}
        self.seen = {e: {} for e in ENGS}
        self.prog = {e: [] for e in ENGS}
        self.pending = {e: [] for e in ENGS}

    def _waits(self, eng, reads, writes, extra=(), strict=False):
        deps = {}

        def need(tok):
            if tok is None:
                return
            k, v = tok
            if deps.get(k, 0) < v:
                deps[k] = v
        for b in reads:
            need(b.w)
        for b in writes:
            if b.w is not None and (strict or b.w[0] != eng):
                need(b.w)
            for k, v in b.r.items():
                if strict or k != eng:
                    need((k, v))
        for tok in extra:
            need(tok)
        out = []
        sn = self.seen[eng]
        for k, v in deps.items():
            if sn.get(k, 0) < v:
                sn[k] = v
                out.append((k, v))
        return out

    def emit(self, eng, fn, reads=(), writes=(), sig=True):
        waits = self._waits(eng, reads, writes, strict=(eng != 'pe'))
        ticket = self.cnt[eng] + 1
        if sig:
            self.cnt[eng] = ticket
        self.prog[eng].append((waits, fn, (eng, 1) if sig else None))
        for b in reads:
            if b.r.get(eng, 0) < ticket:
                b.r[eng] = ticket
        for b in writes:
            b.w = (eng, ticket)
            b.r = {}

    def dma(self, q, out, in_, reads=(), writes=()):
        ks = self.dq[q]
        k = ks[self.rr[q] % len(ks)]
        self.rr[q] += 1
        extra = [(k, self.cnt[k])] if self.cnt[k] > 0 else []
        waits = self._waits(q, reads, writes, extra, strict=True)
        self.cnt[k] += 16
        v = self.cnt[k]
        self.prog[q].append((waits, lambda e: e.dma_start(out=out, in_=in_), (k, 16)))
        for b in reads:
            if b.r.get(k, 0) < v:
                b.r[k] = v
        for b in writes:
            b.w = (k, v)
            b.r = {}

    def barrier(self):
        for e in ENGS:
            waits = []
            for k, v in self.cnt.items():
                if v > 0 and k != e and self.seen[e].get(k, 0) < v:
                    self.seen[e][k] = v
                    waits.append((k, v))
            if waits:
                self.prog[e].append((waits, None, None))

    def flush(self):
        nc = self.nc
        semh = self.semh
        prog = self.prog

        def replay(name, e):
            for waits, fn, inc in prog[name]:
                for k, v in waits:
                    e.wait_ge(semh[k], v)
                if fn is None:
                    continue
                ins = fn(e)
                if inc is not None:
                    ins.then_inc(semh[inc[0]], inc[1])

        with nc.Block() as block:
            @block.tensor
            def _(e):
                replay("pe", e)

            @block.scalar
            def _(e):
                replay("act", e)

            @block.vector
            def _(e):
                replay("dve", e)

            @block.gpsimd
            def _(e):
                replay("pool", e)

            @block.sync
            def _(e):
                replay("sp", e)
        self.prog = {e: [] for e in ENGS}

    def mm(self, out, lhsT, rhs, start, stop, reads=(), writes=(), sig=None):
        if sig is None:
            sig = stop
        self.emit("pe", lambda e: e.matmul(out, lhsT, rhs, start=start, stop=stop), reads, writes, sig)

    def tr(self, out, in_, ident, reads=(), writes=()):
        self.emit("pe", lambda e: e.transpose(out, in_, ident), reads, writes)

    def actf(self, out, in_, func, reads=(), writes=(), bias=0.0, scale=1.0, accum=None, eng="act"):
        if accum is None:
            self.emit("act", lambda e: e.activation(out, in_, func, bias=bias, scale=scale), reads, writes)
        else:
            self.emit("act", lambda e: e.activation(out, in_, func, bias=bias, scale=scale, accum_out=accum),
                      reads, writes)

    def tt(self, eng, out, in0, in1, op, reads=(), writes=()):
        self.emit(eng, lambda e: e.tensor_tensor(out, in0, in1, op), reads, writes)

    def ts(self, eng, out, in0, s1, s2, op0, op1=None, reads=(), writes=()):
        if op1 is None:
            self.emit(eng, lambda e: e.tensor_scalar(out, in0, s1, None, op0), reads, writes)
        else:
            self.emit(eng, lambda e: e.tensor_scalar(out, in0, s1, s2, op0, op1), reads, writes)

    def stt(self, eng, out, in0, scalar, in1, op0, op1, reads=(), writes=()):
        self.emit(eng, lambda e: e.scalar_tensor_tensor(out, in0, scalar, in1, op0, op1), reads, writes)

    def cp(self, eng, out, in_, reads=(), writes=()):
        if eng == "act":
            self.emit("act", lambda e: e.copy(out, in_), reads, writes)
        else:
            self.emit(eng, lambda e: e.tensor_copy(out, in_), reads, writes)

    def memset(self, eng, ap, val, writes=()):
        self.emit(eng, lambda e: e.memset(ap, val), (), writes)


class Rec:
    def __init__(self, S):
        self.S = S
        self.ops = []

    def __getattr__(self, name):
        f = getattr(self.S, name)

        def wrap(*a, **k):
            self.ops.append(lambda: f(*a, **k))
        return wrap


class Ring:
    def __init__(self, items):
        self.items = items
        self.i = 0

    def next(self):
        it = self.items[self.i % len(self.items)]
        self.i += 1
        return it


def build_program(stop_after=99, debug=False):
    nc = bass.Bass("TRN2", target_bir_lowering=False)
    es = ExitStack()

    def din(name, shape, dt=F32):
        return nc.dram_tensor(name, list(shape), dt, kind="ExternalInput").ap()

    skind = "ExternalOutput" if debug else "Internal"
    xT_d = din("xT", [D, S_LEN])
    x_d = din("x", [S_LEN, D])
    memT_d = din("memT", [D, 256])
    w_in_d = din("w_in", [D, 10240])
    cpar_d = din("cpar", [128, 8, 8])
    wa_d = din("lru_wa", [16, 64, 64])
    wx_d = din("lru_wx", [16, 64, 64])
    wkv_d = din("w_mem_kv", [D, 1024])
    wbra_d = din("w_br_attn", [512, D])
    wbrl_d = din("w_br_lru", [1024, D])
    wbrm_d = din("w_br_mem", [512, D])
    wout_d = din("w_out", [D, D])
    bgate_d = din("b_gate", [1, 3072])
    lnp_d = din("lnp", [4, D])
    wqT_d = din("peer_wqT", [2048, D])
    keysT_d = din("keysT", [128, 16, 128])
    ul_d = din("u_l", [16384, 1024])
    v_d = din("peer_v", [16384, 1024])
    out_d = nc.dram_tensor("out", [S_LEN, D], F32, kind="ExternalOutput").ap()
    br_s = nc.dram_tensor("br_s", [2048, S_LEN], BF16, kind=skind).ap()
    x1_s = nc.dram_tensor("x1_s", [S_LEN, D], F32, kind=skind).ap()
    uv_s = nc.dram_tensor("uv_s", [16384, 2048], BF16, kind="Internal").ap()

    S = Sched(nc, es)

    def sb(st, name, shape, dt):
        return st.enter_context(nc.sbuf_tensor("sb_" + name, list(shape), dt))

    def psb(st, name, shape=(128, 512), dt=F32):
        return st.enter_context(nc.psum_tensor("ps_" + name, list(shape), dt))

    ident = sb(es, "ident", [128, 128], F32)
    ones_bf = sb(es, "ones_bf", [128, 128], BF16)
    iota_i = sb(es, "iota_i", [128, 128], mybir.dt.int32)
    iota_f = sb(es, "iota_f", [128, 128], F32)
    iota_p = sb(es, "iota_p", [128, 1], F32)
    t_const = Trk()
    S.emit("pool", lambda e: e.iota(iota_i[:], pattern=[[1, 128]], base=0, channel_multiplier=0), (), [t_const])
    S.cp("dve", iota_f[:], iota_i[:], [t_const], [t_const])
    S.emit("pool", lambda e: e.iota(iota_i[:, 0:1], pattern=[[1, 1]], base=0, channel_multiplier=1), [t_const], [t_const])
    S.cp("dve", iota_p[:], iota_i[:, 0:1], [t_const], [t_const])
    S.ts("dve", ident[:], iota_f[:], iota_p[:, 0:1], None, ALU.is_equal, reads=[t_const], writes=[t_const])
    S.memset("dve", ones_bf[:], 1.0, [t_const])
    iota_b = sb(es, "iota_b", [128, 128], BF16)
    S.cp("dve", iota_b[:], iota_f[:], [t_const], [t_const])

    t_us, t_vs = Trk(), Trk()
    NCH = 32
    rows = 16384 // NCH

    cast_i = [0]

    def next_cast(n=1):
        for _ in range(n):
            i = cast_i[0]
            if stop_after < 4 or i >= 2 * NCH:
                return
            cast_i[0] += 1
            j = i // 2
            if i % 2 == 0:
                S.dma("pool", uv_s[j * rows:(j + 1) * rows, 0:1024], ul_d[j * rows:(j + 1) * rows, :], (), ())
            else:
                S.dma("pool", uv_s[j * rows:(j + 1) * rows, 1024:2048], v_d[j * rows:(j + 1) * rows, :], (), ())

    with ExitStack() as p12:
        xT = sb(p12, "xT", [128, 8, S_LEN], BF16)
        t_xT = trks(8)
        xT_v = xT_d.rearrange("(k p) t -> p k t", p=128)
        for k in range(8):
            S.dma("pool", xT[:, k, :], xT_v[:, k, :], (), [t_xT[k]])
        w_in_v = w_in_d.rearrange("(k p) n -> p k n", p=128)

        with ExitStack() as p1:
            cpar = sb(p1, "cpar", [128, 8, 8], F32)
            cneg = sb(p1, "cneg", [128, 8], F32)
            cneg2 = sb(p1, "cneg2", [128, 8], F32)
            wa_bd = sb(p1, "wa_bd", [128, 8, 128], BF16)
            wx_bd = sb(p1, "wx_bd", [128, 8, 128], BF16)
            t_cp, t_wbd = Trk(), Trk()
            S.dma("sp", cpar[:], cpar_d, (), [t_cp])
            S.actf(cneg[:], cpar[:, :, 7], AF.Exp, [t_cp], [t_cp], scale=-1.0)
            S.actf(cneg[:], cneg[:], AF.Ln, [t_cp], [t_cp], bias=1.0)
            S.ts("dve", cneg2[:], cneg[:], -16.0, None, ALU.mult, reads=[t_cp], writes=[t_cp])
            S.ts("dve", cneg[:], cneg[:], -8.0, None, ALU.mult, reads=[t_cp], writes=[t_cp])
            S.memset("dve", wa_bd[:], 0.0, [t_wbd])
            S.memset("dve", wx_bd[:], 0.0, [t_wbd])
            for (wd, wsb) in ((wa_d, wa_bd), (wx_d, wx_bd)):
                wv = wd.rearrange("(c two) i j -> two i c j", two=2)
                S.dma("pool", wsb[0:64, :, 0:64], wv[0], (), [t_wbd])
                S.dma("pool", wsb[64:128, :, 64:128], wv[1], (), [t_wbd])

            HL = 1024
            NBC = S_LEN // HL
            NSET = 4
            XR = [sb(p1, "XR%d" % i, [128, HL + 4], F32) for i in range(NSET)]
            XC = [sb(p1, "XC%d" % i, [128, HL], F32) for i in range(NSET)]
            XCb = [sb(p1, "XCb%d" % i, [128, HL], BF16) for i in range(NSET)]
            RA = [sb(p1, "RA%d" % i, [128, HL], F32) for i in range(NSET)]
            IB = [sb(p1, "IB%d" % i, [128, HL], F32) for i in range(NSET)]
            MU = [sb(p1, "MU%d" % i, [128, HL], F32) for i in range(NSET)]
            YG = [sb(p1, "YG%d" % i, [128, HL], F32) for i in range(NSET)]
            REC = [sb(p1, "REC%d" % i, [128, HL], BF16) for i in range(NSET)]
            wxy = [sb(p1, "wxy%d" % i, [128, 8, 256], BF16) for i in range(2)]
            t_wxy = trks(2)
            t_XR, t_XC, t_XCb, t_RA, t_IB, t_MU, t_YG, t_REC = [trks(NSET) for _ in range(8)]
            pp = [psb(p1, "p1ps%d" % i) for i in range(8)]
            t_pp = trks(8)
            ppr = Ring(list(zip(pp, t_pp)))

            def block_ops(c, hh, it):
                r1, r2, r3 = Rec(S), Rec(S), Rec(S)
                wt, t_w = wxy[c % 2], t_wxy[c % 2]
                if hh == 0:
                    for cn in ([0, 1] if c == 0 else [c + 1]):
                        if cn < 8:
                            r1.dma("pool", wxy[cn % 2][:, :, 0:128],
                                   w_in_v[:, :, XROFF + cn * 128:XROFF + (cn + 1) * 128], (), [t_wxy[cn % 2]])
                            r1.dma("pool", wxy[cn % 2][:, :, 128:256],
                                   w_in_v[:, :, YGOFF + cn * 128:YGOFF + (cn + 1) * 128], (), [t_wxy[cn % 2]])
                pb = it % NSET
                pv = (it - 1) % NSET
                xr, xc, xcb, ra, ib, mu, yg, rec = XR[pb], XC[pb], XCb[pb], RA[pb], IB[pb], MU[pb], YG[pb], REC[pb]
                for which, (dst, t_dst, off) in enumerate(((xr, t_XR[pb], 4), (yg, t_YG[pb], 0))):
                    for tb in range(HL // 512):
                        ps, t_ps = ppr.next()
                        t0 = hh * HL + tb * 512
                        for k in range(8):
                            r1.mm(ps[:], wt[:, k, which * 128:(which + 1) * 128], xT[:, k, t0:t0 + 512],
                                  k == 0, k == 7, [t_w, t_xT[k]], [t_ps])
                        eng = "act" if tb % 2 == 0 else "dve"
                        r1.cp(eng, dst[:, off + tb * 512: off + (tb + 1) * 512], ps[:], [t_ps], [t_dst])
                if hh == 0:
                    r2.memset("dve", xr[:, 0:4], 0.0, [t_XR[pb]])
                else:
                    r2.cp("dve", xr[:, 0:4], XR[pv][:, HL:HL + 4], [t_XR[pv]], [t_XR[pb]])
                r2.actf(xc[:], xr[:, 4:4 + HL], AF.Identity, [t_XR[pb], t_cp], [t_XC[pb]],
                        bias=cpar[:, c, 4:5], scale=cpar[:, c, 3:4])
                for j in range(3):
                    r2.stt("dve", xc[:], xr[:, 1 + j:1 + j + HL], cpar[:, c, j:j + 1], xc[:],
                           ALU.mult, ALU.add, [t_XR[pb], t_cp, t_XC[pb]], [t_XC[pb]])
                r2.cp("pool", xcb[:], xc[:], [t_XC[pb]], [t_XCb[pb]])
                for (wsb, dst, t_dst, bcol) in ((wa_bd, ra, t_RA[pb], 5), (wx_bd, ib, t_IB[pb], 6)):
                    for tb in range(HL // 512):
                        ps, t_ps = ppr.next()
                        r2.mm(ps[:], wsb[:, c, :], xcb[:, tb * 512:(tb + 1) * 512], True, True,
                              [t_wbd, t_XCb[pb]], [t_ps])
                        r2.actf(dst[:, tb * 512:(tb + 1) * 512], ps[:], AF.Sigmoid, [t_ps, t_cp], [t_dst],
                                bias=cpar[:, c, bcol:bcol + 1])
                r3.actf(mu[:], ra[:], AF.Exp, [t_RA[pb], t_cp], [t_MU[pb]], scale=cneg2[:, c:c + 1])
                r3.actf(ra[:], ra[:], AF.Exp, [t_RA[pb], t_cp], [t_RA[pb]], scale=cneg[:, c:c + 1])
                r3.actf(mu[:], mu[:], AF.Sqrt, [t_MU[pb]], [t_MU[pb]], bias=1.0, scale=-1.0)
                r3.tt("dve", ib[:], ib[:], xc[:], ALU.mult, [t_IB[pb], t_XC[pb]], [t_IB[pb]])
                r3.tt("pool", ib[:], ib[:], mu[:], ALU.mult, [t_IB[pb], t_MU[pb]], [t_IB[pb]])
                if hh == 0:
                    r3.emit("dve", (lambda e, mu=mu, ra=ra, ib=ib: e.tensor_tensor_scan(
                        mu[:], ra[:], ib[:], 0.0, ALU.mult, ALU.add)),
                        [t_RA[pb], t_IB[pb], t_MU[pb]], [t_MU[pb]])
                else:
                    r3.emit("dve", (lambda e, mu=mu, ra=ra, ib=ib, pm=MU[pv]: e.tensor_tensor_scan(
                        mu[:], ra[:], ib[:], pm[:, HL - 1:HL], ALU.mult, ALU.add)),
                        [t_RA[pb], t_IB[pb], t_MU[pb], t_MU[pv]], [t_MU[pb]])
                r3.tt("pool", xc[:], yg[:], yg[:], ALU.mult, [t_YG[pb]], [t_XC[pb]])
                r3.ts("dve", xc[:], xc[:], 0.044715, 1.0, ALU.mult, ALU.add, reads=[t_XC[pb]], writes=[t_XC[pb]])
                r3.tt("pool", xc[:], xc[:], yg[:], ALU.mult, [t_XC[pb], t_YG[pb]], [t_XC[pb]])
                r3.actf(xc[:], xc[:], AF.Sigmoid, [t_XC[pb]], [t_XC[pb]], scale=1.5957691216057308)
                r3.tt("pool", xc[:], xc[:], yg[:], ALU.mult, [t_XC[pb], t_YG[pb]], [t_XC[pb]])
                r3.tt("dve", rec[:], mu[:], xc[:], ALU.mult, [t_MU[pb], t_XC[pb]], [t_REC[pb]])
                r3.dma("sp", br_s[512 + c * 128:512 + (c + 1) * 128, hh * HL:(hh + 1) * HL], rec[:], [t_REC[pb]], ())
                if it % 2 == 0:
                    r3.ops.append(lambda: next_cast(1))
                return r1.ops, r2.ops, r3.ops

            blocks = [(c, hh) for c in range(8 if stop_after >= 1 else 0) for hh in range(NBC)]
            stages = []
            for it in range(len(blocks) + 2):
                if it < len(blocks):
                    stages.append(block_ops(blocks[it][0], blocks[it][1], it))
                lists = []
                if it < len(blocks):
                    lists.append(stages[it][0])
                if 0 <= it - 1 < len(blocks):
                    lists.append(stages[it - 1][1])
                if 0 <= it - 2 < len(blocks):
                    lists.append(stages[it - 2][2])
                n = max(len(l) for l in lists)
                pos = [0] * len(lists)
                for step in range(1, n + 1):
                    for li, l in enumerate(lists):
                        want = (step * len(l)) // n
                        while pos[li] < want:
                            l[pos[li]]()
                            pos[li] += 1
            S.barrier()
            S.flush()

        with ExitStack() as p2:
            QT = sb(p2, "QT", [128, S_LEN], BF16)
            KT = sb(p2, "KT", [128, S_LEN], BF16)
            VV = sb(p2, "VV", [128, 32, 128], BF16)
            ACCN = sb(p2, "ACCN", [128, S_LEN], F32)
            ACCD = sb(p2, "ACCD", [128, S_LEN], F32)
            ATT = sb(p2, "ATT", [128, S_LEN], BF16)
            mask2 = sb(p2, "mask2", [128, 512], BF16)
            mtmp = sb(p2, "mtmp", [128, 128], F32)
            wqkv = [sb(p2, "wqkv%d" % i, [128, 8, 384], BF16) for i in range(2)]
            PT = [sb(p2, "PT%d" % i, [128, 512], BF16) for i in range(3)]
            t_QT, t_KT, t_VV, t_ACCN, t_ACCD, t_ATT, t_mask = trks(7)
            t_wqkv = trks(2)
            t_PT = trks(3)
            pp = [psb(p2, "p2ps%d" % i) for i in range(8)]
            t_pp = trks(8)
            ppr = Ring(list(zip(pp, t_pp)))
            S.ts("dve", mtmp[:], iota_f[:], iota_p[:, 0:1], None, ALU.is_ge, reads=[t_const], writes=[t_mask])
            S.cp("dve", mask2[:, 0:128], mtmp[:], [t_mask], [t_mask])
            S.cp("dve", mask2[:, 256:384], mtmp[:], [t_mask], [t_mask])
            S.ts("dve", mtmp[:], iota_f[:], iota_p[:, 0:1], None, ALU.is_le, reads=[t_const, t_mask], writes=[t_mask])
            S.cp("dve", mask2[:, 128:256], mtmp[:], [t_mask], [t_mask])
            S.cp("dve", mask2[:, 384:512], mtmp[:], [t_mask], [t_mask])
            inv_sqrt = 1.0 / math.sqrt(128.0)
            hi = 0
            for h in range(4 if stop_after >= 2 else 0):
                for g in range(3):
                    d = DILS[g]
                    L = S_LEN // d
                    nb = L // 128
                    wt, t_w = wqkv[hi % 2], t_wqkv[hi % 2]
                    for hn in ([0, 1] if hi == 0 else [hi + 1]):
                        if hn < 12:
                            coln = (hn % 3) * 512 + (hn // 3) * 128
                            for j, off in enumerate((QOFF, KOFF, VOFF)):
                                S.dma("pool", wqkv[hn % 2][:, :, j * 128:(j + 1) * 128],
                                      w_in_v[:, :, off + coln:off + coln + 128], (), [t_wqkv[hn % 2]])
                    hi += 1
                    next_cast(2)
                    col = g * 512 + h * 128
                    for j, (dst, t_dst) in enumerate(((QT, t_QT), (KT, t_KT))):
                        dv = dst[:].rearrange("p (r m) -> p m r", r=d)
                        for tb in range(8):
                            ps, t_ps = ppr.next()
                            for k in range(8):
                                S.mm(ps[:], wt[:, k, j * 128:(j + 1) * 128], xT[:, k, tb * 512:(tb + 1) * 512],
                                     k == 0, k == 7, [t_w, t_xT[k]], [t_ps])
                            m0 = tb * (512 // d)
                            o_ap = dv[:, m0:m0 + 512 // d, :]
                            i_ap = ps[:].rearrange("p (m r) -> p m r", r=d)
                            if j == 0:
                                S.actf(o_ap, i_ap, AF.Copy, [t_ps], [t_dst], scale=inv_sqrt)
                            else:
                                S.cp("dve", o_ap, i_ap, [t_ps], [t_dst])
                    xTg = [xT[:, k, :].rearrange("p (m r) -> p r m", r=d) for k in range(8)]
                    for jb4 in range(8):
                        ps, t_ps = ppr.next()
                        for q4 in range(4):
                            jb = jb4 * 4 + q4
                            r, n = jb // nb, jb % nb
                            for k in range(8):
                                S.mm(ps[:, q4 * 128:(q4 + 1) * 128], xTg[k][:, r, n * 128:(n + 1) * 128],
                                     wt[:, k, 256:384], k == 0, k == 7, [t_w, t_xT[k]], [t_ps],
                                     sig=(k == 7 and q4 == 3))
                        S.cp("act" if jb4 % 2 else "dve", VV[:, jb4 * 4:(jb4 + 1) * 4, :],
                             ps[:].rearrange("p (a b) -> p a b", b=128), [t_ps], [t_VV])
                    accn_v = ACCN[:].rearrange("p (m r) -> p r m", r=d)
                    accd_v = ACCD[:].rearrange("p (m r) -> p r m", r=d)
                    prevPT = None
                    for jp in range(16):
                        ps, t_ps = ppr.next()
                        for u in range(2):
                            jb = jp * 2 + u
                            n = jb % nb
                            ncol = 256 if n < nb - 1 else 128
                            S.mm(ps[:, u * 256:u * 256 + ncol], KT[:, jb * 128:(jb + 1) * 128],
                                 QT[:, jb * 128:jb * 128 + ncol], True, True, [t_KT, t_QT], [t_ps], sig=(u == 1))
                        pt, t_pt = PT[jp % 3], t_PT[jp % 3]
                        wv = 512 if ((jp * 2 + 1) % nb) < nb - 1 else 384
                        S.actf(pt[:, 0:wv], ps[:, 0:wv], AF.Exp, [t_ps], [t_pt])
                        S.tt("pool" if jp % 2 else "dve", pt[:, 0:wv], pt[:, 0:wv], mask2[:, 0:wv], ALU.mult,
                             [t_pt, t_mask], [t_pt])
                        pso, t_pso = ppr.next()
                        for u in range(2):
                            jb = jp * 2 + u
                            n = jb % nb
                            srcs = []
                            if n > 0:
                                if u == 0:
                                    srcs.append((jb - 1, prevPT[0][:, 384:512], prevPT[1]))
                                else:
                                    srcs.append((jb - 1, pt[:, 128:256], t_pt))
                            srcs.append((jb, pt[:, u * 256:u * 256 + 128], t_pt))
                            for half, use_ones in ((0, False), (1, True)):
                                oc = half * 256 + u * 128
                                for si, (kb, p_ap, t_p) in enumerate(srcs):
                                    lhs = ones_bf[:] if use_ones else VV[:, kb, :]
                                    S.mm(pso[:, oc:oc + 128], lhs, p_ap, si == 0, si == len(srcs) - 1,
                                         [t_p, t_VV, t_const], [t_pso], sig=(si == len(srcs) - 1 and half == 1 and u == 1))
                        prevPT = (pt, t_pt)
                        c0 = jp * 256
                        if L >= 256:
                            r0, m0 = c0 // L, c0 % L
                            on = accn_v[:, r0, m0:m0 + 256]
                            od = accd_v[:, r0, m0:m0 + 256]
                            inn, ind = pso[:, 0:256], pso[:, 256:512]
                        else:
                            raise AssertionError
                        if g == 0:
                            S.cp("act", on, inn, [t_pso], [t_ACCN])
                            S.cp("act", od, ind, [t_pso], [t_ACCD])
                        else:
                            S.tt("dve", on, on, inn, ALU.add, [t_pso, t_ACCN], [t_ACCN])
                            S.tt("dve", od, od, ind, ALU.add, [t_pso, t_ACCD], [t_ACCD])
                S.emit("dve", lambda e: e.reciprocal(ACCD[:], ACCD[:]), [t_ACCD], [t_ACCD])
                S.tt("pool", ATT[:], ACCN[:], ACCD[:], ALU.mult, [t_ACCN, t_ACCD], [t_ATT])
                S.dma("sp", br_s[h * 128:(h + 1) * 128, :], ATT[:], [t_ATT], ())

            if stop_after >= 2:
                memT = sb(p2, "memT", [128, 8, 256], BF16)
                wkv = sb(p2, "wkv", [128, 8, 1024], BF16)
                wmq = sb(p2, "wmq", [128, 8, 512], BF16)
                KmT = sb(p2, "KmT", [128, 4, 256], BF16)
                Vm = sb(p2, "Vm", [128, 2, 512], BF16)
                MQ = [sb(p2, "MQ%d" % i, [128, 512], BF16) for i in range(2)]
                PM = [sb(p2, "PM%d" % i, [128, 2, 512], BF16) for i in range(2)]
                MO = [sb(p2, "MO%d" % i, [128, 512], F32) for i in range(2)]
                MOb = [sb(p2, "MOb%d" % i, [128, 512], BF16) for i in range(2)]
                t_memT, t_wkv, t_wmq, t_KmT, t_Vm = trks(5)
                t_MQ, t_PM, t_MO, t_MOb = trks(2), trks(2), trks(2), trks(2)
                S.dma("pool", memT[:], memT_d.rearrange("(k p) m -> p k m", p=128), (), [t_memT])
                S.dma("pool", wkv[:], wkv_d.rearrange("(k p) n -> p k n", p=128), (), [t_wkv])
                S.dma("pool", wmq[:], w_in_v[:, :, MQOFF:MQOFF + 512], (), [t_wmq])
                for hh in range(4):
                    ps, t_ps = ppr.next()
                    for k in range(8):
                        S.mm(ps[:, 0:256], wkv[:, k, hh * 128:(hh + 1) * 128], memT[:, k, :], k == 0, k == 7,
                             [t_wkv, t_memT], [t_ps])
                    S.cp("dve", KmT[:, hh, :], ps[:, 0:256], [t_ps], [t_KmT])
                for mc in range(2):
                    ps, t_ps = ppr.next()
                    for k in range(8):
                        S.mm(ps[:], memT[:, k, mc * 128:(mc + 1) * 128], wkv[:, k, 512:1024], k == 0, k == 7,
                             [t_wkv, t_memT], [t_ps])
                    S.cp("dve", Vm[:, mc, :], ps[:], [t_ps], [t_Vm])
                it = 0
                for tb in range(8):
                    for hh in range(4):
                        b2 = it % 2
                        it += 1
                        ps, t_ps = ppr.next()
                        for k in range(8):
                            S.mm(ps[:], wmq[:, k, hh * 128:(hh + 1) * 128], xT[:, k, tb * 512:(tb + 1) * 512],
                                 k == 0, k == 7, [t_wmq, t_xT[k]], [t_ps])
                        S.actf(MQ[b2][:], ps[:], AF.Copy, [t_ps], [t_MQ[b2]], scale=inv_sqrt)
                        for mc in range(2):
                            ps, t_ps = ppr.next()
                            S.mm(ps[:], KmT[:, hh, mc * 128:(mc + 1) * 128], MQ[b2][:], True, True,
                                 [t_KmT, t_MQ[b2]], [t_ps])
                            S.actf(PM[b2][:, mc, :], ps[:], AF.Exp, [t_ps], [t_PM[b2]])
                        pn, t_pn = ppr.next()
                        pd, t_pd = ppr.next()
                        for mc in range(2):
                            S.mm(pn[:], Vm[:, mc, hh * 128:(hh + 1) * 128], PM[b2][:, mc, :], mc == 0, mc == 1,
                                 [t_Vm, t_PM[b2]], [t_pn])
                        for mc in range(2):
                            S.mm(pd[:], ones_bf[:], PM[b2][:, mc, :], mc == 0, mc == 1, [t_const, t_PM[b2]], [t_pd])
                        S.emit("dve", lambda e, o=MO[b2], i=pd: e.reciprocal(o[:], i[:]), [t_pd], [t_MO[b2]])
                        S.tt("dve", MOb[b2][:], MO[b2][:], pn[:], ALU.mult, [t_MO[b2], t_pn], [t_MOb[b2]])
                        S.dma("sp", br_s[1536 + hh * 128:1536 + (hh + 1) * 128, tb * 512:(tb + 1) * 512], MOb[b2][:],
                              [t_MOb[b2]], ())
            S.barrier()
            S.flush()

    t_br = Trk()

    with ExitStack() as p3:
        wg = sb(p3, "wg", [128, 8, 3072], BF16)
        wbr = sb(p3, "wbr", [128, 16, 1024], BF16)
        wout = sb(p3, "wout", [128, 8, 1024], BF16)
        bg = sb(p3, "bg", [1, 3072], F32)
        ones_f = sb(p3, "ones_f", [1, 128], F32)
        lng = sb(p3, "lng", [128, 1024], F32)
        lnb = sb(p3, "lnb", [128, 1024], F32)
        t_wg, t_wbr, t_wout, t_bg, t_ln = trks(5)
        if stop_after >= 3:
            for k in range(8):
                S.dma("pool", wg[:, k, :], w_in_v[:, k, GLOFF:GLOFF + 3072], (), [t_wg])
            S.dma("pool", wbr[:, 0:4, :], wbra_d.rearrange("(k p) n -> p k n", p=128), (), [t_wbr])
            S.dma("pool", wbr[:, 4:12, :], wbrl_d.rearrange("(k p) n -> p k n", p=128), (), [t_wbr])
            S.dma("pool", wbr[:, 12:16, :], wbrm_d.rearrange("(k p) n -> p k n", p=128), (), [t_wbr])
            S.dma("pool", wout[:], wout_d.rearrange("(k p) n -> p k n", p=128), (), [t_wout])
            S.dma("sp", bg[:], bgate_d, (), [t_bg])
            S.memset("dve", ones_f[:], 1.0, [t_bg])
            S.dma("sp", lng[:], lnp_d[0:1, :].to_broadcast([128, 1024]), (), [t_ln])
            S.dma("sp", lnb[:], lnp_d[1:2, :].to_broadcast([128, 1024]), (), [t_ln])
        xTb = [sb(p3, "xTb%d" % i, [128, 8, 512], BF16) for i in range(2)]
        brT = [sb(p3, "brT%d" % i, [128, 16, 512], BF16) for i in range(2)]
        xres = [sb(p3, "xres%d" % i, [128, 1024], F32) for i in range(2)]
        gate = sb(p3, "gate", [128, 1024], F32)
        macc = sb(p3, "macc", [128, 1024], F32)
        macc2 = sb(p3, "macc2", [128, 1024], F32)
        t_macc2 = Trk()
        mtmp3 = sb(p3, "mtmp3", [128, 1024], F32)
        mT = sb(p3, "mT", [128, 8, 128], BF16)
        yb = [sb(p3, "yb%d" % i, [128, 1024], F32) for i in range(2)]
        junk = sb(p3, "junk3", [128, 1024], F32)
        st = [sb(p3, "st%d" % i, [128, 8], F32) for i in range(2)]
        t_xTb, t_brT, t_xres, t_yb, t_st = trks(2), trks(2), trks(2), trks(2), trks(2)
        t_gate, t_macc, t_mtmp, t_mT, t_junk = trks(5)
        pg = [psb(p3, "p3g%d" % i) for i in range(2)]
        pj = [psb(p3, "p3j%d" % i) for i in range(2)]
        ptr = [psb(p3, "p3t%d" % i) for i in range(2)]
        pm = [psb(p3, "p3m%d" % i) for i in range(2)]
        t_pg, t_pj, t_ptr, t_pm = trks(2), trks(2), trks(2), trks(2)
        xT_v2 = xT_d.rearrange("(k p) t -> p k t", p=128)
        br_v = br_s.rearrange("(k p) t -> p k t", p=128)
        brch = ((0, 4), (4, 12), (12, 16))
        maccr = [macc, macc2]
        t_maccr = [t_macc, t_macc2]

        def tile_ops(ti):
            tb, tl = ti // 4, ti % 4
            b2 = tb % 2
            b1 = ti % 2
            mac, t_mac = maccr[b1], t_maccr[b1]
            RA_, RB_ = Rec(S), Rec(S)
            if tl == 0:
                for tbn in ([0, 1] if tb == 0 else [tb + 1]):
                    if tbn < 8:
                        RA_.dma("pool", xTb[tbn % 2][:], xT_v2[:, :, tbn * 512:(tbn + 1) * 512], (), [t_xTb[tbn % 2]])
                        RA_.dma("sp", brT[tbn % 2][:], br_v[:, :, tbn * 512:(tbn + 1) * 512], [t_br], [t_brT[tbn % 2]])
            tsl = slice(tl * 128, (tl + 1) * 128)
            RA_.dma("sp", xres[b1][:], x_d[ti * 128:(ti + 1) * 128, :], (), [t_xres[b1]])
            RA_.ops.append(lambda: next_cast(1))
            for br in range(3):
                for hf in range(2):
                    cs = br * 1024 + hf * 512
                    RA_.mm(pg[hf][:], ones_f[0:1, :], bg[0:1, cs:cs + 512], True, False, [t_bg], [t_pg[hf]], sig=False)
                    for k in range(8):
                        RA_.mm(pg[hf][:], xTb[b2][:, k, tsl], wg[:, k, cs:cs + 512], False, k == 7,
                               [t_xTb[b2], t_wg], [t_pg[hf]])
                    RA_.actf(gate[:, hf * 512:(hf + 1) * 512], pg[hf][:], AF.Sigmoid, [t_pg[hf]], [t_gate])
                    c0, c1 = brch[br]
                    for c in range(c0, c1):
                        RA_.mm(pj[hf][:], brT[b2][:, c, tsl], wbr[:, c, hf * 512:(hf + 1) * 512], c == c0, c == c1 - 1,
                               [t_brT[b2], t_wbr], [t_pj[hf]])
                    hs = slice(hf * 512, (hf + 1) * 512)
                    if br == 0:
                        RA_.tt("dve", mac[:, hs], gate[:, hs], pj[hf][:], ALU.mult, [t_gate, t_pj[hf]], [t_mac])
                    else:
                        RA_.tt("dve", mtmp3[:, hs], gate[:, hs], pj[hf][:], ALU.mult, [t_gate, t_pj[hf]], [t_mtmp])
                        RA_.tt("pool", mac[:, hs], mac[:, hs], mtmp3[:, hs], ALU.add, [t_mtmp, t_mac], [t_mac])
            for k in range(8):
                RB_.tr(ptr[k // 4][:, (k % 4) * 128:(k % 4 + 1) * 128], mac[:, k * 128:(k + 1) * 128], ident[:],
                       [t_mac, t_const], [t_ptr[k // 4]])
            for hf in range(2):
                RB_.cp("act", mT[:, hf * 4:(hf + 1) * 4, :], ptr[hf][:].rearrange("p (a b) -> p a b", b=128),
                       [t_ptr[hf]], [t_mT])
            for hf in range(2):
                for k in range(8):
                    RB_.mm(pm[hf][:], mT[:, k, :], wout[:, k, hf * 512:(hf + 1) * 512], k == 0, k == 7,
                           [t_mT, t_wout], [t_pm[hf]])
                RB_.stt("dve", yb[b1][:, hf * 512:(hf + 1) * 512], xres[b1][:, hf * 512:(hf + 1) * 512], ALPHA,
                        pm[hf][:], ALU.mult, ALU.add, [t_xres[b1], t_pm[hf]], [t_yb[b1]])
            layer_norm(RB_, yb[b1], t_yb[b1], st[b1], t_st[b1], junk, t_junk, lng, lnb, t_ln)
            RB_.dma("sp", x1_s[ti * 128:(ti + 1) * 128, :], yb[b1][:], [t_yb[b1]], ())
            return RA_.ops, RB_.ops

        prevB3 = []
        for ti in range(32 if stop_after >= 3 else 0):
            A3, B3 = tile_ops(ti)
            nA, nB = len(A3), len(prevB3)
            jb = 0
            for ia, fa in enumerate(A3):
                fa()
                want = ((ia + 1) * nB) // nA
                while jb < want:
                    prevB3[jb]()
                    jb += 1
            while jb < nB:
                prevB3[jb]()
                jb += 1
            prevB3 = B3
        for fb in prevB3:
            fb()
        next_cast(2 * NCH)
        S.barrier()
        S.flush()

    if stop_after >= 4:
        with ExitStack() as p4:
            peer_phase(nc, S, p4, sb, psb, ident, iota_f, iota_b, t_const, x1_s, wqT_d, keysT_d, uv_s, lnp_d, out_d)
            S.barrier()
            S.flush()
    else:
        S.barrier()
        S.flush()
    es.close()
    return nc


def layer_norm(S, y, t_y, st, t_st, junk, t_junk, lng, lnb, t_ln):
    S.emit("dve", lambda e: e.tensor_reduce(st[:, 0:1], y[:], AX.X, ALU.add), [t_y], [t_st])
    S.ts("dve", st[:, 1:2], st[:, 0:1], -1.0 / D, None, ALU.mult, reads=[t_st], writes=[t_st])
    S.actf(y[:], y[:], AF.Identity, [t_y, t_st], [t_y], bias=st[:, 1:2])
    S.tt("pool", junk[:], y[:], y[:], ALU.mult, [t_y], [t_junk])
    S.emit("dve", lambda e: e.tensor_reduce(st[:, 2:3], junk[:], AX.X, ALU.add), [t_junk, t_st], [t_st])
    S.ts("dve", st[:, 3:4], st[:, 2:3], 1.0 / D, LN_EPS, ALU.mult, ALU.add, reads=[t_st], writes=[t_st])
    S.actf(st[:, 3:4], st[:, 3:4], AF.Sqrt, [t_st], [t_st])
    S.emit("dve", lambda e: e.reciprocal(st[:, 3:4], st[:, 3:4]), [t_st], [t_st])
    S.actf(y[:], y[:], AF.Copy, [t_y, t_st], [t_y], scale=st[:, 3:4])
    S.tt("pool", y[:], y[:], lng[:], ALU.mult, [t_y, t_ln], [t_y])
    S.tt("pool", y[:], y[:], lnb[:], ALU.add, [t_y, t_ln], [t_y])


def peer_phase(nc, S, p4, sb, psb, ident, iota_f, iota_b, t_const, x1_s, wqT_d, keysT_d, uv_s, lnp_d, out_d):
    TG = 256
    NG = S_LEN // TG
    Wsc = sb(p4, "Wsc", [128, 8, 2048], BF16)
    lng = sb(p4, "lng2", [128, 1024], F32)
    lnb = sb(p4, "lnb2", [128, 1024], F32)
    t_Wsc, t_ln = trks(2)
    S.dma("sp", lng[:], lnp_d[2:3, :].to_broadcast([128, 1024]), (), [t_ln])
    S.dma("sp", lnb[:], lnp_d[3:4, :].to_broadcast([128, 1024]), (), [t_ln])
    with ExitStack() as pre:
        wqT = sb(pre, "wqT", [128, 16, 1024], F32)
        keysT = sb(pre, "keysTf", [128, 16, 128], F32)
        t_wqT, t_keys = trks(2)
        wqT_v = wqT_d.rearrange("(e c) d -> c e d", c=128)
        for e4 in range(4):
            S.dma("sp", wqT[:, e4 * 4:(e4 + 1) * 4, :], wqT_v[:, e4 * 4:(e4 + 1) * 4, :], (), [t_wqT])
        S.dma("sp", keysT[:], keysT_d, (), [t_keys])
        pw = [psb(pre, "p4w%d" % i) for i in range(4)]
        t_pw = trks(4)
        i = 0
        for k in range(8):
            for e4 in range(4):
                ps, t_ps = pw[i % 4], t_pw[i % 4]
                i += 1
                for ee in range(4):
                    e_ = e4 * 4 + ee
                    S.mm(ps[:, ee * 128:(ee + 1) * 128], wqT[:, e_, k * 128:(k + 1) * 128], keysT[:, e_, :], True, True,
                         [t_wqT, t_keys], [t_ps], sig=(ee == 3))
                S.cp("act" if i % 2 else "dve", Wsc[:, k, e4 * 512:(e4 + 1) * 512], ps[:], [t_ps], [t_Wsc])
        S.barrier()
        S.flush()

    Gd = sb(p4, "Gd", [128, TG, 128], BF16)
    x1r = [sb(p4, "x1r%d" % i, [128, 1024], F32) for i in range(2)]
    x1Tr = [sb(p4, "x1Tr%d" % i, [128, 8, TG], BF16) for i in range(2)]
    SELTr = [sb(p4, "SELT%d" % i, [128, 3, 128], F32) for i in range(4)]
    BIG = sb(p4, "BIG", [128, 2048], F32)
    SC = BIG[:].rearrange("p (a b) -> p a b", b=128)
    CAND = BIG[:].rearrange("p (h c) -> p h c", c=256)
    OH = BIG[:].rearrange("p (h k i) -> p h k i", k=16, i=16)
    SC2 = sb(p4, "SC2", [128, 256], F32)
    M1 = sb(p4, "M1", [128, 16, 16], F32)
    IDX = sb(p4, "IDX", [128, 16, 16], U32)
    IDXF = sb(p4, "IDXF", [128, 16, 16], F32)
    CS = sb(p4, "CS", [128, 8, 16], F32)
    CI = sb(p4, "CI", [128, 8, 16], U32)
    II = sb(p4, "II", [128, 8, 16], U32)
    IIF = sb(p4, "IIF", [128, 2, 8, 16], F32)
    SELr = [sb(p4, "SEL%d" % i, [128, 3, 128], F32) for i in range(2)]
    sm = sb(p4, "sm", [128, 8, 4], F32)
    ABoh = [sb(p4, "ABoh%d" % i, [128, 2, 16, 128], BF16) for i in range(2)]
    Boh = [sb(p4, "Boh%d" % i, [128, 16, 128], BF16) for i in range(2)]
    NUV = 5
    UV = [sb(p4, "UV%d" % i, [128, 2048], BF16) for i in range(NUV)]
    HG = [sb(p4, "HG%d" % i, [128, TG], BF16) for i in range(4)]
    WW = [sb(p4, "WW%d" % i, [128, TG], BF16) for i in range(4)]
    ybr = [sb(p4, "yb4_%d" % i, [128, 1024], F32) for i in range(2)]
    junk = sb(p4, "junk4", [128, 1024], F32)
    st = sb(p4, "st4", [128, 8], F32)
    (t_Gd, t_big, t_SC2, t_M1, t_IDX, t_CS, t_CI, t_II, t_sm, t_junk, t_st) = trks(11)
    t_ybr = trks(2)
    t_SELr = trks(2)
    t_x1r, t_x1Tr, t_SELT = trks(2), trks(2), trks(4)
    t_A, t_B, t_Bt = trks(2), trks(2), trks(2)
    t_UV, t_HG, t_WW = trks(5), trks(4), trks(4)
    t_pBh = trks(4)
    pA = psb(p4, "p4A", [128, 2048])
    pB = [psb(p4, "p4B%d" % i) for i in range(2)]
    pC = [psb(p4, "p4C%d" % i) for i in range(2)]
    t_pA, = trks(1)
    t_pB, t_pC = trks(2), trks(2)
    uv_v = uv_s.rearrange("(a p) f -> p a f", p=128)

    def build_sel(g):
        ops = []
        late = []

        def op(f, *a, **k):
            ops.append(lambda: f(*a, **k))

        def op_late(f, *a, **k):
            late.append(lambda: f(*a, **k))
        gb = g % 2
        xtb, t_xtb = x1Tr[gb], t_x1Tr[gb]
        for tl in range(2):
            ti = g * 2 + tl
            xb, t_xb = x1r[tl], t_x1r[tl]
            SELT, t_selt = SELTr[ti % 4], t_SELT[ti % 4]
            SEL, t_SEL = SELr[tl], t_SELr[tl]
            op(S.dma, "sp", xb[:], x1_s[ti * 128:(ti + 1) * 128, :], (), [t_xb])
            for hf in range(2):
                for k4 in range(4):
                    k = hf * 4 + k4
                    op(S.tr, pC[1][:, k4 * 128:(k4 + 1) * 128], xb[:, k * 128:(k + 1) * 128], ident[:],
                       [t_xb, t_const], [t_pC[1]])
                op(S.cp, "dve", xtb[:, hf * 4:(hf + 1) * 4, tl * 128:(tl + 1) * 128],
                   pC[1][:].rearrange("p (a b) -> p a b", b=128), [t_pC[1]], [t_xtb])
            for pc in range(4):
                for k in range(8):
                    op(S.mm, pC[1][:], xtb[:, k, tl * 128:(tl + 1) * 128], Wsc[:, k, pc * 512:(pc + 1) * 512],
                       k == 0, k == 7, [t_xtb, t_Wsc], [t_pC[1]])
                op(S.cp, "dve", BIG[:, pc * 512:(pc + 1) * 512], pC[1][:], [t_pC[1]], [t_big])
            for e_ in range(16):
                op(S.emit, "dve", (lambda e, e_=e_: e.max(M1[:, e_, 0:8], SC[:, e_, :])), [t_big], [t_M1])
                op(S.emit, "dve", (lambda e, e_=e_: e.match_replace(SC2[:, 0:128], M1[:, e_, 0:8], SC[:, e_, :], -1e30)),
                   [t_big, t_M1], [t_SC2])
                op(S.emit, "dve", (lambda e, e_=e_: e.max(M1[:, e_, 8:16], SC2[:, 0:128])), [t_SC2], [t_M1])
                op(S.emit, "dve", (lambda e, e_=e_: e.max_index(IDX[:, e_, 0:8], M1[:, e_, 0:8], SC[:, e_, :])),
                   [t_big, t_M1], [t_IDX])
                op(S.emit, "dve", (lambda e, e_=e_: e.max_index(IDX[:, e_, 8:16], M1[:, e_, 8:16], SC[:, e_, :])),
                   [t_big, t_M1], [t_IDX])
            op(S.cp, "dve", IDXF[:], IDX[:], [t_IDX], [t_IDX])
            M1v = M1[:].rearrange("p (h two) k -> p h two k", two=2)
            op(S.tt, "dve", CAND.rearrange("p h (i j) -> p h i j", j=16),
               M1v[:, :, 0, :].unsqueeze(3).to_broadcast([128, 8, 16, 16]),
               M1v[:, :, 1, :].unsqueeze(2).to_broadcast([128, 8, 16, 16]), ALU.add, [t_M1], [t_big])
            for h in range(8):
                op(S.emit, "dve", (lambda e, h=h: e.max(CS[:, h, 0:8], CAND[:, h, :])), [t_big], [t_CS])
                op(S.emit, "dve", (lambda e, h=h: e.match_replace(SC2[:], CS[:, h, 0:8], CAND[:, h, :], -1e30)),
                   [t_big, t_CS], [t_SC2])
                op(S.emit, "dve", (lambda e, h=h: e.max(CS[:, h, 8:16], SC2[:])), [t_SC2], [t_CS])
                op(S.emit, "dve", (lambda e, h=h: e.max_index(CI[:, h, 0:8], CS[:, h, 0:8], CAND[:, h, :])),
                   [t_big, t_CS], [t_CI])
                op(S.emit, "dve", (lambda e, h=h: e.max_index(CI[:, h, 8:16], CS[:, h, 8:16], CAND[:, h, :])),
                   [t_big, t_CS], [t_CI])
            op(S.emit, "dve", (lambda e: e.tensor_single_scalar(II[:], CI[:], 4, ALU.logical_shift_right)), [t_CI], [t_II])
            op(S.cp, "dve", IIF[:, 0], II[:], [t_II], [t_II])
            op(S.emit, "dve", (lambda e: e.tensor_single_scalar(II[:], CI[:], 15, ALU.bitwise_and)), [t_CI, t_II], [t_II])
            op(S.cp, "dve", IIF[:, 1], II[:], [t_II], [t_II])
            IDXv = IDXF[:].rearrange("p (h two) k -> p h two k", two=2)
            for pp_ in range(2):
                op(S.tt, "dve", OH, IIF[:, pp_].unsqueeze(3).to_broadcast([128, 8, 16, 16]),
                   iota_f[:, 0:16].unsqueeze(1).unsqueeze(1).to_broadcast([128, 8, 16, 16]), ALU.is_equal,
                   [t_II, t_const, t_big], [t_big])
                op(S.tt, "dve", OH, OH, IDXv[:, :, pp_, :].unsqueeze(2).to_broadcast([128, 8, 16, 16]), ALU.mult,
                   [t_big, t_IDX], [t_big])
                op(S.emit, "dve", (lambda e, pp_=pp_, SEL=SEL: e.tensor_reduce(
                    SEL[:, pp_, :], OH.rearrange("p h k i -> p (h k) i"), AX.X, ALU.add)), [t_big], [t_SEL])
            op(S.emit, "dve", (lambda e: e.tensor_reduce(sm[:, :, 0], CS[:], AX.X, ALU.max)), [t_CS], [t_sm])
            op(S.tt, "dve", CS[:], CS[:], sm[:, :, 0:1].to_broadcast([128, 8, 16]), ALU.subtract, [t_CS, t_sm], [t_CS])
            op(S.actf, CS[:], CS[:], AF.Exp, [t_CS], [t_CS])
            op(S.emit, "dve", (lambda e: e.tensor_reduce(sm[:, :, 1], CS[:], AX.X, ALU.add)), [t_CS, t_sm], [t_sm])
            op(S.emit, "dve", (lambda e: e.reciprocal(sm[:, :, 2], sm[:, :, 1])), [t_sm], [t_sm])
            op(S.tt, "dve", SEL[:, 2, :].rearrange("p (h k) -> p h k", k=16), CS[:],
               sm[:, :, 2:3].to_broadcast([128, 8, 16]), ALU.mult, [t_CS, t_sm], [t_SEL])
            for j in range(3):
                op_late(S.tr, pC[1][:, j * 128:(j + 1) * 128], SEL[:, j, :], ident[:], [t_SEL, t_const], [t_pC[1]])
            op_late(S.cp, "dve", SELT[:].rearrange("p a b -> p (a b)"), pC[1][:, 0:384], [t_pC[1]], [t_selt])
        return ops, late

    def gd_build(g, pend):
        qi = 0
        pi = [0]

        def pop(n):
            for _ in range(n):
                if pi[0] < len(pend):
                    pend[pi[0]]()
                    pi[0] += 1
        for tl in range(2):
            ti = g * 2 + tl
            SELT, t_selt = SELTr[ti % 4], t_SELT[ti % 4]
            for q in range(8):
                b2 = qi % 2
                qi += 1
                tq = slice(q * 16, (q + 1) * 16)
                S.tt("dve", ABoh[b2][:], iota_f[:].unsqueeze(1).unsqueeze(1).to_broadcast([128, 2, 16, 128]),
                     SELT[:, 0:2, tq].unsqueeze(3).to_broadcast([128, 2, 16, 128]), ALU.is_equal,
                     [t_const, t_selt], [t_A[b2]])
                S.tt("pool", Boh[b2][:], ABoh[b2][:, 1],
                     SELT[:, 2, tq].unsqueeze(2).to_broadcast([128, 16, 128]), ALU.mult,
                     [t_A[b2], t_selt], [t_B[b2]])
                for tt_ in range(16):
                    S.mm(pA[:, tt_ * 128:(tt_ + 1) * 128], Boh[b2][:, tt_, :], ABoh[b2][:, 0, tt_, :], True, True,
                         [t_A[b2], t_B[b2]], [t_pA], sig=(tt_ == 15))
                c0 = tl * 128 + q * 16
                S.cp("act", Gd[:, c0:c0 + 16, :], pA[:].rearrange("p (t a) -> p t a", a=128), [t_pA], [t_Gd])
                pop(2)
        pop(len(pend))

    SKEW = 2
    hbank = [pB[0], pB[1], pC[0]]

    def dense(g, extra_ops, late_ops):
        gb = g % 2
        xtb, t_xtb = x1Tr[gb], t_x1Tr[gb]
        per = -(-len(extra_ops) // 112) if extra_ops else 0
        pos = 0

        def stage_h(a):
            r4 = a % 4
            r3 = a % 3
            r5 = a % NUV
            S.dma("sp", UV[r5][:], uv_v[:, a, :], (), [t_UV[r5]])
            hp = hbank[r3][:, 0:TG]
            for k in range(8):
                S.mm(hp, UV[r5][:, k * 128:(k + 1) * 128], xtb[:, k, :], k == 0, k == 7,
                     [t_UV[r5], t_xtb], [t_pBh[r3]])
            S.actf(HG[r4][:], hp, AF.Gelu, [t_pBh[r3]], [t_HG[r4]])
            S.tt("pool", WW[r4][:], HG[r4][:], Gd[:, :, a], ALU.mult, [t_HG[r4], t_Gd], [t_WW[r4]])

        def stage_o(a):
            r4 = a % 4
            r5 = a % NUV
            for tl in range(2):
                for hf in range(2):
                    S.mm(pA[:, (tl * 2 + hf) * 512:(tl * 2 + hf + 1) * 512], WW[r4][:, tl * 128:(tl + 1) * 128],
                         UV[r5][:, 1024 + hf * 512:1024 + (hf + 1) * 512], a == 0, a == 127, [t_WW[r4], t_UV[r5]], [t_pA],
                         sig=(tl == 1 and hf == 1))

        for a in range(128 + SKEW):
            if a < 128:
                stage_h(a)
            if a >= SKEW:
                stage_o(a - SKEW)
            for _ in range(per):
                if pos < len(extra_ops):
                    extra_ops[pos]()
                    pos += 1
        while pos < len(extra_ops):
            extra_ops[pos]()
            pos += 1
        for f in late_ops:
            f()

    def final(g):
        gb = g % 2
        th = []
        for tl in range(2):
            ti = g * 2 + tl
            S.dma("sp", ybr[tl][:], x1_s[ti * 128:(ti + 1) * 128, :], (), [t_ybr[tl]])
        for tl in range(2):
            S.stt("dve", ybr[tl][:], ybr[tl][:], ALPHA, pA[:, tl * 1024:(tl + 1) * 1024], ALU.mult, ALU.add,
                  [t_ybr[tl], t_pA], [t_ybr[tl]])
        for tl in range(2):
            ti = g * 2 + tl
            rec = Rec(S)
            layer_norm(rec, ybr[tl], t_ybr[tl], st, t_st, junk, t_junk, lng, lnb, t_ln)
            th.extend(rec.ops)
            th.append(lambda tl=tl, ti=ti: S.dma("sp", out_d[ti * 128:(ti + 1) * 128, :], ybr[tl][:], [t_ybr[tl]], ()))
        return th

    o0, l0 = build_sel(0)
    for f in o0 + l0:
        f()
    pend = []
    for g in range(NG):
        gd_build(g, pend)
        o1, l1 = build_sel(g + 1) if g + 1 < NG else ([], [])
        dense(g, o1, l1)
        pend = final(g)
    for f in pend:
        f()


def host_inputs(inputs, b):
    f = np.float32
    x = np.ascontiguousarray(inputs["x"][b], dtype=f)
    cp = np.stack([inputs["conv_w"][0][0], inputs["conv_w"][0][1], inputs["conv_w"][0][2], inputs["conv_w"][0][3],
                   inputs["conv_b"][0], inputs["lru_ba"][0], inputs["lru_bx"][0], inputs["lru_lambda"][0]], axis=-1)
    cpar = np.ascontiguousarray(cp.reshape(8, 128, 8).transpose(1, 0, 2), dtype=f)
    keys = inputs["peer_keys"][0]
    keysT = np.ascontiguousarray(keys.reshape(16, 128, 128).transpose(2, 0, 1), dtype=f)
    u = inputs["peer_u"][0]
    u_l = np.ascontiguousarray(u.reshape(128, 128, 8, 128).transpose(0, 3, 2, 1), dtype=f).reshape(16384, 1024)
    lnp = np.stack([inputs["ln1_g"][0], inputs["ln1_b"][0], inputs["ln2_g"][0], inputs["ln2_b"][0]], axis=0)
    return {
        "xT": np.ascontiguousarray(x.T),
        "x": x,
        "memT": np.ascontiguousarray(inputs["mem"][b].T, dtype=f),
        "w_in": np.ascontiguousarray(inputs["w_in"][0], dtype=f),
        "cpar": cpar,
        "lru_wa": np.ascontiguousarray(inputs["lru_wa"][0], dtype=f),
        "lru_wx": np.ascontiguousarray(inputs["lru_wx"][0], dtype=f),
        "w_mem_kv": np.ascontiguousarray(inputs["w_mem_kv"][0], dtype=f),
        "w_br_attn": np.ascontiguousarray(inputs["w_br_attn"][0], dtype=f),
        "w_br_lru": np.ascontiguousarray(inputs["w_br_lru"][0], dtype=f),
        "w_br_mem": np.ascontiguousarray(inputs["w_br_mem"][0], dtype=f),
        "w_out": np.ascontiguousarray(inputs["w_out"][0], dtype=f),
        "b_gate": np.ascontiguousarray(inputs["b_gate"][0].reshape(1, 3072), dtype=f),
        "lnp": np.ascontiguousarray(lnp, dtype=f),
        "peer_wqT": np.ascontiguousarray(inputs["peer_wq"][0].T, dtype=f),
        "keysT": keysT,
        "u_l": u_l,
        "peer_v": np.ascontiguousarray(inputs["peer_v"][0], dtype=f),
    }


def kernel(**inputs):
    inputs = {k: np.asarray(v) for k, v in inputs.items()}
    nc = build_program()
    shared = host_inputs(inputs, 0)
    in_maps = []
    for b in range(8):
        m = dict(shared)
        x = np.ascontiguousarray(inputs["x"][b], dtype=np.float32)
        m["x"] = x
        m["xT"] = np.ascontiguousarray(x.T)
        m["memT"] = np.ascontiguousarray(inputs["mem"][b].T, dtype=np.float32)
        in_maps.append(m)
    res = run_bass_kernel_spmd(nc, in_maps, core_ids=list(range(8)))
    out = np.stack([np.asarray(r["out"]) for r in res.results], axis=0)
    return out.astype(np.float32)
```

```python
import math
import os
DBG_P1 = int(os.environ.get('DBG_P1', '99'))
from contextlib import ExitStack

import numpy as np
import concourse.bass as bass
import concourse.mybir as mybir
from concourse.bass_utils import run_bass_kernel_spmd

F32 = mybir.dt.float32
BF16 = mybir.dt.bfloat16
U32 = mybir.dt.uint32
AF = mybir.ActivationFunctionType
ALU = mybir.AluOpType
AX = mybir.AxisListType

S_LEN = 4096
D = 1024
NT = S_LEN // 128
ALPHA = 2.0 ** 0.25
LN_EPS = 1e-5
QOFF, KOFF, VOFF, XROFF, YGOFF, MQOFF, GLOFF = 0, 1536, 3072, 4608, 5632, 6656, 7168
DILS = (1, 4, 16)
ENGS = ("pe", "act", "dve", "pool", "sp")


class Trk:
    __slots__ = ("w", "r")

    def __init__(self):
        self.w = None
        self.r = {}


def trks(n):
    return [Trk() for _ in range(n)]


class Sched:
    def __init__(self, nc, es):
        self.nc = nc
        self.semh = {}
        for e in ("pe", "act", "dve", "pool"):
            self.semh[e] = es.enter_context(nc.semaphore("s_" + e))
        self.dq = {"sp": [], "pool": [], "act": []}
        nd = {"sp": 10, "pool": 8, "act": 2}
        for q, n in nd.items():
            for i in range(n):
                k = "d_%s%d" % (q, i)
                self.semh[k] = es.enter_context(nc.semaphore(k))
                self.dq[q].append(k)
        self.cnt = {k: 0 for k in self.semh}
        self.rr = {q: 0 for q in self.dq}
        self.seen = {e: {} for e in ENGS}
        self.prog = {e: [] for e in ENGS}
        self.pending = {e: [] for e in ENGS}

    def _waits(self, eng, reads, writes, extra=(), strict=False):
        deps = {}

        def need(tok):
            if tok is None:
                return
            k, v = tok
            if deps.get(k, 0) < v:
                deps[k] = v
        for b in reads:
            need(b.w)
        for b in writes:
            if b.w is not None and (strict or b.w[0] != eng):
                need(b.w)
            for k, v in b.r.items():
                if strict or k != eng:
                    need((k, v))
        for tok in extra:
            need(tok)
        out = []
        sn = self.seen[eng]
        for k, v in deps.items():
            if sn.get(k, 0) < v:
                sn[k] = v
                out.append((k, v))
        return out

    def emit(self, eng, fn, reads=(), writes=(), sig=True):
        waits = self._waits(eng, reads, writes, strict=(eng != 'pe'))
        ticket = self.cnt[eng] + 1
        if sig:
            self.cnt[eng] = ticket
        self.prog[eng].append((waits, fn, (eng, 1) if sig else None))
        for b in reads:
            if b.r.get(eng, 0) < ticket:
                b.r[eng] = ticket
        for b in writes:
            b.w = (eng, ticket)
            b.r = {}

    def dma(self, q, out, in_, reads=(), writes=()):
        ks = self.dq[q]
        k = ks[self.rr[q] % len(ks)]
        self.rr[q] += 1
        extra = [(k, self.cnt[k])] if self.cnt[k] > 0 else []
        waits = self._waits(q, reads, writes, extra, strict=True)
        self.cnt[k] += 16
        v = self.cnt[k]
        self.prog[q].append((waits, lambda e: e.dma_start(out=out, in_=in_), (k, 16)))
        for b in reads:
            if b.r.get(k, 0) < v:
                b.r[k] = v
        for b in writes:
            b.w = (k, v)
            b.r = {}

    def barrier(self):
        for e in ENGS:
            waits = []
            for k, v in self.cnt.items():
                if v > 0 and k != e and self.seen[e].get(k, 0) < v:
                    self.seen[e][k] = v
                    waits.append((k, v))
            if waits:
                self.prog[e].append((waits, None, None))

    def flush(self):
        nc = self.nc
        semh = self.semh
        prog = self.prog

        def replay(name, e):
            for waits, fn, inc in prog[name]:
                for k, v in waits:
                    e.wait_ge(semh[k], v)
                if fn is None:
                    continue
                ins = fn(e)
                if inc is not None:
                    ins.then_inc(semh[inc[0]], inc[1])

        with nc.Block() as block:
            @block.tensor
            def _(e):
                replay("pe", e)

            @block.scalar
            def _(e):
                replay("act", e)

            @block.vector
            def _(e):
                replay("dve", e)

            @block.gpsimd
            def _(e):
                replay("pool", e)

            @block.sync
            def _(e):
                replay("sp", e)
        self.prog = {e: [] for e in ENGS}

    def mm(self, out, lhsT, rhs, start, stop, reads=(), writes=(), sig=None):
        if sig is None:
            sig = stop
        self.emit("pe", lambda e: e.matmul(out, lhsT, rhs, start=start, stop=stop), reads, writes, sig)

    def tr(self, out, in_, ident, reads=(), writes=()):
        self.emit("pe", lambda e: e.transpose(out, in_, ident), reads, writes)

    def actf(self, out, in_, func, reads=(), writes=(), bias=0.0, scale=1.0, accum=None, eng="act"):
        if accum is None:
            self.emit("act", lambda e: e.activation(out, in_, func, bias=bias, scale=scale), reads, writes)
        else:
            self.emit("act", lambda e: e.activation(out, in_, func, bias=bias, scale=scale, accum_out=accum),
                      reads, writes)

    def tt(self, eng, out, in0, in1, op, reads=(), writes=()):
        self.emit(eng, lambda e: e.tensor_tensor(out, in0, in1, op), reads, writes)

    def ts(self, eng, out, in0, s1, s2, op0, op1=None, reads=(), writes=()):
        if op1 is None:
            self.emit(eng, lambda e: e.tensor_scalar(out, in0, s1, None, op0), reads, writes)
        else:
            self.emit(eng, lambda e: e.tensor_scalar(out, in0, s1, s2, op0, op1), reads, writes)

    def stt(self, eng, out, in0, scalar, in1, op0, op1, reads=(), writes=()):
        self.emit(eng, lambda e: e.scalar_tensor_tensor(out, in0, scalar, in1, op0, op1), reads, writes)

    def cp(self, eng, out, in_, reads=(), writes=()):
        if eng == "act":
            self.emit("act", lambda e: e.copy(out, in_), reads, writes)
        else:
            self.emit(eng, lambda e: e.tensor_copy(out, in_), reads, writes)

    def memset(self, eng, ap, val, writes=()):
        self.emit(eng, lambda e: e.memset(ap, val), (), writes)


class Rec:
    def __init__(self, S):
        self.S = S
        self.ops = []

    def __getattr__(self, name):
        f = getattr(self.S, name)

        def wrap(*a, **k):
            self.ops.append(lambda: f(*a, **k))
        return wrap


class Ring:
    def __init__(self, items):
        self.items = items
        self.i = 0

    def next(self):
        it = self.items[self.i % len(self.items)]
        self.i += 1
        return it


def build_program(stop_after=99, debug=False):
    nc = bass.Bass("TRN2", target_bir_lowering=False)
    es = ExitStack()

    def din(name, shape, dt=F32):
        return nc.dram_tensor(name, list(shape), dt, kind="ExternalInput").ap()

    skind = "ExternalOutput" if debug else "Internal"
    xT_d = din("xT", [D, S_LEN])
    x_d = din("x", [S_LEN, D])
    memT_d = din("memT", [D, 256])
    w_in_d = din("w_in", [D, 10240])
    cpar_d = din("cpar", [128, 8, 8])
    wa_d = din("lru_wa", [16, 64, 64])
    wx_d = din("lru_wx", [16, 64, 64])
    wkv_d = din("w_mem_kv", [D, 1024])
    wbra_d = din("w_br_attn", [512, D])
    wbrl_d = din("w_br_lru", [1024, D])
    wbrm_d = din("w_br_mem", [512, D])
    wout_d = din("w_out", [D, D])
    bgate_d = din("b_gate", [1, 3072])
    lnp_d = din("lnp", [4, D])
    wqT_d = din("peer_wqT", [2048, D])
    keysT_d = din("keysT", [128, 16, 128])
    ul_d = din("u_l", [16384, 1024])
    v_d = din("peer_v", [16384, 1024])
    out_d = nc.dram_tensor("out", [S_LEN, D], F32, kind="ExternalOutput").ap()
    br_s = nc.dram_tensor("br_s", [2048, S_LEN], BF16, kind=skind).ap()
    x1_s = nc.dram_tensor("x1_s", [S_LEN, D], F32, kind=skind).ap()
    uv_s = nc.dram_tensor("uv_s", [16384, 2048], BF16, kind="Internal").ap()

    S = Sched(nc, es)

    def sb(st, name, shape, dt):
        return st.enter_context(nc.sbuf_tensor("sb_" + name, list(shape), dt))

    def psb(st, name, shape=(128, 512), dt=F32):
        return st.enter_context(nc.psum_tensor("ps_" + name, list(shape), dt))

    ident = sb(es, "ident", [128, 128], F32)
    ones_bf = sb(es, "ones_bf", [128, 128], BF16)
    iota_i = sb(es, "iota_i", [128, 128], mybir.dt.int32)
    iota_f = sb(es, "iota_f", [128, 128], F32)
    iota_p = sb(es, "iota_p", [128, 1], F32)
    t_const = Trk()
    S.emit("pool", lambda e: e.iota(iota_i[:], pattern=[[1, 128]], base=0, channel_multiplier=0), (), [t_const])
    S.cp("dve", iota_f[:], iota_i[:], [t_const], [t_const])
    S.emit("pool", lambda e: e.iota(iota_i[:, 0:1], pattern=[[1, 1]], base=0, channel_multiplier=1), [t_const], [t_const])
    S.cp("dve", iota_p[:], iota_i[:, 0:1], [t_const], [t_const])
    S.ts("dve", ident[:], iota_f[:], iota_p[:, 0:1], None, ALU.is_equal, reads=[t_const], writes=[t_const])
    S.memset("dve", ones_bf[:], 1.0, [t_const])
    iota_b = sb(es, "iota_b", [128, 128], BF16)
    S.cp("dve", iota_b[:], iota_f[:], [t_const], [t_const])

    t_us, t_vs = Trk(), Trk()
    NCH = 32
    rows = 16384 // NCH

    cast_i = [0]

    def next_cast(n=1):
        for _ in range(n):
            i = cast_i[0]
            if stop_after < 4 or i >= 2 * NCH:
                return
            cast_i[0] += 1
            j = i // 2
            if i % 2 == 0:
                S.dma("pool", uv_s[j * rows:(j + 1) * rows, 0:1024], ul_d[j * rows:(j + 1) * rows, :], (), ())
            else:
                S.dma("pool", uv_s[j * rows:(j + 1) * rows, 1024:2048], v_d[j * rows:(j + 1) * rows, :], (), ())

    with ExitStack() as p12:
        xT = sb(p12, "xT", [128, 8, S_LEN], BF16)
        t_xT = trks(8)
        xT_v = xT_d.rearrange("(k p) t -> p k t", p=128)
        for k in range(8):
            S.dma("pool", xT[:, k, :], xT_v[:, k, :], (), [t_xT[k]])
        w_in_v = w_in_d.rearrange("(k p) n -> p k n", p=128)

        with ExitStack() as p1:
            cpar = sb(p1, "cpar", [128, 8, 8], F32)
            cneg = sb(p1, "cneg", [128, 8], F32)
            cneg2 = sb(p1, "cneg2", [128, 8], F32)
            wa_bd = sb(p1, "wa_bd", [128, 8, 128], BF16)
            wx_bd = sb(p1, "wx_bd", [128, 8, 128], BF16)
            t_cp, t_wbd = Trk(), Trk()
            S.dma("sp", cpar[:], cpar_d, (), [t_cp])
            S.actf(cneg[:], cpar[:, :, 7], AF.Exp, [t_cp], [t_cp], scale=-1.0)
            S.actf(cneg[:], cneg[:], AF.Ln, [t_cp], [t_cp], bias=1.0)
            S.ts("dve", cneg2[:], cneg[:], -16.0, None, ALU.mult, reads=[t_cp], writes=[t_cp])
            S.ts("dve", cneg[:], cneg[:], -8.0, None, ALU.mult, reads=[t_cp], writes=[t_cp])
            S.memset("dve", wa_bd[:], 0.0, [t_wbd])
            S.memset("dve", wx_bd[:], 0.0, [t_wbd])
            for (wd, wsb) in ((wa_d, wa_bd), (wx_d, wx_bd)):
                wv = wd.rearrange("(c two) i j -> two i c j", two=2)
                S.dma("pool", wsb[0:64, :, 0:64], wv[0], (), [t_wbd])
                S.dma("pool", wsb[64:128, :, 64:128], wv[1], (), [t_wbd])

            HL = 1024
            NBC = S_LEN // HL
            NSET = 4
            XR = [sb(p1, "XR%d" % i, [128, HL + 4], F32) for i in range(NSET)]
            XC = [sb(p1, "XC%d" % i, [128, HL], F32) for i in range(NSET)]
            XCb = [sb(p1, "XCb%d" % i, [128, HL], BF16) for i in range(NSET)]
            RA = [sb(p1, "RA%d" % i, [128, HL], F32) for i in range(NSET)]
            IB = [sb(p1, "IB%d" % i, [128, HL], F32) for i in range(NSET)]
            MU = [sb(p1, "MU%d" % i, [128, HL], F32) for i in range(NSET)]
            YG = [sb(p1, "YG%d" % i, [128, HL], F32) for i in range(NSET)]
            REC = [sb(p1, "REC%d" % i, [128, HL], BF16) for i in range(NSET)]
            wxy = [sb(p1, "wxy%d" % i, [128, 8, 256], BF16) for i in range(2)]
            t_wxy = trks(2)
            t_XR, t_XC, t_XCb, t_RA, t_IB, t_MU, t_YG, t_REC = [trks(NSET) for _ in range(8)]
            pp = [psb(p1, "p1ps%d" % i) for i in range(8)]
            t_pp = trks(8)
            ppr = Ring(list(zip(pp, t_pp)))

            def block_ops(c, hh, it):
                r1, r2, r3 = Rec(S), Rec(S), Rec(S)
                wt, t_w = wxy[c % 2], t_wxy[c % 2]
                if hh == 0:
                    for cn in ([0, 1] if c == 0 else [c + 1]):
                        if cn < 8:
                            r1.dma("pool", wxy[cn % 2][:, :, 0:128],
                                   w_in_v[:, :, XROFF + cn * 128:XROFF + (cn + 1) * 128], (), [t_wxy[cn % 2]])
                            r1.dma("pool", wxy[cn % 2][:, :, 128:256],
                                   w_in_v[:, :, YGOFF + cn * 128:YGOFF + (cn + 1) * 128], (), [t_wxy[cn % 2]])
                pb = it % NSET
                pv = (it - 1) % NSET
                xr, xc, xcb, ra, ib, mu, yg, rec = XR[pb], XC[pb], XCb[pb], RA[pb], IB[pb], MU[pb], YG[pb], REC[pb]
                for which, (dst, t_dst, off) in enumerate(((xr, t_XR[pb], 4), (yg, t_YG[pb], 0))):
                    for tb in range(HL // 512):
                        ps, t_ps = ppr.next()
                        t0 = hh * HL + tb * 512
                        for k in range(8):
                            r1.mm(ps[:], wt[:, k, which * 128:(which + 1) * 128], xT[:, k, t0:t0 + 512],
                                  k == 0, k == 7, [t_w, t_xT[k]], [t_ps])
                        eng = "act" if tb % 2 == 0 else "dve"
                        r1.cp(eng, dst[:, off + tb * 512: off + (tb + 1) * 512], ps[:], [t_ps], [t_dst])
                if hh == 0:
                    r2.memset("dve", xr[:, 0:4], 0.0, [t_XR[pb]])
                else:
                    r2.cp("dve", xr[:, 0:4], XR[pv][:, HL:HL + 4], [t_XR[pv]], [t_XR[pb]])
                r2.actf(xc[:], xr[:, 4:4 + HL], AF.Identity, [t_XR[pb], t_cp], [t_XC[pb]],
                        bias=cpar[:, c, 4:5], scale=cpar[:, c, 3:4])
                for j in range(3):
                    r2.stt("dve", xc[:], xr[:, 1 + j:1 + j + HL], cpar[:, c, j:j + 1], xc[:],
                           ALU.mult, ALU.add, [t_XR[pb], t_cp, t_XC[pb]], [t_XC[pb]])
                r2.cp("pool", xcb[:], xc[:], [t_XC[pb]], [t_XCb[pb]])
                for (wsb, dst, t_dst, bcol) in ((wa_bd, ra, t_RA[pb], 5), (wx_bd, ib, t_IB[pb], 6)):
                    for tb in range(HL // 512):
                        ps, t_ps = ppr.next()
                        r2.mm(ps[:], wsb[:, c, :], xcb[:, tb * 512:(tb + 1) * 512], True, True,
                              [t_wbd, t_XCb[pb]], [t_ps])
                        r2.actf(dst[:, tb * 512:(tb + 1) * 512], ps[:], AF.Sigmoid, [t_ps, t_cp], [t_dst],
                                bias=cpar[:, c, bcol:bcol + 1])
                r3.actf(mu[:], ra[:], AF.Exp, [t_RA[pb], t_cp], [t_MU[pb]], scale=cneg2[:, c:c + 1])
                r3.actf(ra[:], ra[:], AF.Exp, [t_RA[pb], t_cp], [t_RA[pb]], scale=cneg[:, c:c + 1])
                r3.actf(mu[:], mu[:], AF.Sqrt, [t_MU[pb]], [t_MU[pb]], bias=1.0, scale=-1.0)
                r3.tt("dve", ib[:], ib[:], xc[:], ALU.mult, [t_IB[pb], t_XC[pb]], [t_IB[pb]])
                r3.tt("pool", ib[:], ib[:], mu[:], ALU.mult, [t_IB[pb], t_MU[pb]], [t_IB[pb]])
                if hh == 0:
                    r3.emit("dve", (lambda e, mu=mu, ra=ra, ib=ib: e.tensor_tensor_scan(
                        mu[:], ra[:], ib[:], 0.0, ALU.mult, ALU.add)),
                        [t_RA[pb], t_IB[pb], t_MU[pb]], [t_MU[pb]])
                else:
                    r3.emit("dve", (lambda e, mu=mu, ra=ra, ib=ib, pm=MU[pv]: e.tensor_tensor_scan(
                        mu[:], ra[:], ib[:], pm[:, HL - 1:HL], ALU.mult, ALU.add)),
                        [t_RA[pb], t_IB[pb], t_MU[pb], t_MU[pv]], [t_MU[pb]])
                r3.tt("pool", xc[:], yg[:], yg[:], ALU.mult, [t_YG[pb]], [t_XC[pb]])
                r3.ts("dve", xc[:], xc[:], 0.044715, 1.0, ALU.mult, ALU.add, reads=[t_XC[pb]], writes=[t_XC[pb]])
                r3.tt("pool", xc[:], xc[:], yg[:], ALU.mult, [t_XC[pb], t_YG[pb]], [t_XC[pb]])
                r3.actf(xc[:], xc[:], AF.Sigmoid, [t_XC[pb]], [t_XC[pb]], scale=1.5957691216057308)
                r3.tt("pool", xc[:], xc[:], yg[:], ALU.mult, [t_XC[pb], t_YG[pb]], [t_XC[pb]])
                r3.tt("dve", rec[:], mu[:], xc[:], ALU.mult, [t_MU[pb], t_XC[pb]], [t_REC[pb]])
                r3.dma("sp", br_s[512 + c * 128:512 + (c + 1) * 128, hh * HL:(hh + 1) * HL], rec[:], [t_REC[pb]], ())
                if it % 2 == 0:
                    r3.ops.append(lambda: next_cast(1))
                return r1.ops, r2.ops, r3.ops

            blocks = [(c, hh) for c in range(8 if stop_after >= 1 else 0) for hh in range(NBC)]
            stages = []
            for it in range(len(blocks) + 2):
                if it < len(blocks):
                    stages.append(block_ops(blocks[it][0], blocks[it][1], it))
                lists = []
                if it < len(blocks):
                    lists.append(stages[it][0])
                if 0 <= it - 1 < len(blocks):
                    lists.append(stages[it - 1][1])
                if 0 <= it - 2 < len(blocks):
                    lists.append(stages[it - 2][2])
                n = max(len(l) for l in lists)
                pos = [0] * len(lists)
                for step in range(1, n + 1):
                    for li, l in enumerate(lists):
                        want = (step * len(l)) // n
                        while pos[li] < want:
                            l[pos[li]]()
                            pos[li] += 1
            S.barrier()
            S.flush()

        with ExitStack() as p2:
            QT = sb(p2, "QT", [128, S_LEN], BF16)
            KT = sb(p2, "KT", [128, S_LEN], BF16)
            VV = sb(p2, "VV", [128, 32, 128], BF16)
            ACCN = sb(p2, "ACCN", [128, S_LEN], F32)
            ACCD = sb(p2, "ACCD", [128, S_LEN], F32)
            ATT = sb(p2, "ATT", [128, S_LEN], BF16)
            mask2 = sb(p2, "mask2", [128, 512], BF16)
            mtmp = sb(p2, "mtmp", [128, 128], F32)
            wqkv = [sb(p2, "wqkv%d" % i, [128, 8, 384], BF16) for i in range(2)]
            PT = [sb(p2, "PT%d" % i, [128, 512], BF16) for i in range(3)]
            t_QT, t_KT, t_VV, t_ACCN, t_ACCD, t_ATT, t_mask = trks(7)
            t_wqkv = trks(2)
            t_PT = trks(3)
            pp = [psb(p2, "p2ps%d" % i) for i in range(8)]
            t_pp = trks(8)
            ppr = Ring(list(zip(pp, t_pp)))
            S.ts("dve", mtmp[:], iota_f[:], iota_p[:, 0:1], None, ALU.is_ge, reads=[t_const], writes=[t_mask])
            S.cp("dve", mask2[:, 0:128], mtmp[:], [t_mask], [t_mask])
            S.cp("dve", mask2[:, 256:384], mtmp[:], [t_mask], [t_mask])
            S.ts("dve", mtmp[:], iota_f[:], iota_p[:, 0:1], None, ALU.is_le, reads=[t_const, t_mask], writes=[t_mask])
            S.cp("dve", mask2[:, 128:256], mtmp[:], [t_mask], [t_mask])
            S.cp("dve", mask2[:, 384:512], mtmp[:], [t_mask], [t_mask])
            inv_sqrt = 1.0 / math.sqrt(128.0)
            hi = 0
            for h in range(4 if stop_after >= 2 else 0):
                for g in range(3):
                    d = DILS[g]
                    L = S_LEN // d
                    nb = L // 128
                    wt, t_w = wqkv[hi % 2], t_wqkv[hi % 2]
                    for hn in ([0, 1] if hi == 0 else [hi + 1]):
                        if hn < 12:
                            coln = (hn % 3) * 512 + (hn // 3) * 128
                            for j, off in enumerate((QOFF, KOFF, VOFF)):
                                S.dma("pool", wqkv[hn % 2][:, :, j * 128:(j + 1) * 128],
                                      w_in_v[:, :, off + coln:off + coln + 128], (), [t_wqkv[hn % 2]])
                    hi += 1
                    next_cast(2)
                    col = g * 512 + h * 128
                    for j, (dst, t_dst) in enumerate(((QT, t_QT), (KT, t_KT))):
                        dv = dst[:].rearrange("p (r m) -> p m r", r=d)
                        for tb in range(8):
                            ps, t_ps = ppr.next()
                            for k in range(8):
                                S.mm(ps[:], wt[:, k, j * 128:(j + 1) * 128], xT[:, k, tb * 512:(tb + 1) * 512],
                                     k == 0, k == 7, [t_w, t_xT[k]], [t_ps])
                            m0 = tb * (512 // d)
                            o_ap = dv[:, m0:m0 + 512 // d, :]
                            i_ap = ps[:].rearrange("p (m r) -> p m r", r=d)
                            if j == 0:
                                S.actf(o_ap, i_ap, AF.Copy, [t_ps], [t_dst], scale=inv_sqrt)
                            else:
                                S.cp("dve", o_ap, i_ap, [t_ps], [t_dst])
                    xTg = [xT[:, k, :].rearrange("p (m r) -> p r m", r=d) for k in range(8)]
                    for jb4 in range(8):
                        ps, t_ps = ppr.next()
                        for q4 in range(4):
                            jb = jb4 * 4 + q4
                            r, n = jb // nb, jb % nb
                            for k in range(8):
                                S.mm(ps[:, q4 * 128:(q4 + 1) * 128], xTg[k][:, r, n * 128:(n + 1) * 128],
                                     wt[:, k, 256:384], k == 0, k == 7, [t_w, t_xT[k]], [t_ps],
                                     sig=(k == 7 and q4 == 3))
                        S.cp("act" if jb4 % 2 else "dve", VV[:, jb4 * 4:(jb4 + 1) * 4, :],
                             ps[:].rearrange("p (a b) -> p a b", b=128), [t_ps], [t_VV])
                    accn_v = ACCN[:].rearrange("p (m r) -> p r m", r=d)
                    accd_v = ACCD[:].rearrange("p (m r) -> p r m", r=d)
                    prevPT = None
                    for jp in range(16):
                        ps, t_ps = ppr.next()
                        for u in range(2):
                            jb = jp * 2 + u
                            n = jb % nb
                            ncol = 256 if n < nb - 1 else 128
                            S.mm(ps[:, u * 256:u * 256 + ncol], KT[:, jb * 128:(jb + 1) * 128],
                                 QT[:, jb * 128:jb * 128 + ncol], True, True, [t_KT, t_QT], [t_ps], sig=(u == 1))
                        pt, t_pt = PT[jp % 3], t_PT[jp % 3]
                        wv = 512 if ((jp * 2 + 1) % nb) < nb - 1 else 384
                        S.actf(pt[:, 0:wv], ps[:, 0:wv], AF.Exp, [t_ps], [t_pt])
                        S.tt("pool" if jp % 2 else "dve", pt[:, 0:wv], pt[:, 0:wv], mask2[:, 0:wv], ALU.mult,
                             [t_pt, t_mask], [t_pt])
                        pso, t_pso = ppr.next()
                        for u in range(2):
                            jb = jp * 2 + u
                            n = jb % nb
                            srcs = []
                            if n > 0:
                                if u == 0:
                                    srcs.append((jb - 1, prevPT[0][:, 384:512], prevPT[1]))
                                else:
                                    srcs.append((jb - 1, pt[:, 128:256], t_pt))
                            srcs.append((jb, pt[:, u * 256:u * 256 + 128], t_pt))
                            for half, use_ones in ((0, False), (1, True)):
                                oc = half * 256 + u * 128
                                for si, (kb, p_ap, t_p) in enumerate(srcs):
                                    lhs = ones_bf[:] if use_ones else VV[:, kb, :]
                                    S.mm(pso[:, oc:oc + 128], lhs, p_ap, si == 0, si == len(srcs) - 1,
                                         [t_p, t_VV, t_const], [t_pso], sig=(si == len(srcs) - 1 and half == 1 and u == 1))
                        prevPT = (pt, t_pt)
                        c0 = jp * 256
                        if L >= 256:
                            r0, m0 = c0 // L, c0 % L
                            on = accn_v[:, r0, m0:m0 + 256]
                            od = accd_v[:, r0, m0:m0 + 256]
                            inn, ind = pso[:, 0:256], pso[:, 256:512]
                        else:
                            raise AssertionError
                        if g == 0:
                            S.cp("act", on, inn, [t_pso], [t_ACCN])
                            S.cp("act", od, ind, [t_pso], [t_ACCD])
                        else:
                            S.tt("dve", on, on, inn, ALU.add, [t_pso, t_ACCN], [t_ACCN])
                            S.tt("dve", od, od, ind, ALU.add, [t_pso, t_ACCD], [t_ACCD])
                S.emit("dve", lambda e: e.reciprocal(ACCD[:], ACCD[:]), [t_ACCD], [t_ACCD])
                S.tt("pool", ATT[:], ACCN[:], ACCD[:], ALU.mult, [t_ACCN, t_ACCD], [t_ATT])
                S.dma("sp", br_s[h * 128:(h + 1) * 128, :], ATT[:], [t_ATT], ())

            if stop_after >= 2:
                memT = sb(p2, "memT", [128, 8, 256], BF16)
                wkv = sb(p2, "wkv", [128, 8, 1024], BF16)
                wmq = sb(p2, "wmq", [128, 8, 512], BF16)
                KmT = sb(p2, "KmT", [128, 4, 256], BF16)
                Vm = sb(p2, "Vm", [128, 2, 512], BF16)
                MQ = [sb(p2, "MQ%d" % i, [128, 512], BF16) for i in range(2)]
                PM = [sb(p2, "PM%d" % i, [128, 2, 512], BF16) for i in range(2)]
                MO = [sb(p2, "MO%d" % i, [128, 512], F32) for i in range(2)]
                MOb = [sb(p2, "MOb%d" % i, [128, 512], BF16) for i in range(2)]
                t_memT, t_wkv, t_wmq, t_KmT, t_Vm = trks(5)
                t_MQ, t_PM, t_MO, t_MOb = trks(2), trks(2), trks(2), trks(2)
                S.dma("pool", memT[:], memT_d.rearrange("(k p) m -> p k m", p=128), (), [t_memT])
                S.dma("pool", wkv[:], wkv_d.rearrange("(k p) n -> p k n", p=128), (), [t_wkv])
                S.dma("pool", wmq[:], w_in_v[:, :, MQOFF:MQOFF + 512], (), [t_wmq])
                for hh in range(4):
                    ps, t_ps = ppr.next()
                    for k in range(8):
                        S.mm(ps[:, 0:256], wkv[:, k, hh * 128:(hh + 1) * 128], memT[:, k, :], k == 0, k == 7,
                             [t_wkv, t_memT], [t_ps])
                    S.cp("dve", KmT[:, hh, :], ps[:, 0:256], [t_ps], [t_KmT])
                for mc in range(2):
                    ps, t_ps = ppr.next()
                    for k in range(8):
                        S.mm(ps[:], memT[:, k, mc * 128:(mc + 1) * 128], wkv[:, k, 512:1024], k == 0, k == 7,
                             [t_wkv, t_memT], [t_ps])
                    S.cp("dve", Vm[:, mc, :], ps[:], [t_ps], [t_Vm])
                it = 0
                for tb in range(8):
                    for hh in range(4):
                        b2 = it % 2
                        it += 1
                        ps, t_ps = ppr.next()
                        for k in range(8):
                            S.mm(ps[:], wmq[:, k, hh * 128:(hh + 1) * 128], xT[:, k, tb * 512:(tb + 1) * 512],
                                 k == 0, k == 7, [t_wmq, t_xT[k]], [t_ps])
                        S.actf(MQ[b2][:], ps[:], AF.Copy, [t_ps], [t_MQ[b2]], scale=inv_sqrt)
                        for mc in range(2):
                            ps, t_ps = ppr.next()
                            S.mm(ps[:], KmT[:, hh, mc * 128:(mc + 1) * 128], MQ[b2][:], True, True,
                                 [t_KmT, t_MQ[b2]], [t_ps])
                            S.actf(PM[b2][:, mc, :], ps[:], AF.Exp, [t_ps], [t_PM[b2]])
                        pn, t_pn = ppr.next()
                        pd, t_pd = ppr.next()
                        for mc in range(2):
                            S.mm(pn[:], Vm[:, mc, hh * 128:(hh + 1) * 128], PM[b2][:, mc, :], mc == 0, mc == 1,
                                 [t_Vm, t_PM[b2]], [t_pn])
                        for mc in range(2):
                            S.mm(pd[:], ones_bf[:], PM[b2][:, mc, :], mc == 0, mc == 1, [t_const, t_PM[b2]], [t_pd])
                        S.emit("dve", lambda e, o=MO[b2], i=pd: e.reciprocal(o[:], i[:]), [t_pd], [t_MO[b2]])
                        S.tt("dve", MOb[b2][:], MO[b2][:], pn[:], ALU.mult, [t_MO[b2], t_pn], [t_MOb[b2]])
                        S.dma("sp", br_s[1536 + hh * 128:1536 + (hh + 1) * 128, tb * 512:(tb + 1) * 512], MOb[b2][:],
                              [t_MOb[b2]], ())
            S.barrier()
            S.flush()

    t_br = Trk()

    with ExitStack() as p3:
        wg = sb(p3, "wg", [128, 8, 3072], BF16)
        wbr = sb(p3, "wbr", [128, 16, 1024], BF16)
        wout = sb(p3, "wout", [128, 8, 1024], BF16)
        bg = sb(p3, "bg", [1, 3072], F32)
        ones_f = sb(p3, "ones_f", [1, 128], F32)
        lng = sb(p3, "lng", [128, 1024], F32)
        lnb = sb(p3, "lnb", [128, 1024], F32)
        t_wg, t_wbr, t_wout, t_bg, t_ln = trks(5)
        if stop_after >= 3:
            for k in range(8):
                S.dma("pool", wg[:, k, :], w_in_v[:, k, GLOFF:GLOFF + 3072], (), [t_wg])
            S.dma("pool", wbr[:, 0:4, :], wbra_d.rearrange("(k p) n -> p k n", p=128), (), [t_wbr])
            S.dma("pool", wbr[:, 4:12, :], wbrl_d.rearrange("(k p) n -> p k n", p=128), (), [t_wbr])
            S.dma("pool", wbr[:, 12:16, :], wbrm_d.rearrange("(k p) n -> p k n", p=128), (), [t_wbr])
            S.dma("pool", wout[:], wout_d.rearrange("(k p) n -> p k n", p=128), (), [t_wout])
            S.dma("sp", bg[:], bgate_d, (), [t_bg])
            S.memset("dve", ones_f[:], 1.0, [t_bg])
            S.dma("sp", lng[:], lnp_d[0:1, :].to_broadcast([128, 1024]), (), [t_ln])
            S.dma("sp", lnb[:], lnp_d[1:2, :].to_broadcast([128, 1024]), (), [t_ln])
        xTb = [sb(p3, "xTb%d" % i, [128, 8, 512], BF16) for i in range(2)]
        brT = [sb(p3, "brT%d" % i, [128, 16, 512], BF16) for i in range(2)]
        xres = [sb(p3, "xres%d" % i, [128, 1024], F32) for i in range(2)]
        gate = sb(p3, "gate", [128, 1024], F32)
        macc = sb(p3, "macc", [128, 1024], F32)
        macc2 = sb(p3, "macc2", [128, 1024], F32)
        t_macc2 = Trk()
        mtmp3 = sb(p3, "mtmp3", [128, 1024], F32)
        mT = sb(p3, "mT", [128, 8, 128], BF16)
        yb = [sb(p3, "yb%d" % i, [128, 1024], F32) for i in range(2)]
        junk = sb(p3, "junk3", [128, 1024], F32)
        st = [sb(p3, "st%d" % i, [128, 8], F32) for i in range(2)]
        t_xTb, t_brT, t_xres, t_yb, t_st = trks(2), trks(2), trks(2), trks(2), trks(2)
        t_gate, t_macc, t_mtmp, t_mT, t_junk = trks(5)
        pg = [psb(p3, "p3g%d" % i) for i in range(2)]
        pj = [psb(p3, "p3j%d" % i) for i in range(2)]
        ptr = [psb(p3, "p3t%d" % i) for i in range(2)]
        pm = [psb(p3, "p3m%d" % i) for i in range(2)]
        t_pg, t_pj, t_ptr, t_pm = trks(2), trks(2), trks(2), trks(2)
        xT_v2 = xT_d.rearrange("(k p) t -> p k t", p=128)
        br_v = br_s.rearrange("(k p) t -> p k t", p=128)
        brch = ((0, 4), (4, 12), (12, 16))
        maccr = [macc, macc2]
        t_maccr = [t_macc, t_macc2]

        def tile_ops(ti):
            tb, tl = ti // 4, ti % 4
            b2 = tb % 2
            b1 = ti % 2
            mac, t_mac = maccr[b1], t_maccr[b1]
            RA_, RB_ = Rec(S), Rec(S)
            if tl == 0:
                for tbn in ([0, 1] if tb == 0 else [tb + 1]):
                    if tbn < 8:
                        RA_.dma("pool", xTb[tbn % 2][:], xT_v2[:, :, tbn * 512:(tbn + 1) * 512], (), [t_xTb[tbn % 2]])
                        RA_.dma("sp", brT[tbn % 2][:], br_v[:, :, tbn * 512:(tbn + 1) * 512], [t_br], [t_brT[tbn % 2]])
            tsl = slice(tl * 128, (tl + 1) * 128)
            RA_.dma("sp", xres[b1][:], x_d[ti * 128:(ti + 1) * 128, :], (), [t_xres[b1]])
            RA_.ops.append(lambda: next_cast(1))
            for br in range(3):
                for hf in range(2):
                    cs = br * 1024 + hf * 512
                    RA_.mm(pg[hf][:], ones_f[0:1, :], bg[0:1, cs:cs + 512], True, False, [t_bg], [t_pg[hf]], sig=False)
                    for k in range(8):
                        RA_.mm(pg[hf][:], xTb[b2][:, k, tsl], wg[:, k, cs:cs + 512], False, k == 7,
                               [t_xTb[b2], t_wg], [t_pg[hf]])
                    RA_.actf(gate[:, hf * 512:(hf + 1) * 512], pg[hf][:], AF.Sigmoid, [t_pg[hf]], [t_gate])
                    c0, c1 = brch[br]
                    for c in range(c0, c1):
                        RA_.mm(pj[hf][:], brT[b2][:, c, tsl], wbr[:, c, hf * 512:(hf + 1) * 512], c == c0, c == c1 - 1,
                               [t_brT[b2], t_wbr], [t_pj[hf]])
                    hs = slice(hf * 512, (hf + 1) * 512)
                    if br == 0:
                        RA_.tt("dve", mac[:, hs], gate[:, hs], pj[hf][:], ALU.mult, [t_gate, t_pj[hf]], [t_mac])
                    else:
                        RA_.tt("dve", mtmp3[:, hs], gate[:, hs], pj[hf][:], ALU.mult, [t_gate, t_pj[hf]], [t_mtmp])
                        RA_.tt("pool", mac[:, hs], mac[:, hs], mtmp3[:, hs], ALU.add, [t_mtmp, t_mac], [t_mac])
            for k in range(8):
                RB_.tr(ptr[k // 4][:, (k % 4) * 128:(k % 4 + 1) * 128], mac[:, k * 128:(k + 1) * 128], ident[:],
                       [t_mac, t_const], [t_ptr[k // 4]])
            for hf in range(2):
                RB_.cp("act", mT[:, hf * 4:(hf + 1) * 4, :], ptr[hf][:].rearrange("p (a b) -> p a b", b=128),
                       [t_ptr[hf]], [t_mT])
            for hf in range(2):
                for k in range(8):
                    RB_.mm(pm[hf][:], mT[:, k, :], wout[:, k, hf * 512:(hf + 1) * 512], k == 0, k == 7,
                           [t_mT, t_wout], [t_pm[hf]])
                RB_.stt("dve", yb[b1][:, hf * 512:(hf + 1) * 512], xres[b1][:, hf * 512:(hf + 1) * 512], ALPHA,
                        pm[hf][:], ALU.mult, ALU.add, [t_xres[b1], t_pm[hf]], [t_yb[b1]])
            layer_norm(RB_, yb[b1], t_yb[b1], st[b1], t_st[b1], junk, t_junk, lng, lnb, t_ln)
            RB_.dma("sp", x1_s[ti * 128:(ti + 1) * 128, :], yb[b1][:], [t_yb[b1]], ())
            return RA_.ops, RB_.ops

        prevB3 = []
        for ti in range(32 if stop_after >= 3 else 0):
            A3, B3 = tile_ops(ti)
            nA, nB = len(A3), len(prevB3)
            jb = 0
            for ia, fa in enumerate(A3):
                fa()
                want = ((ia + 1) * nB) // nA
                while jb < want:
                    prevB3[jb]()
                    jb += 1
            while jb < nB:
                prevB3[jb]()
                jb += 1
            prevB3 = B3
        for fb in prevB3:
            fb()
        next_cast(2 * NCH)
        S.barrier()
        S.flush()

    if stop_after >= 4:
        with ExitStack() as p4:
            peer_phase(nc, S, p4, sb, psb, ident, iota_f, iota_b, t_const, x1_s, wqT_d, keysT_d, uv_s, lnp_d, out_d)
            S.barrier()
            S.flush()
    else:
        S.barrier()
        S.flush()
    es.close()
    return nc


def layer_norm(S, y, t_y, st, t_st, junk, t_junk, lng, lnb, t_ln):
    S.emit("dve", lambda e: e.tensor_reduce(st[:, 0:1], y[:], AX.X, ALU.add), [t_y], [t_st])
    S.ts("dve", st[:, 1:2], st[:, 0:1], -1.0 / D, None, ALU.mult, reads=[t_st], writes=[t_st])
    S.actf(y[:], y[:], AF.Identity, [t_y, t_st], [t_y], bias=st[:, 1:2])
    S.tt("pool", junk[:], y[:], y[:], ALU.mult, [t_y], [t_junk])
    S.emit("dve", lambda e: e.tensor_reduce(st[:, 2:3], junk[:], AX.X, ALU.add), [t_junk, t_st], [t_st])
    S.ts("dve", st[:, 3:4], st[:, 2:3], 1.0 / D, LN_EPS, ALU.mult, ALU.add, reads=[t_st], writes=[t_st])
    S.actf(st[:, 3:4], st[:, 3:4], AF.Sqrt, [t_st], [t_st])
    S.emit("dve", lambda e: e.reciprocal(st[:, 3:4], st[:, 3:4]), [t_st], [t_st])
    S.actf(y[:], y[:], AF.Copy, [t_y, t_st], [t_y], scale=st[:, 3:4])
    S.tt("pool", y[:], y[:], lng[:], ALU.mult, [t_y, t_ln], [t_y])
    S.tt("pool", y[:], y[:], lnb[:], ALU.add, [t_y, t_ln], [t_y])


def peer_phase(nc, S, p4, sb, psb, ident, iota_f, iota_b, t_const, x1_s, wqT_d, keysT_d, uv_s, lnp_d, out_d):
    TG = 256
    NG = S_LEN // TG
    Wsc = sb(p4, "Wsc", [128, 8, 2048], BF16)
    lng = sb(p4, "lng2", [128, 1024], F32)
    lnb = sb(p4, "lnb2", [128, 1024], F32)
    t_Wsc, t_ln = trks(2)
    S.dma("sp", lng[:], lnp_d[2:3, :].to_broadcast([128, 1024]), (), [t_ln])
    S.dma("sp", lnb[:], lnp_d[3:4, :].to_broadcast([128, 1024]), (), [t_ln])
    with ExitStack() as pre:
        wqT = sb(pre, "wqT", [128, 16, 1024], F32)
        keysT = sb(pre, "keysTf", [128, 16, 128], F32)
        t_keys = Trk()
        t_wqT4 = trks(4)
        wqT_v = wqT_d.rearrange("(e c) d -> c e d", c=128)
        S.dma("sp", keysT[:], keysT_d, (), [t_keys])
        for e4 in range(4):
            S.dma("sp", wqT[:, e4 * 4:(e4 + 1) * 4, :], wqT_v[:, e4 * 4:(e4 + 1) * 4, :], (), [t_wqT4[e4]])
        pw = [psb(pre, "p4w%d" % i) for i in range(4)]
        t_pw = trks(4)
        i = 0
        for e4 in range(4):
            for k in range(8):
                ps, t_ps = pw[i % 4], t_pw[i % 4]
                i += 1
                for ee in range(4):
                    e_ = e4 * 4 + ee
                    S.mm(ps[:, ee * 128:(ee + 1) * 128], wqT[:, e_, k * 128:(k + 1) * 128], keysT[:, e_, :], True, True,
                         [t_wqT4[e4], t_keys], [t_ps], sig=(ee == 3))
                S.cp("act" if i % 2 else "dve", Wsc[:, k, e4 * 512:(e4 + 1) * 512], ps[:], [t_ps], [t_Wsc])
        S.barrier()
        S.flush()

    Gd = sb(p4, "Gd", [128, TG, 128], BF16)
    x1r = [sb(p4, "x1r%d" % i, [128, 1024], F32) for i in range(2)]
    x1Tr = [sb(p4, "x1Tr%d" % i, [128, 8, TG], BF16) for i in range(2)]
    SELTr = [sb(p4, "SELT%d" % i, [128, 3, 128], F32) for i in range(4)]
    BIG = sb(p4, "BIG", [128, 2048], F32)
    SC = BIG[:].rearrange("p (a b) -> p a b", b=128)
    CAND = BIG[:].rearrange("p (h c) -> p h c", c=256)
    OH = BIG[:].rearrange("p (h k i) -> p h k i", k=16, i=16)
    SC2 = sb(p4, "SC2", [128, 256], F32)
    M1 = sb(p4, "M1", [128, 16, 16], F32)
    IDX = sb(p4, "IDX", [128, 16, 16], U32)
    IDXF = sb(p4, "IDXF", [128, 16, 16], F32)
    CS = sb(p4, "CS", [128, 8, 16], F32)
    CI = sb(p4, "CI", [128, 8, 16], U32)
    II = sb(p4, "II", [128, 8, 16], U32)
    IIF = sb(p4, "IIF", [128, 2, 8, 16], F32)
    SELr = [sb(p4, "SEL%d" % i, [128, 3, 128], F32) for i in range(2)]
    sm = sb(p4, "sm", [128, 8, 4], F32)
    ABoh = [sb(p4, "ABoh%d" % i, [128, 2, 16, 128], BF16) for i in range(2)]
    Boh = [sb(p4, "Boh%d" % i, [128, 16, 128], BF16) for i in range(2)]
    NUV = 5
    UV = [sb(p4, "UV%d" % i, [128, 2048], BF16) for i in range(NUV)]
    HG = [sb(p4, "HG%d" % i, [128, TG], BF16) for i in range(4)]
    WW = [sb(p4, "WW%d" % i, [128, TG], BF16) for i in range(4)]
    ybr = [sb(p4, "yb4_%d" % i, [128, 1024], F32) for i in range(2)]
    junk = sb(p4, "junk4", [128, 1024], F32)
    st = sb(p4, "st4", [128, 8], F32)
    (t_Gd, t_big, t_SC2, t_M1, t_IDX, t_CS, t_CI, t_II, t_sm, t_junk, t_st) = trks(11)
    t_ybr = trks(2)
    t_SELr = trks(2)
    t_x1r, t_x1Tr, t_SELT = trks(2), trks(2), trks(4)
    t_A, t_B, t_Bt = trks(2), trks(2), trks(2)
    t_UV, t_HG, t_WW = trks(5), trks(4), trks(4)
    t_pBh = trks(4)
    pA = psb(p4, "p4A", [128, 2048])
    pB = [psb(p4, "p4B%d" % i) for i in range(2)]
    pC = [psb(p4, "p4C%d" % i) for i in range(2)]
    t_pA, = trks(1)
    t_pB, t_pC = trks(2), trks(2)
    uv_v = uv_s.rearrange("(a p) f -> p a f", p=128)

    def build_sel(g):
        ops = []
        late = []

        def op(f, *a, **k):
            ops.append(lambda: f(*a, **k))

        def op_late(f, *a, **k):
            late.append(lambda: f(*a, **k))
        gb = g % 2
        xtb, t_xtb = x1Tr[gb], t_x1Tr[gb]
        for tl in range(2):
            ti = g * 2 + tl
            xb, t_xb = x1r[tl], t_x1r[tl]
            SELT, t_selt = SELTr[ti % 4], t_SELT[ti % 4]
            SEL, t_SEL = SELr[tl], t_SELr[tl]
            op(S.dma, "sp", xb[:], x1_s[ti * 128:(ti + 1) * 128, :], (), [t_xb])
            for hf in range(2):
                for k4 in range(4):
                    k = hf * 4 + k4
                    op(S.tr, pC[1][:, k4 * 128:(k4 + 1) * 128], xb[:, k * 128:(k + 1) * 128], ident[:],
                       [t_xb, t_const], [t_pC[1]])
                op(S.cp, "dve", xtb[:, hf * 4:(hf + 1) * 4, tl * 128:(tl + 1) * 128],
                   pC[1][:].rearrange("p (a b) -> p a b", b=128), [t_pC[1]], [t_xtb])
            for pc in range(4):
                for k in range(8):
                    op(S.mm, pC[1][:], xtb[:, k, tl * 128:(tl + 1) * 128], Wsc[:, k, pc * 512:(pc + 1) * 512],
                       k == 0, k == 7, [t_xtb, t_Wsc], [t_pC[1]])
                op(S.cp, "dve", BIG[:, pc * 512:(pc + 1) * 512], pC[1][:], [t_pC[1]], [t_big])
            for e_ in range(16):
                op(S.emit, "dve", (lambda e, e_=e_: e.max(M1[:, e_, 0:8], SC[:, e_, :])), [t_big], [t_M1])
                op(S.emit, "dve", (lambda e, e_=e_: e.match_replace(SC2[:, 0:128], M1[:, e_, 0:8], SC[:, e_, :], -1e30)),
                   [t_big, t_M1], [t_SC2])
                op(S.emit, "dve", (lambda e, e_=e_: e.max(M1[:, e_, 8:16], SC2[:, 0:128])), [t_SC2], [t_M1])
                op(S.emit, "dve", (lambda e, e_=e_: e.max_index(IDX[:, e_, 0:8], M1[:, e_, 0:8], SC[:, e_, :])),
                   [t_big, t_M1], [t_IDX])
                op(S.emit, "dve", (lambda e, e_=e_: e.max_index(IDX[:, e_, 8:16], M1[:, e_, 8:16], SC[:, e_, :])),
                   [t_big, t_M1], [t_IDX])
            op(S.cp, "dve", IDXF[:], IDX[:], [t_IDX], [t_IDX])
            M1v = M1[:].rearrange("p (h two) k -> p h two k", two=2)
            op(S.tt, "dve", CAND.rearrange("p h (i j) -> p h i j", j=16),
               M1v[:, :, 0, :].unsqueeze(3).to_broadcast([128, 8, 16, 16]),
               M1v[:, :, 1, :].unsqueeze(2).to_broadcast([128, 8, 16, 16]), ALU.add, [t_M1], [t_big])
            for h in range(8):
                op(S.emit, "dve", (lambda e, h=h: e.max(CS[:, h, 0:8], CAND[:, h, :])), [t_big], [t_CS])
                op(S.emit, "dve", (lambda e, h=h: e.match_replace(SC2[:], CS[:, h, 0:8], CAND[:, h, :], -1e30)),
                   [t_big, t_CS], [t_SC2])
                op(S.emit, "dve", (lambda e, h=h: e.max(CS[:, h, 8:16], SC2[:])), [t_SC2], [t_CS])
                op(S.emit, "dve", (lambda e, h=h: e.max_index(CI[:, h, 0:8], CS[:, h, 0:8], CAND[:, h, :])),
                   [t_big, t_CS], [t_CI])
                op(S.emit, "dve", (lambda e, h=h: e.max_index(CI[:, h, 8:16], CS[:, h, 8:16], CAND[:, h, :])),
                   [t_big, t_CS], [t_CI])
            op(S.emit, "dve", (lambda e: e.tensor_single_scalar(II[:], CI[:], 4, ALU.logical_shift_right)), [t_CI], [t_II])
            op(S.cp, "dve", IIF[:, 0], II[:], [t_II], [t_II])
            op(S.emit, "dve", (lambda e: e.tensor_single_scalar(II[:], CI[:], 15, ALU.bitwise_and)), [t_CI, t_II], [t_II])
            op(S.cp, "dve", IIF[:, 1], II[:], [t_II], [t_II])
            IDXv = IDXF[:].rearrange("p (h two) k -> p h two k", two=2)
            for pp_ in range(2):
                op(S.tt, "dve", OH, IIF[:, pp_].unsqueeze(3).to_broadcast([128, 8, 16, 16]),
                   iota_f[:, 0:16].unsqueeze(1).unsqueeze(1).to_broadcast([128, 8, 16, 16]), ALU.is_equal,
                   [t_II, t_const, t_big], [t_big])
                op(S.tt, "dve", OH, OH, IDXv[:, :, pp_, :].unsqueeze(2).to_broadcast([128, 8, 16, 16]), ALU.mult,
                   [t_big, t_IDX], [t_big])
                op(S.emit, "dve", (lambda e, pp_=pp_, SEL=SEL: e.tensor_reduce(
                    SEL[:, pp_, :], OH.rearrange("p h k i -> p (h k) i"), AX.X, ALU.add)), [t_big], [t_SEL])
            op(S.emit, "dve", (lambda e: e.tensor_reduce(sm[:, :, 0], CS[:], AX.X, ALU.max)), [t_CS], [t_sm])
            op(S.tt, "dve", CS[:], CS[:], sm[:, :, 0:1].to_broadcast([128, 8, 16]), ALU.subtract, [t_CS, t_sm], [t_CS])
            op(S.actf, CS[:], CS[:], AF.Exp, [t_CS], [t_CS])
            op(S.emit, "dve", (lambda e: e.tensor_reduce(sm[:, :, 1], CS[:], AX.X, ALU.add)), [t_CS, t_sm], [t_sm])
            op(S.emit, "dve", (lambda e: e.reciprocal(sm[:, :, 2], sm[:, :, 1])), [t_sm], [t_sm])
            op(S.tt, "dve", SEL[:, 2, :].rearrange("p (h k) -> p h k", k=16), CS[:],
               sm[:, :, 2:3].to_broadcast([128, 8, 16]), ALU.mult, [t_CS, t_sm], [t_SEL])
            for j in range(3):
                op_late(S.tr, pC[1][:, j * 128:(j + 1) * 128], SEL[:, j, :], ident[:], [t_SEL, t_const], [t_pC[1]])
            op_late(S.cp, "dve", SELT[:].rearrange("p a b -> p (a b)"), pC[1][:, 0:384], [t_pC[1]], [t_selt])
        return ops, late

    def gd_build(g, pend):
        qi = 0
        pi = [0]

        def pop(n):
            for _ in range(n):
                if pi[0] < len(pend):
                    pend[pi[0]]()
                    pi[0] += 1
        for tl in range(2):
            ti = g * 2 + tl
            SELT, t_selt = SELTr[ti % 4], t_SELT[ti % 4]
            for q in range(8):
                b2 = qi % 2
                qi += 1
                tq = slice(q * 16, (q + 1) * 16)
                S.tt("dve", ABoh[b2][:], iota_f[:].unsqueeze(1).unsqueeze(1).to_broadcast([128, 2, 16, 128]),
                     SELT[:, 0:2, tq].unsqueeze(3).to_broadcast([128, 2, 16, 128]), ALU.is_equal,
                     [t_const, t_selt], [t_A[b2]])
                S.tt("pool" if q % 4 != 3 else "dve", Boh[b2][:], ABoh[b2][:, 1],
                     SELT[:, 2, tq].unsqueeze(2).to_broadcast([128, 16, 128]), ALU.mult,
                     [t_A[b2], t_selt], [t_B[b2]])
                for tt_ in range(16):
                    S.mm(pA[:, tt_ * 128:(tt_ + 1) * 128], Boh[b2][:, tt_, :], ABoh[b2][:, 0, tt_, :], True, True,
                         [t_A[b2], t_B[b2]], [t_pA], sig=(tt_ == 15))
                c0 = tl * 128 + q * 16
                S.cp("act", Gd[:, c0:c0 + 16, :], pA[:].rearrange("p (t a) -> p t a", a=128), [t_pA], [t_Gd])
                pop(2)
        pop(len(pend))

    SKEW = 2
    hbank = [pB[0], pB[1], pC[0]]

    def dense(g, extra_ops, late_ops):
        gb = g % 2
        xtb, t_xtb = x1Tr[gb], t_x1Tr[gb]
        per = -(-len(extra_ops) // 112) if extra_ops else 0
        pos = 0

        def stage_h(a):
            r4 = a % 4
            r3 = a % 3
            r5 = a % NUV
            S.dma("sp", UV[r5][:], uv_v[:, a, :], (), [t_UV[r5]])
            hp = hbank[r3][:, 0:TG]
            for k in range(8):
                S.mm(hp, UV[r5][:, k * 128:(k + 1) * 128], xtb[:, k, :], k == 0, k == 7,
                     [t_UV[r5], t_xtb], [t_pBh[r3]])
            S.actf(HG[r4][:], hp, AF.Gelu, [t_pBh[r3]], [t_HG[r4]])
            S.tt("pool", WW[r4][:], HG[r4][:], Gd[:, :, a], ALU.mult, [t_HG[r4], t_Gd], [t_WW[r4]])

        def stage_o(a):
            r4 = a % 4
            r5 = a % NUV
            for tl in range(2):
                for hf in range(2):
                    S.mm(pA[:, (tl * 2 + hf) * 512:(tl * 2 + hf + 1) * 512], WW[r4][:, tl * 128:(tl + 1) * 128],
                         UV[r5][:, 1024 + hf * 512:1024 + (hf + 1) * 512], a == 0, a == 127, [t_WW[r4], t_UV[r5]], [t_pA],
                         sig=(tl == 1 and hf == 1))

        for a in range(128 + SKEW):
            if a < 128:
                stage_h(a)
            if a >= SKEW:
                stage_o(a - SKEW)
            for _ in range(per):
                if pos < len(extra_ops):
                    extra_ops[pos]()
                    pos += 1
        while pos < len(extra_ops):
            extra_ops[pos]()
            pos += 1
        for f in late_ops:
            f()

    def final(g):
        gb = g % 2
        th = []
        for tl in range(2):
            ti = g * 2 + tl
            S.dma("sp", ybr[tl][:], x1_s[ti * 128:(ti + 1) * 128, :], (), [t_ybr[tl]])
        for tl in range(2):
            S.stt("dve", ybr[tl][:], ybr[tl][:], ALPHA, pA[:, tl * 1024:(tl + 1) * 1024], ALU.mult, ALU.add,
                  [t_ybr[tl], t_pA], [t_ybr[tl]])
        for tl in range(2):
            ti = g * 2 + tl
            rec = Rec(S)
            layer_norm(rec, ybr[tl], t_ybr[tl], st, t_st, junk, t_junk, lng, lnb, t_ln)
            th.extend(rec.ops)
            th.append(lambda tl=tl, ti=ti: S.dma("sp", out_d[ti * 128:(ti + 1) * 128, :], ybr[tl][:], [t_ybr[tl]], ()))
        return th

    o0, l0 = build_sel(0)
    for f in o0 + l0:
        f()
    pend = []
    for g in range(NG):
        gd_build(g, pend)
        o1, l1 = build_sel(g + 1) if g + 1 < NG else ([], [])
        dense(g, o1, l1)
        pend = final(g)
    for f in pend:
        f()


def host_inputs(inputs, b):
    f = np.float32
    x = np.ascontiguousarray(inputs["x"][b], dtype=f)
    cp = np.stack([inputs["conv_w"][0][0], inputs["conv_w"][0][1], inputs["conv_w"][0][2], inputs["conv_w"][0][3],
                   inputs["conv_b"][0], inputs["lru_ba"][0], inputs["lru_bx"][0], inputs["lru_lambda"][0]], axis=-1)
    cpar = np.ascontiguousarray(cp.reshape(8, 128, 8).transpose(1, 0, 2), dtype=f)
    keys = inputs["peer_keys"][0]
    keysT = np.ascontiguousarray(keys.reshape(16, 128, 128).transpose(2, 0, 1), dtype=f)
    u = inputs["peer_u"][0]
    u_l = np.ascontiguousarray(u.reshape(128, 128, 8, 128).transpose(0, 3, 2, 1), dtype=f).reshape(16384, 1024)
    lnp = np.stack([inputs["ln1_g"][0], inputs["ln1_b"][0], inputs["ln2_g"][0], inputs["ln2_b"][0]], axis=0)
    return {
        "xT": np.ascontiguousarray(x.T),
        "x": x,
        "memT": np.ascontiguousarray(inputs["mem"][b].T, dtype=f),
        "w_in": np.ascontiguousarray(inputs["w_in"][0], dtype=f),
        "cpar": cpar,
        "lru_wa": np.ascontiguousarray(inputs["lru_wa"][0], dtype=f),
        "lru_wx": np.ascontiguousarray(inputs["lru_wx"][0], dtype=f),
        "w_mem_kv": np.ascontiguousarray(inputs["w_mem_kv"][0], dtype=f),
        "w_br_attn": np.ascontiguousarray(inputs["w_br_attn"][0], dtype=f),
        "w_br_lru": np.ascontiguousarray(inputs["w_br_lru"][0], dtype=f),
        "w_br_mem": np.ascontiguousarray(inputs["w_br_mem"][0], dtype=f),
        "w_out": np.ascontiguousarray(inputs["w_out"][0], dtype=f),
        "b_gate": np.ascontiguousarray(inputs["b_gate"][0].reshape(1, 3072), dtype=f),
        "lnp": np.ascontiguousarray(lnp, dtype=f),
        "peer_wqT": np.ascontiguousarray(inputs["peer_wq"][0].T, dtype=f),
        "keysT": keysT,
        "u_l": u_l,
        "peer_v": np.ascontiguousarray(inputs["peer_v"][0], dtype=f),
    }


def kernel(**inputs):
    inputs = {k: np.asarray(v) for k, v in inputs.items()}
    nc = build_program()
    shared = host_inputs(inputs, 0)
    in_maps = []
    for b in range(8):
        m = dict(shared)
        x = np.ascontiguousarray(inputs["x"][b], dtype=np.float32)
        m["x"] = x
        m["xT"] = np.ascontiguousarray(x.T)
        m["memT"] = np.ascontiguousarray(inputs["mem"][b].T, dtype=np.float32)
        in_maps.append(m)
    res = run_bass_kernel_spmd(nc, in_maps, core_ids=list(range(8)))
    out = np.stack([np.asarray(r["out"]) for r in res.results], axis=0)
    return out.astype(np.float32)
```

```python
import math
import os
DBG_P1 = int(os.environ.get('DBG_P1', '99'))
from contextlib import ExitStack

import numpy as np
import concourse.bass as bass
import concourse.mybir as mybir
from concourse.bass_utils import run_bass_kernel_spmd

F32 = mybir.dt.float32
BF16 = mybir.dt.bfloat16
U32 = mybir.dt.uint32
AF = mybir.ActivationFunctionType
ALU = mybir.AluOpType
AX = mybir.AxisListType

S_LEN = 4096
D = 1024
NT = S_LEN // 128
ALPHA = 2.0 ** 0.25
LN_EPS = 1e-5
QOFF, KOFF, VOFF, XROFF, YGOFF, MQOFF, GLOFF = 0, 1536, 3072, 4608, 5632, 6656, 7168
DILS = (1, 4, 16)
ENGS = ("pe", "act", "dve", "pool", "sp")


class Trk:
    __slots__ = ("w", "r")

    def __init__(self):
        self.w = None
        self.r = {}


def trks(n):
    return [Trk() for _ in range(n)]


class Sched:
    def __init__(self, nc, es):
        self.nc = nc
        self.semh = {}
        for e in ("pe", "act", "dve", "pool"):
            self.semh[e] = es.enter_context(nc.semaphore("s_" + e))
        self.dq = {"sp": [], "pool": [], "act": []}
        nd = {"sp": 10, "pool": 8, "act": 2}
        for q, n in nd.items():
            for i in range(n):
                k = "d_%s%d" % (q, i)
                self.semh[k] = es.enter_context(nc.semaphore(k))
                self.dq[q].append(k)
        self.cnt = {k: 0 for k in self.semh}
        self.rr = {q: 0 for q in self.dq}
        self.seen = {e: {} for e in ENGS}
        self.prog = {e: [] for e in ENGS}
        self.pending = {e: [] for e in ENGS}

    def _waits(self, eng, reads, writes, extra=(), strict=False):
        deps = {}

        def need(tok):
            if tok is None:
                return
            k, v = tok
            if deps.get(k, 0) < v:
                deps[k] = v
        for b in reads:
            need(b.w)
        for b in writes:
            if b.w is not None and (strict or b.w[0] != eng):
                need(b.w)
            for k, v in b.r.items():
                if strict or k != eng:
                    need((k, v))
        for tok in extra:
            need(tok)
        out = []
        sn = self.seen[eng]
        for k, v in deps.items():
            if sn.get(k, 0) < v:
                sn[k] = v
                out.append((k, v))
        return out

    def emit(self, eng, fn, reads=(), writes=(), sig=True):
        waits = self._waits(eng, reads, writes, strict=(eng != 'pe'))
        ticket = self.cnt[eng] + 1
        if sig:
            self.cnt[eng] = ticket
        self.prog[eng].append((waits, fn, (eng, 1) if sig else None))
        for b in reads:
            if b.r.get(eng, 0) < ticket:
                b.r[eng] = ticket
        for b in writes:
            b.w = (eng, ticket)
            b.r = {}

    def dma(self, q, out, in_, reads=(), writes=()):
        ks = self.dq[q]
        k = ks[self.rr[q] % len(ks)]
        self.rr[q] += 1
        extra = [(k, self.cnt[k])] if self.cnt[k] > 0 else []
        waits = self._waits(q, reads, writes, extra, strict=True)
        self.cnt[k] += 16
        v = self.cnt[k]
        self.prog[q].append((waits, lambda e: e.dma_start(out=out, in_=in_), (k, 16)))
        for b in reads:
            if b.r.get(k, 0) < v:
                b.r[k] = v
        for b in writes:
            b.w = (k, v)
            b.r = {}

    def barrier(self):
        for e in ENGS:
            waits = []
            for k, v in self.cnt.items():
                if v > 0 and k != e and self.seen[e].get(k, 0) < v:
                    self.seen[e][k] = v
                    waits.append((k, v))
            if waits:
                self.prog[e].append((waits, None, None))

    def flush(self):
        nc = self.nc
        semh = self.semh
        prog = self.prog

        def replay(name, e):
            for waits, fn, inc in prog[name]:
                for k, v in waits:
                    e.wait_ge(semh[k], v)
                if fn is None:
                    continue
                ins = fn(e)
                if inc is not None:
                    ins.then_inc(semh[inc[0]], inc[1])

        with nc.Block() as block:
            @block.tensor
            def _(e):
                replay("pe", e)

            @block.scalar
            def _(e):
                replay("act", e)

            @block.vector
            def _(e):
                replay("dve", e)

            @block.gpsimd
            def _(e):
                replay("pool", e)

            @block.sync
            def _(e):
                replay("sp", e)
        self.prog = {e: [] for e in ENGS}

    def mm(self, out, lhsT, rhs, start, stop, reads=(), writes=(), sig=None):
        if sig is None:
            sig = stop
        self.emit("pe", lambda e: e.matmul(out, lhsT, rhs, start=start, stop=stop), reads, writes, sig)

    def tr(self, out, in_, ident, reads=(), writes=()):
        self.emit("pe", lambda e: e.transpose(out, in_, ident), reads, writes)

    def actf(self, out, in_, func, reads=(), writes=(), bias=0.0, scale=1.0, accum=None, eng="act"):
        if accum is None:
            self.emit("act", lambda e: e.activation(out, in_, func, bias=bias, scale=scale), reads, writes)
        else:
            self.emit("act", lambda e: e.activation(out, in_, func, bias=bias, scale=scale, accum_out=accum),
                      reads, writes)

    def tt(self, eng, out, in0, in1, op, reads=(), writes=()):
        self.emit(eng, lambda e: e.tensor_tensor(out, in0, in1, op), reads, writes)

    def ts(self, eng, out, in0, s1, s2, op0, op1=None, reads=(), writes=()):
        if op1 is None:
            self.emit(eng, lambda e: e.tensor_scalar(out, in0, s1, None, op0), reads, writes)
        else:
            self.emit(eng, lambda e: e.tensor_scalar(out, in0, s1, s2, op0, op1), reads, writes)

    def stt(self, eng, out, in0, scalar, in1, op0, op1, reads=(), writes=()):
        self.emit(eng, lambda e: e.scalar_tensor_tensor(out, in0, scalar, in1, op0, op1), reads, writes)

    def cp(self, eng, out, in_, reads=(), writes=()):
        if eng == "act":
            self.emit("act", lambda e: e.copy(out, in_), reads, writes)
        else:
            self.emit(eng, lambda e: e.tensor_copy(out, in_), reads, writes)

    def memset(self, eng, ap, val, writes=()):
        self.emit(eng, lambda e: e.memset(ap, val), (), writes)


class Rec:
    def __init__(self, S):
        self.S = S
        self.ops = []

    def __getattr__(self, name):
        f = getattr(self.S, name)

        def wrap(*a, **k):
            self.ops.append(lambda: f(*a, **k))
        return wrap


class Ring:
    def __init__(self, items):
        self.items = items
        self.i = 0

    def next(self):
        it = self.items[self.i % len(self.items)]
        self.i += 1
        return it


def build_program(stop_after=99, debug=False):
    nc = bass.Bass("TRN2", target_bir_lowering=False)
    es = ExitStack()

    def din(name, shape, dt=F32):
        return nc.dram_tensor(name, list(shape), dt, kind="ExternalInput").ap()

    skind = "ExternalOutput" if debug else "Internal"
    xT_d = din("xT", [D, S_LEN])
    x_d = din("x", [S_LEN, D])
    memT_d = din("memT", [D, 256])
    w_in_d = din("w_in", [D, 10240])
    cpar_d = din("cpar", [128, 8, 8])
    wa_d = din("lru_wa", [16, 64, 64])
    wx_d = din("lru_wx", [16, 64, 64])
    wkv_d = din("w_mem_kv", [D, 1024])
    wbra_d = din("w_br_attn", [512, D])
    wbrl_d = din("w_br_lru", [1024, D])
    wbrm_d = din("w_br_mem", [512, D])
    wout_d = din("w_out", [D, D])
    bgate_d = din("b_gate", [1, 3072])
    lnp_d = din("lnp", [4, D])
    wqT_d = din("peer_wqT", [2048, D])
    keysT_d = din("keysT", [128, 16, 128])
    ul_d = din("u_l", [16384, 1024])
    v_d = din("peer_v", [16384, 1024])
    out_d = nc.dram_tensor("out", [S_LEN, D], F32, kind="ExternalOutput").ap()
    br_s = nc.dram_tensor("br_s", [2048, S_LEN], BF16, kind=skind).ap()
    x1_s = nc.dram_tensor("x1_s", [S_LEN, D], F32, kind=skind).ap()
    uv_s = nc.dram_tensor("uv_s", [16384, 2048], BF16, kind="Internal").ap()

    S = Sched(nc, es)

    def sb(st, name, shape, dt):
        return st.enter_context(nc.sbuf_tensor("sb_" + name, list(shape), dt))

    def psb(st, name, shape=(128, 512), dt=F32):
        return st.enter_context(nc.psum_tensor("ps_" + name, list(shape), dt))

    ident = sb(es, "ident", [128, 128], F32)
    ones_bf = sb(es, "ones_bf", [128, 128], BF16)
    iota_i = sb(es, "iota_i", [128, 128], mybir.dt.int32)
    iota_f = sb(es, "iota_f", [128, 128], F32)
    iota_p = sb(es, "iota_p", [128, 1], F32)
    t_const = Trk()
    S.emit("pool", lambda e: e.iota(iota_i[:], pattern=[[1, 128]], base=0, channel_multiplier=0), (), [t_const])
    S.cp("dve", iota_f[:], iota_i[:], [t_const], [t_const])
    S.emit("pool", lambda e: e.iota(iota_i[:, 0:1], pattern=[[1, 1]], base=0, channel_multiplier=1), [t_const], [t_const])
    S.cp("dve", iota_p[:], iota_i[:, 0:1], [t_const], [t_const])
    S.ts("dve", ident[:], iota_f[:], iota_p[:, 0:1], None, ALU.is_equal, reads=[t_const], writes=[t_const])
    S.memset("dve", ones_bf[:], 1.0, [t_const])
    iota_b = sb(es, "iota_b", [128, 128], BF16)
    S.cp("dve", iota_b[:], iota_f[:], [t_const], [t_const])

    t_us, t_vs = Trk(), Trk()
    NCH = 32
    rows = 16384 // NCH

    cast_i = [0]

    def next_cast(n=1):
        for _ in range(n):
            i = cast_i[0]
            if stop_after < 4 or i >= 2 * NCH:
                return
            cast_i[0] += 1
            j = i // 2
            if i % 2 == 0:
                S.dma("pool", uv_s[j * rows:(j + 1) * rows, 0:1024], ul_d[j * rows:(j + 1) * rows, :], (), ())
            else:
                S.dma("pool", uv_s[j * rows:(j + 1) * rows, 1024:2048], v_d[j * rows:(j + 1) * rows, :], (), ())

    with ExitStack() as p12:
        xT = sb(p12, "xT", [128, 8, S_LEN], BF16)
        t_xT = trks(8)
        xT_v = xT_d.rearrange("(k p) t -> p k t", p=128)
        for k in range(8):
            S.dma("pool", xT[:, k, :], xT_v[:, k, :], (), [t_xT[k]])
        w_in_v = w_in_d.rearrange("(k p) n -> p k n", p=128)

        with ExitStack() as p1:
            cpar = sb(p1, "cpar", [128, 8, 8], F32)
            cneg = sb(p1, "cneg", [128, 8], F32)
            cneg2 = sb(p1, "cneg2", [128, 8], F32)
            wa_bd = sb(p1, "wa_bd", [128, 8, 128], BF16)
            wx_bd = sb(p1, "wx_bd", [128, 8, 128], BF16)
            t_cp, t_wbd = Trk(), Trk()
            S.dma("sp", cpar[:], cpar_d, (), [t_cp])
            S.actf(cneg[:], cpar[:, :, 7], AF.Exp, [t_cp], [t_cp], scale=-1.0)
            S.actf(cneg[:], cneg[:], AF.Ln, [t_cp], [t_cp], bias=1.0)
            S.ts("dve", cneg2[:], cneg[:], -16.0, None, ALU.mult, reads=[t_cp], writes=[t_cp])
            S.ts("dve", cneg[:], cneg[:], -8.0, None, ALU.mult, reads=[t_cp], writes=[t_cp])
            S.memset("dve", wa_bd[:], 0.0, [t_wbd])
            S.memset("dve", wx_bd[:], 0.0, [t_wbd])
            for (wd, wsb) in ((wa_d, wa_bd), (wx_d, wx_bd)):
                wv = wd.rearrange("(c two) i j -> two i c j", two=2)
                S.dma("pool", wsb[0:64, :, 0:64], wv[0], (), [t_wbd])
                S.dma("pool", wsb[64:128, :, 64:128], wv[1], (), [t_wbd])

            HL = 1024
            NBC = S_LEN // HL
            NSET = 4
            XR = [sb(p1, "XR%d" % i, [128, HL + 4], F32) for i in range(NSET)]
            XC = [sb(p1, "XC%d" % i, [128, HL], F32) for i in range(NSET)]
            XCb = [sb(p1, "XCb%d" % i, [128, HL], BF16) for i in range(NSET)]
            RA = [sb(p1, "RA%d" % i, [128, HL], F32) for i in range(NSET)]
            IB = [sb(p1, "IB%d" % i, [128, HL], F32) for i in range(NSET)]
            MU = [sb(p1, "MU%d" % i, [128, HL], F32) for i in range(NSET)]
            YG = [sb(p1, "YG%d" % i, [128, HL], F32) for i in range(NSET)]
            REC = [sb(p1, "REC%d" % i, [128, HL], BF16) for i in range(NSET)]
            wxy = [sb(p1, "wxy%d" % i, [128, 8, 256], BF16) for i in range(2)]
            t_wxy = trks(2)
            t_XR, t_XC, t_XCb, t_RA, t_IB, t_MU, t_YG, t_REC = [trks(NSET) for _ in range(8)]
            pp = [psb(p1, "p1ps%d" % i) for i in range(8)]
            t_pp = trks(8)
            ppr = Ring(list(zip(pp, t_pp)))

            def block_ops(c, hh, it):
                r1, r2, r3 = Rec(S), Rec(S), Rec(S)
                wt, t_w = wxy[c % 2], t_wxy[c % 2]
                if hh == 0:
                    for cn in ([0, 1] if c == 0 else [c + 1]):
                        if cn < 8:
                            r1.dma("pool", wxy[cn % 2][:, :, 0:128],
                                   w_in_v[:, :, XROFF + cn * 128:XROFF + (cn + 1) * 128], (), [t_wxy[cn % 2]])
                            r1.dma("pool", wxy[cn % 2][:, :, 128:256],
                                   w_in_v[:, :, YGOFF + cn * 128:YGOFF + (cn + 1) * 128], (), [t_wxy[cn % 2]])
                pb = it % NSET
                pv = (it - 1) % NSET
                xr, xc, xcb, ra, ib, mu, yg, rec = XR[pb], XC[pb], XCb[pb], RA[pb], IB[pb], MU[pb], YG[pb], REC[pb]
                for which, (dst, t_dst, off) in enumerate(((xr, t_XR[pb], 4), (yg, t_YG[pb], 0))):
                    for tb in range(HL // 512):
                        ps, t_ps = ppr.next()
                        t0 = hh * HL + tb * 512
                        for k in range(8):
                            r1.mm(ps[:], wt[:, k, which * 128:(which + 1) * 128], xT[:, k, t0:t0 + 512],
                                  k == 0, k == 7, [t_w, t_xT[k]], [t_ps])
                        eng = "act" if tb % 2 == 0 else "dve"
                        r1.cp(eng, dst[:, off + tb * 512: off + (tb + 1) * 512], ps[:], [t_ps], [t_dst])
                if hh == 0:
                    r2.memset("dve", xr[:, 0:4], 0.0, [t_XR[pb]])
                else:
                    r2.cp("dve", xr[:, 0:4], XR[pv][:, HL:HL + 4], [t_XR[pv]], [t_XR[pb]])
                r2.actf(xc[:], xr[:, 4:4 + HL], AF.Identity, [t_XR[pb], t_cp], [t_XC[pb]],
                        bias=cpar[:, c, 4:5], scale=cpar[:, c, 3:4])
                for j in range(3):
                    r2.stt("dve", xc[:], xr[:, 1 + j:1 + j + HL], cpar[:, c, j:j + 1], xc[:],
                           ALU.mult, ALU.add, [t_XR[pb], t_cp, t_XC[pb]], [t_XC[pb]])
                r2.cp("pool", xcb[:], xc[:], [t_XC[pb]], [t_XCb[pb]])
                for (wsb, dst, t_dst, bcol) in ((wa_bd, ra, t_RA[pb], 5), (wx_bd, ib, t_IB[pb], 6)):
                    for tb in range(HL // 512):
                        ps, t_ps = ppr.next()
                        r2.mm(ps[:], wsb[:, c, :], xcb[:, tb * 512:(tb + 1) * 512], True, True,
                              [t_wbd, t_XCb[pb]], [t_ps])
                        r2.actf(dst[:, tb * 512:(tb + 1) * 512], ps[:], AF.Sigmoid, [t_ps, t_cp], [t_dst],
                                bias=cpar[:, c, bcol:bcol + 1])
                r3.actf(mu[:], ra[:], AF.Exp, [t_RA[pb], t_cp], [t_MU[pb]], scale=cneg2[:, c:c + 1])
                r3.actf(ra[:], ra[:], AF.Exp, [t_RA[pb], t_cp], [t_RA[pb]], scale=cneg[:, c:c + 1])
                r3.actf(mu[:], mu[:], AF.Sqrt, [t_MU[pb]], [t_MU[pb]], bias=1.0, scale=-1.0)
                r3.tt("dve", ib[:], ib[:], xc[:], ALU.mult, [t_IB[pb], t_XC[pb]], [t_IB[pb]])
                r3.tt("pool", ib[:], ib[:], mu[:], ALU.mult, [t_IB[pb], t_MU[pb]], [t_IB[pb]])
                if hh == 0:
                    r3.emit("dve", (lambda e, mu=mu, ra=ra, ib=ib: e.tensor_tensor_scan(
                        mu[:], ra[:], ib[:], 0.0, ALU.mult, ALU.add)),
                        [t_RA[pb], t_IB[pb], t_MU[pb]], [t_MU[pb]])
                else:
                    r3.emit("dve", (lambda e, mu=mu, ra=ra, ib=ib, pm=MU[pv]: e.tensor_tensor_scan(
                        mu[:], ra[:], ib[:], pm[:, HL - 1:HL], ALU.mult, ALU.add)),
                        [t_RA[pb], t_IB[pb], t_MU[pb], t_MU[pv]], [t_MU[pb]])
                r3.tt("pool", xc[:], yg[:], yg[:], ALU.mult, [t_YG[pb]], [t_XC[pb]])
                r3.ts("dve", xc[:], xc[:], 0.044715, 1.0, ALU.mult, ALU.add, reads=[t_XC[pb]], writes=[t_XC[pb]])
                r3.tt("pool", xc[:], xc[:], yg[:], ALU.mult, [t_XC[pb], t_YG[pb]], [t_XC[pb]])
                r3.actf(xc[:], xc[:], AF.Sigmoid, [t_XC[pb]], [t_XC[pb]], scale=1.5957691216057308)
                r3.tt("pool", xc[:], xc[:], yg[:], ALU.mult, [t_XC[pb], t_YG[pb]], [t_XC[pb]])
                r3.tt("dve", rec[:], mu[:], xc[:], ALU.mult, [t_MU[pb], t_XC[pb]], [t_REC[pb]])
                r3.dma("sp", br_s[512 + c * 128:512 + (c + 1) * 128, hh * HL:(hh + 1) * HL], rec[:], [t_REC[pb]], ())
                if it % 2 == 0:
                    r3.ops.append(lambda: next_cast(1))
                return r1.ops, r2.ops, r3.ops

            blocks = [(c, hh) for c in range(8 if stop_after >= 1 else 0) for hh in range(NBC)]
            stages = []
            for it in range(len(blocks) + 2):
                if it < len(blocks):
                    stages.append(block_ops(blocks[it][0], blocks[it][1], it))
                lists = []
                if it < len(blocks):
                    lists.append(stages[it][0])
                if 0 <= it - 1 < len(blocks):
                    lists.append(stages[it - 1][1])
                if 0 <= it - 2 < len(blocks):
                    lists.append(stages[it - 2][2])
                n = max(len(l) for l in lists)
                pos = [0] * len(lists)
                for step in range(1, n + 1):
                    for li, l in enumerate(lists):
                        want = (step * len(l)) // n
                        while pos[li] < want:
                            l[pos[li]]()
                            pos[li] += 1
            S.barrier()
            S.flush()

        with ExitStack() as p2:
            QT = sb(p2, "QT", [128, S_LEN], BF16)
            KT = sb(p2, "KT", [128, S_LEN], BF16)
            VV = sb(p2, "VV", [128, 32, 128], BF16)
            ACCN = sb(p2, "ACCN", [128, S_LEN], F32)
            ACCD = sb(p2, "ACCD", [128, S_LEN], F32)
            ATT = sb(p2, "ATT", [128, S_LEN], BF16)
            mask2 = sb(p2, "mask2", [128, 512], BF16)
            mtmp = sb(p2, "mtmp", [128, 128], F32)
            wqkv = [sb(p2, "wqkv%d" % i, [128, 8, 384], BF16) for i in range(2)]
            PT = [sb(p2, "PT%d" % i, [128, 512], BF16) for i in range(3)]
            t_QT, t_KT, t_VV, t_ACCN, t_ACCD, t_ATT, t_mask = trks(7)
            t_wqkv = trks(2)
            t_PT = trks(3)
            pp = [psb(p2, "p2ps%d" % i) for i in range(8)]
            t_pp = trks(8)
            ppr = Ring(list(zip(pp, t_pp)))
            S.ts("dve", mtmp[:], iota_f[:], iota_p[:, 0:1], None, ALU.is_ge, reads=[t_const], writes=[t_mask])
            S.cp("dve", mask2[:, 0:128], mtmp[:], [t_mask], [t_mask])
            S.cp("dve", mask2[:, 256:384], mtmp[:], [t_mask], [t_mask])
            S.ts("dve", mtmp[:], iota_f[:], iota_p[:, 0:1], None, ALU.is_le, reads=[t_const, t_mask], writes=[t_mask])
            S.cp("dve", mask2[:, 128:256], mtmp[:], [t_mask], [t_mask])
            S.cp("dve", mask2[:, 384:512], mtmp[:], [t_mask], [t_mask])
            inv_sqrt = 1.0 / math.sqrt(128.0)
            hi = 0
            for h in range(4 if stop_after >= 2 else 0):
                for g in range(3):
                    d = DILS[g]
                    L = S_LEN // d
                    nb = L // 128
                    wt, t_w = wqkv[hi % 2], t_wqkv[hi % 2]
                    for hn in ([0, 1] if hi == 0 else [hi + 1]):
                        if hn < 12:
                            coln = (hn % 3) * 512 + (hn // 3) * 128
                            for j, off in enumerate((QOFF, KOFF, VOFF)):
                                S.dma("pool", wqkv[hn % 2][:, :, j * 128:(j + 1) * 128],
                                      w_in_v[:, :, off + coln:off + coln + 128], (), [t_wqkv[hn % 2]])
                    hi += 1
                    next_cast(2)
                    col = g * 512 + h * 128
                    for j, (dst, t_dst) in enumerate(((QT, t_QT), (KT, t_KT))):
                        dv = dst[:].rearrange("p (r m) -> p m r", r=d)
                        for tb in range(8):
                            ps, t_ps = ppr.next()
                            for k in range(8):
                                S.mm(ps[:], wt[:, k, j * 128:(j + 1) * 128], xT[:, k, tb * 512:(tb + 1) * 512],
                                     k == 0, k == 7, [t_w, t_xT[k]], [t_ps])
                            m0 = tb * (512 // d)
                            o_ap = dv[:, m0:m0 + 512 // d, :]
                            i_ap = ps[:].rearrange("p (m r) -> p m r", r=d)
                            if j == 0:
                                S.actf(o_ap, i_ap, AF.Copy, [t_ps], [t_dst], scale=inv_sqrt)
                            else:
                                S.cp("dve", o_ap, i_ap, [t_ps], [t_dst])
                    xTg = [xT[:, k, :].rearrange("p (m r) -> p r m", r=d) for k in range(8)]
                    for jb4 in range(8):
                        ps, t_ps = ppr.next()
                        for q4 in range(4):
                            jb = jb4 * 4 + q4
                            r, n = jb // nb, jb % nb
                            for k in range(8):
                                S.mm(ps[:, q4 * 128:(q4 + 1) * 128], xTg[k][:, r, n * 128:(n + 1) * 128],
                                     wt[:, k, 256:384], k == 0, k == 7, [t_w, t_xT[k]], [t_ps],
                                     sig=(k == 7 and q4 == 3))
                        S.cp("act" if jb4 % 2 else "dve", VV[:, jb4 * 4:(jb4 + 1) * 4, :],
                             ps[:].rearrange("p (a b) -> p a b", b=128), [t_ps], [t_VV])
                    accn_v = ACCN[:].rearrange("p (m r) -> p r m", r=d)
                    accd_v = ACCD[:].rearrange("p (m r) -> p r m", r=d)
                    prevPT = None
                    for jp in range(16):
                        ps, t_ps = ppr.next()
                        for u in range(2):
                            jb = jp * 2 + u
                            n = jb % nb
                            ncol = 256 if n < nb - 1 else 128
                            S.mm(ps[:, u * 256:u * 256 + ncol], KT[:, jb * 128:(jb + 1) * 128],
                                 QT[:, jb * 128:jb * 128 + ncol], True, True, [t_KT, t_QT], [t_ps], sig=(u == 1))
                        pt, t_pt = PT[jp % 3], t_PT[jp % 3]
                        wv = 512 if ((jp * 2 + 1) % nb) < nb - 1 else 384
                        S.actf(pt[:, 0:wv], ps[:, 0:wv], AF.Exp, [t_ps], [t_pt])
                        S.tt("pool" if jp % 2 else "dve", pt[:, 0:wv], pt[:, 0:wv], mask2[:, 0:wv], ALU.mult,
                             [t_pt, t_mask], [t_pt])
                        pso, t_pso = ppr.next()
                        for u in range(2):
                            jb = jp * 2 + u
                            n = jb % nb
                            srcs = []
                            if n > 0:
                                if u == 0:
                                    srcs.append((jb - 1, prevPT[0][:, 384:512], prevPT[1]))
                                else:
                                    srcs.append((jb - 1, pt[:, 128:256], t_pt))
                            srcs.append((jb, pt[:, u * 256:u * 256 + 128], t_pt))
                            for half, use_ones in ((0, False), (1, True)):
                                oc = half * 256 + u * 128
                                for si, (kb, p_ap, t_p) in enumerate(srcs):
                                    lhs = ones_bf[:] if use_ones else VV[:, kb, :]
                                    S.mm(pso[:, oc:oc + 128], lhs, p_ap, si == 0, si == len(srcs) - 1,
                                         [t_p, t_VV, t_const], [t_pso], sig=(si == len(srcs) - 1 and half == 1 and u == 1))
                        prevPT = (pt, t_pt)
                        c0 = jp * 256
                        if L >= 256:
                            r0, m0 = c0 // L, c0 % L
                            on = accn_v[:, r0, m0:m0 + 256]
                            od = accd_v[:, r0, m0:m0 + 256]
                            inn, ind = pso[:, 0:256], pso[:, 256:512]
                        else:
                            raise AssertionError
                        if g == 0:
                            S.cp("act", on, inn, [t_pso], [t_ACCN])
                            S.cp("act", od, ind, [t_pso], [t_ACCD])
                        else:
                            S.tt("dve", on, on, inn, ALU.add, [t_pso, t_ACCN], [t_ACCN])
                            S.tt("dve", od, od, ind, ALU.add, [t_pso, t_ACCD], [t_ACCD])
                S.emit("dve", lambda e: e.reciprocal(ACCD[:], ACCD[:]), [t_ACCD], [t_ACCD])
                S.tt("pool", ATT[:], ACCN[:], ACCD[:], ALU.mult, [t_ACCN, t_ACCD], [t_ATT])
                S.dma("sp", br_s[h * 128:(h + 1) * 128, :], ATT[:], [t_ATT], ())

            if stop_after >= 2:
                memT = sb(p2, "memT", [128, 8, 256], BF16)
                wkv = sb(p2, "wkv", [128, 8, 1024], BF16)
                wmq = sb(p2, "wmq", [128, 8, 512], BF16)
                KmT = sb(p2, "KmT", [128, 4, 256], BF16)
                Vm = sb(p2, "Vm", [128, 2, 512], BF16)
                MQ = [sb(p2, "MQ%d" % i, [128, 512], BF16) for i in range(2)]
                PM = [sb(p2, "PM%d" % i, [128, 2, 512], BF16) for i in range(2)]
                MO = [sb(p2, "MO%d" % i, [128, 512], F32) for i in range(2)]
                MOb = [sb(p2, "MOb%d" % i, [128, 512], BF16) for i in range(2)]
                t_memT, t_wkv, t_wmq, t_KmT, t_Vm = trks(5)
                t_MQ, t_PM, t_MO, t_MOb = trks(2), trks(2), trks(2), trks(2)
                S.dma("pool", memT[:], memT_d.rearrange("(k p) m -> p k m", p=128), (), [t_memT])
                S.dma("pool", wkv[:], wkv_d.rearrange("(k p) n -> p k n", p=128), (), [t_wkv])
                S.dma("pool", wmq[:], w_in_v[:, :, MQOFF:MQOFF + 512], (), [t_wmq])
                for hh in range(4):
                    ps, t_ps = ppr.next()
                    for k in range(8):
                        S.mm(ps[:, 0:256], wkv[:, k, hh * 128:(hh + 1) * 128], memT[:, k, :], k == 0, k == 7,
                             [t_wkv, t_memT], [t_ps])
                    S.cp("dve", KmT[:, hh, :], ps[:, 0:256], [t_ps], [t_KmT])
                for mc in range(2):
                    ps, t_ps = ppr.next()
                    for k in range(8):
                        S.mm(ps[:], memT[:, k, mc * 128:(mc + 1) * 128], wkv[:, k, 512:1024], k == 0, k == 7,
                             [t_wkv, t_memT], [t_ps])
                    S.cp("dve", Vm[:, mc, :], ps[:], [t_ps], [t_Vm])
                it = 0
                for tb in range(8):
                    for hh in range(4):
                        b2 = it % 2
                        it += 1
                        ps, t_ps = ppr.next()
                        for k in range(8):
                            S.mm(ps[:], wmq[:, k, hh * 128:(hh + 1) * 128], xT[:, k, tb * 512:(tb + 1) * 512],
                                 k == 0, k == 7, [t_wmq, t_xT[k]], [t_ps])
                        S.actf(MQ[b2][:], ps[:], AF.Copy, [t_ps], [t_MQ[b2]], scale=inv_sqrt)
                        for mc in range(2):
                            ps, t_ps = ppr.next()
                            S.mm(ps[:], KmT[:, hh, mc * 128:(mc + 1) * 128], MQ[b2][:], True, True,
                                 [t_KmT, t_MQ[b2]], [t_ps])
                            S.actf(PM[b2][:, mc, :], ps[:], AF.Exp, [t_ps], [t_PM[b2]])
                        pn, t_pn = ppr.next()
                        pd, t_pd = ppr.next()
                        for mc in range(2):
                            S.mm(pn[:], Vm[:, mc, hh * 128:(hh + 1) * 128], PM[b2][:, mc, :], mc == 0, mc == 1,
                                 [t_Vm, t_PM[b2]], [t_pn])
                        for mc in range(2):
                            S.mm(pd[:], ones_bf[:], PM[b2][:, mc, :], mc == 0, mc == 1, [t_const, t_PM[b2]], [t_pd])
                        S.emit("dve", lambda e, o=MO[b2], i=pd: e.reciprocal(o[:], i[:]), [t_pd], [t_MO[b2]])
                        S.tt("dve", MOb[b2][:], MO[b2][:], pn[:], ALU.mult, [t_MO[b2], t_pn], [t_MOb[b2]])
                        S.dma("sp", br_s[1536 + hh * 128:1536 + (hh + 1) * 128, tb * 512:(tb + 1) * 512], MOb[b2][:],
                              [t_MOb[b2]], ())
            S.barrier()
            S.flush()

    t_br = Trk()

    with ExitStack() as p3:
        wg = sb(p3, "wg", [128, 8, 3072], BF16)
        wbr = sb(p3, "wbr", [128, 16, 1024], BF16)
        wout = sb(p3, "wout", [128, 8, 1024], BF16)
        bg = sb(p3, "bg", [1, 3072], F32)
        ones_f = sb(p3, "ones_f", [1, 128], F32)
        lng = sb(p3, "lng", [128, 1024], F32)
        lnb = sb(p3, "lnb", [128, 1024], F32)
        t_wg, t_wbr, t_wout, t_bg, t_ln = trks(5)
        if stop_after >= 3:
            for k in range(8):
                S.dma("pool", wg[:, k, :], w_in_v[:, k, GLOFF:GLOFF + 3072], (), [t_wg])
            S.dma("pool", wbr[:, 0:4, :], wbra_d.rearrange("(k p) n -> p k n", p=128), (), [t_wbr])
            S.dma("pool", wbr[:, 4:12, :], wbrl_d.rearrange("(k p) n -> p k n", p=128), (), [t_wbr])
            S.dma("pool", wbr[:, 12:16, :], wbrm_d.rearrange("(k p) n -> p k n", p=128), (), [t_wbr])
            S.dma("pool", wout[:], wout_d.rearrange("(k p) n -> p k n", p=128), (), [t_wout])
            S.dma("sp", bg[:], bgate_d, (), [t_bg])
            S.memset("dve", ones_f[:], 1.0, [t_bg])
            S.dma("sp", lng[:], lnp_d[0:1, :].to_broadcast([128, 1024]), (), [t_ln])
            S.dma("sp", lnb[:], lnp_d[1:2, :].to_broadcast([128, 1024]), (), [t_ln])
        xTb = [sb(p3, "xTb%d" % i, [128, 8, 512], BF16) for i in range(2)]
        brT = [sb(p3, "brT%d" % i, [128, 16, 512], BF16) for i in range(2)]
        xres = [sb(p3, "xres%d" % i, [128, 1024], F32) for i in range(2)]
        gate = sb(p3, "gate", [128, 1024], F32)
        macc = sb(p3, "macc", [128, 1024], F32)
        macc2 = sb(p3, "macc2", [128, 1024], F32)
        t_macc2 = Trk()
        mtmp3 = sb(p3, "mtmp3", [128, 1024], F32)
        mT = sb(p3, "mT", [128, 8, 128], BF16)
        yb = [sb(p3, "yb%d" % i, [128, 1024], F32) for i in range(2)]
        junk = sb(p3, "junk3", [128, 1024], F32)
        st = [sb(p3, "st%d" % i, [128, 8], F32) for i in range(2)]
        t_xTb, t_brT, t_xres, t_yb, t_st = trks(2), trks(2), trks(2), trks(2), trks(2)
        t_gate, t_macc, t_mtmp, t_mT, t_junk = trks(5)
        pg = [psb(p3, "p3g%d" % i) for i in range(2)]
        pj = [psb(p3, "p3j%d" % i) for i in range(2)]
        ptr = [psb(p3, "p3t%d" % i) for i in range(2)]
        pm = [psb(p3, "p3m%d" % i) for i in range(2)]
        t_pg, t_pj, t_ptr, t_pm = trks(2), trks(2), trks(2), trks(2)
        xT_v2 = xT_d.rearrange("(k p) t -> p k t", p=128)
        br_v = br_s.rearrange("(k p) t -> p k t", p=128)
        brch = ((0, 4), (4, 12), (12, 16))
        maccr = [macc, macc2]
        t_maccr = [t_macc, t_macc2]

        def tile_ops(ti):
            tb, tl = ti // 4, ti % 4
            b2 = tb % 2
            b1 = ti % 2
            mac, t_mac = maccr[b1], t_maccr[b1]
            RA_, RB_ = Rec(S), Rec(S)
            if tl == 0:
                for tbn in ([0, 1] if tb == 0 else [tb + 1]):
                    if tbn < 8:
                        RA_.dma("pool", xTb[tbn % 2][:], xT_v2[:, :, tbn * 512:(tbn + 1) * 512], (), [t_xTb[tbn % 2]])
                        RA_.dma("sp", brT[tbn % 2][:], br_v[:, :, tbn * 512:(tbn + 1) * 512], [t_br], [t_brT[tbn % 2]])
            tsl = slice(tl * 128, (tl + 1) * 128)
            RA_.dma("sp", xres[b1][:], x_d[ti * 128:(ti + 1) * 128, :], (), [t_xres[b1]])
            RA_.ops.append(lambda: next_cast(1))
            for br in range(3):
                for hf in range(2):
                    cs = br * 1024 + hf * 512
                    RA_.mm(pg[hf][:], ones_f[0:1, :], bg[0:1, cs:cs + 512], True, False, [t_bg], [t_pg[hf]], sig=False)
                    for k in range(8):
                        RA_.mm(pg[hf][:], xTb[b2][:, k, tsl], wg[:, k, cs:cs + 512], False, k == 7,
                               [t_xTb[b2], t_wg], [t_pg[hf]])
                    RA_.actf(gate[:, hf * 512:(hf + 1) * 512], pg[hf][:], AF.Sigmoid, [t_pg[hf]], [t_gate])
                    c0, c1 = brch[br]
                    for c in range(c0, c1):
                        RA_.mm(pj[hf][:], brT[b2][:, c, tsl], wbr[:, c, hf * 512:(hf + 1) * 512], c == c0, c == c1 - 1,
                               [t_brT[b2], t_wbr], [t_pj[hf]])
                    hs = slice(hf * 512, (hf + 1) * 512)
                    if br == 0:
                        RA_.tt("dve", mac[:, hs], gate[:, hs], pj[hf][:], ALU.mult, [t_gate, t_pj[hf]], [t_mac])
                    else:
                        RA_.tt("dve", mtmp3[:, hs], gate[:, hs], pj[hf][:], ALU.mult, [t_gate, t_pj[hf]], [t_mtmp])
                        RA_.tt("pool", mac[:, hs], mac[:, hs], mtmp3[:, hs], ALU.add, [t_mtmp, t_mac], [t_mac])
            for k in range(8):
                RB_.tr(ptr[k // 4][:, (k % 4) * 128:(k % 4 + 1) * 128], mac[:, k * 128:(k + 1) * 128], ident[:],
                       [t_mac, t_const], [t_ptr[k // 4]])
            for hf in range(2):
                RB_.cp("act", mT[:, hf * 4:(hf + 1) * 4, :], ptr[hf][:].rearrange("p (a b) -> p a b", b=128),
                       [t_ptr[hf]], [t_mT])
            for hf in range(2):
                for k in range(8):
                    RB_.mm(pm[hf][:], mT[:, k, :], wout[:, k, hf * 512:(hf + 1) * 512], k == 0, k == 7,
                           [t_mT, t_wout], [t_pm[hf]])
                RB_.stt("dve", yb[b1][:, hf * 512:(hf + 1) * 512], xres[b1][:, hf * 512:(hf + 1) * 512], ALPHA,
                        pm[hf][:], ALU.mult, ALU.add, [t_xres[b1], t_pm[hf]], [t_yb[b1]])
            layer_norm(RB_, yb[b1], t_yb[b1], st[b1], t_st[b1], junk, t_junk, lng, lnb, t_ln)
            RB_.dma("sp", x1_s[ti * 128:(ti + 1) * 128, :], yb[b1][:], [t_yb[b1]], ())
            return RA_.ops, RB_.ops

        prevB3 = []
        for ti in range(32 if stop_after >= 3 else 0):
            A3, B3 = tile_ops(ti)
            nA, nB = len(A3), len(prevB3)
            jb = 0
            for ia, fa in enumerate(A3):
                fa()
                want = ((ia + 1) * nB) // nA
                while jb < want:
                    prevB3[jb]()
                    jb += 1
            while jb < nB:
                prevB3[jb]()
                jb += 1
            prevB3 = B3
        for fb in prevB3:
            fb()
        next_cast(2 * NCH)
        S.barrier()
        S.flush()

    if stop_after >= 4:
        with ExitStack() as p4:
            peer_phase(nc, S, p4, sb, psb, ident, iota_f, iota_b, t_const, x1_s, wqT_d, keysT_d, uv_s, lnp_d, out_d)
            S.barrier()
            S.flush()
    else:
        S.barrier()
        S.flush()
    es.close()
    return nc


def layer_norm(S, y, t_y, st, t_st, junk, t_junk, lng, lnb, t_ln):
    S.emit("dve", lambda e: e.tensor_reduce(st[:, 0:1], y[:], AX.X, ALU.add), [t_y], [t_st])
    S.ts("dve", st[:, 1:2], st[:, 0:1], -1.0 / D, None, ALU.mult, reads=[t_st], writes=[t_st])
    S.actf(y[:], y[:], AF.Identity, [t_y, t_st], [t_y], bias=st[:, 1:2])
    S.tt("pool", junk[:], y[:], y[:], ALU.mult, [t_y], [t_junk])
    S.emit("dve", lambda e: e.tensor_reduce(st[:, 2:3], junk[:], AX.X, ALU.add), [t_junk, t_st], [t_st])
    S.ts("dve", st[:, 3:4], st[:, 2:3], 1.0 / D, LN_EPS, ALU.mult, ALU.add, reads=[t_st], writes=[t_st])
    S.actf(st[:, 3:4], st[:, 3:4], AF.Sqrt, [t_st], [t_st])
    S.emit("dve", lambda e: e.reciprocal(st[:, 3:4], st[:, 3:4]), [t_st], [t_st])
    S.actf(y[:], y[:], AF.Copy, [t_y, t_st], [t_y], scale=st[:, 3:4])
    S.tt("pool", y[:], y[:], lng[:], ALU.mult, [t_y, t_ln], [t_y])
    S.tt("pool", y[:], y[:], lnb[:], ALU.add, [t_y, t_ln], [t_y])


def peer_phase(nc, S, p4, sb, psb, ident, iota_f, iota_b, t_const, x1_s, wqT_d, keysT_d, uv_s, lnp_d, out_d):
    TG = 256
    NG = S_LEN // TG
    Wsc = sb(p4, "Wsc", [128, 8, 2048], BF16)
    lng = sb(p4, "lng2", [128, 1024], F32)
    lnb = sb(p4, "lnb2", [128, 1024], F32)
    t_Wsc, t_ln = trks(2)
    S.dma("sp", lng[:], lnp_d[2:3, :].to_broadcast([128, 1024]), (), [t_ln])
    S.dma("sp", lnb[:], lnp_d[3:4, :].to_broadcast([128, 1024]), (), [t_ln])
    with ExitStack() as pre:
        wqT = sb(pre, "wqT", [128, 16, 1024], F32)
        keysT = sb(pre, "keysTf", [128, 16, 128], F32)
        t_keys = Trk()
        t_wqT4 = trks(4)
        wqT_v = wqT_d.rearrange("(e c) d -> c e d", c=128)
        S.dma("sp", keysT[:], keysT_d, (), [t_keys])
        for e4 in range(4):
            S.dma("sp", wqT[:, e4 * 4:(e4 + 1) * 4, :], wqT_v[:, e4 * 4:(e4 + 1) * 4, :], (), [t_wqT4[e4]])
        pw = [psb(pre, "p4w%d" % i) for i in range(4)]
        t_pw = trks(4)
        i = 0
        for e4 in range(4):
            for k in range(8):
                ps, t_ps = pw[i % 4], t_pw[i % 4]
                i += 1
                for ee in range(4):
                    e_ = e4 * 4 + ee
                    S.mm(ps[:, ee * 128:(ee + 1) * 128], wqT[:, e_, k * 128:(k + 1) * 128], keysT[:, e_, :], True, True,
                         [t_wqT4[e4], t_keys], [t_ps], sig=(ee == 3))
                S.cp("act" if i % 2 else "dve", Wsc[:, k, e4 * 512:(e4 + 1) * 512], ps[:], [t_ps], [t_Wsc])
        S.barrier()
        S.flush()

    Gd = sb(p4, "Gd", [128, TG, 128], BF16)
    x1r = [sb(p4, "x1r%d" % i, [128, 1024], F32) for i in range(2)]
    x1Tr = [sb(p4, "x1Tr%d" % i, [128, 8, TG], BF16) for i in range(2)]
    SELTr = [sb(p4, "SELT%d" % i, [128, 3, 128], F32) for i in range(4)]
    BIG = sb(p4, "BIG", [128, 2048], F32)
    SC = BIG[:].rearrange("p (a b) -> p a b", b=128)
    CAND = BIG[:].rearrange("p (h c) -> p h c", c=256)
    OH = BIG[:].rearrange("p (h k i) -> p h k i", k=16, i=16)
    SC2 = sb(p4, "SC2", [128, 256], F32)
    M1 = sb(p4, "M1", [128, 16, 16], F32)
    IDX = sb(p4, "IDX", [128, 16, 16], U32)
    IDXF = sb(p4, "IDXF", [128, 16, 16], F32)
    CS = sb(p4, "CS", [128, 8, 16], F32)
    CI = sb(p4, "CI", [128, 8, 16], U32)
    II = sb(p4, "II", [128, 8, 16], U32)
    IIF = sb(p4, "IIF", [128, 2, 8, 16], F32)
    SELr = [sb(p4, "SEL%d" % i, [128, 3, 128], F32) for i in range(2)]
    sm = sb(p4, "sm", [128, 8, 4], F32)
    ABoh = [sb(p4, "ABoh%d" % i, [128, 2, 16, 128], BF16) for i in range(2)]
    Boh = [sb(p4, "Boh%d" % i, [128, 16, 128], BF16) for i in range(2)]
    NUV = 6
    UV = [sb(p4, "UV%d" % i, [128, 2048], BF16) for i in range(NUV)]
    HG = [sb(p4, "HG%d" % i, [128, TG], BF16) for i in range(4)]
    WW = [sb(p4, "WW%d" % i, [128, TG], BF16) for i in range(4)]
    ybr = [sb(p4, "yb4_%d" % i, [128, 1024], F32) for i in range(2)]
    st = sb(p4, "st4", [128, 8], F32)
    (t_Gd, t_big, t_SC2, t_M1, t_IDX, t_CS, t_CI, t_II, t_sm, t_st) = trks(10)
    junk, t_junk = BIG[:, 0:1024], t_big
    t_ybr = trks(2)
    t_SELr = trks(2)
    t_x1r, t_x1Tr, t_SELT = trks(2), trks(2), trks(4)
    t_A, t_B, t_Bt = trks(2), trks(2), trks(2)
    t_UV, t_HG, t_WW = trks(6), trks(4), trks(4)
    t_pBh = trks(4)
    pA = psb(p4, "p4A", [128, 2048])
    pB = [psb(p4, "p4B%d" % i) for i in range(2)]
    pC = [psb(p4, "p4C%d" % i) for i in range(2)]
    t_pA, = trks(1)
    t_pB, t_pC = trks(2), trks(2)
    uv_v = uv_s.rearrange("(a p) f -> p a f", p=128)

    def build_sel(g):
        ops = []
        late = []

        def op(f, *a, **k):
            ops.append(lambda: f(*a, **k))

        def op_late(f, *a, **k):
            late.append(lambda: f(*a, **k))
        gb = g % 2
        xtb, t_xtb = x1Tr[gb], t_x1Tr[gb]
        for tl in range(2):
            ti = g * 2 + tl
            xb, t_xb = x1r[tl], t_x1r[tl]
            SELT, t_selt = SELTr[ti % 4], t_SELT[ti % 4]
            SEL, t_SEL = SELr[tl], t_SELr[tl]
            op(S.dma, "sp", xb[:], x1_s[ti * 128:(ti + 1) * 128, :], (), [t_xb])
            for hf in range(2):
                for k4 in range(4):
                    k = hf * 4 + k4
                    op(S.tr, pC[1][:, k4 * 128:(k4 + 1) * 128], xb[:, k * 128:(k + 1) * 128], ident[:],
                       [t_xb, t_const], [t_pC[1]])
                op(S.cp, "dve", xtb[:, hf * 4:(hf + 1) * 4, tl * 128:(tl + 1) * 128],
                   pC[1][:].rearrange("p (a b) -> p a b", b=128), [t_pC[1]], [t_xtb])
            for pc in range(4):
                for k in range(8):
                    op(S.mm, pC[1][:], xtb[:, k, tl * 128:(tl + 1) * 128], Wsc[:, k, pc * 512:(pc + 1) * 512],
                       k == 0, k == 7, [t_xtb, t_Wsc], [t_pC[1]])
                op(S.cp, "dve", BIG[:, pc * 512:(pc + 1) * 512], pC[1][:], [t_pC[1]], [t_big])
            for e_ in range(16):
                op(S.emit, "dve", (lambda e, e_=e_: e.max(M1[:, e_, 0:8], SC[:, e_, :])), [t_big], [t_M1])
                op(S.emit, "dve", (lambda e, e_=e_: e.match_replace(SC2[:, 0:128], M1[:, e_, 0:8], SC[:, e_, :], -1e30)),
                   [t_big, t_M1], [t_SC2])
                op(S.emit, "dve", (lambda e, e_=e_: e.max(M1[:, e_, 8:16], SC2[:, 0:128])), [t_SC2], [t_M1])
                op(S.emit, "dve", (lambda e, e_=e_: e.max_index(IDX[:, e_, 0:8], M1[:, e_, 0:8], SC[:, e_, :])),
                   [t_big, t_M1], [t_IDX])
                op(S.emit, "dve", (lambda e, e_=e_: e.max_index(IDX[:, e_, 8:16], M1[:, e_, 8:16], SC[:, e_, :])),
                   [t_big, t_M1], [t_IDX])
            op(S.cp, "dve", IDXF[:], IDX[:], [t_IDX], [t_IDX])
            M1v = M1[:].rearrange("p (h two) k -> p h two k", two=2)
            op(S.tt, "dve", CAND.rearrange("p h (i j) -> p h i j", j=16),
               M1v[:, :, 0, :].unsqueeze(3).to_broadcast([128, 8, 16, 16]),
               M1v[:, :, 1, :].unsqueeze(2).to_broadcast([128, 8, 16, 16]), ALU.add, [t_M1], [t_big])
            for h in range(8):
                op(S.emit, "dve", (lambda e, h=h: e.max(CS[:, h, 0:8], CAND[:, h, :])), [t_big], [t_CS])
                op(S.emit, "dve", (lambda e, h=h: e.match_replace(SC2[:], CS[:, h, 0:8], CAND[:, h, :], -1e30)),
                   [t_big, t_CS], [t_SC2])
                op(S.emit, "dve", (lambda e, h=h: e.max(CS[:, h, 8:16], SC2[:])), [t_SC2], [t_CS])
                op(S.emit, "dve", (lambda e, h=h: e.max_index(CI[:, h, 0:8], CS[:, h, 0:8], CAND[:, h, :])),
                   [t_big, t_CS], [t_CI])
                op(S.emit, "dve", (lambda e, h=h: e.max_index(CI[:, h, 8:16], CS[:, h, 8:16], CAND[:, h, :])),
                   [t_big, t_CS], [t_CI])
            op(S.emit, "dve", (lambda e: e.tensor_single_scalar(II[:], CI[:], 4, ALU.logical_shift_right)), [t_CI], [t_II])
            op(S.cp, "dve", IIF[:, 0], II[:], [t_II], [t_II])
            op(S.emit, "dve", (lambda e: e.tensor_single_scalar(II[:], CI[:], 15, ALU.bitwise_and)), [t_CI, t_II], [t_II])
            op(S.cp, "dve", IIF[:, 1], II[:], [t_II], [t_II])
            IDXv = IDXF[:].rearrange("p (h two) k -> p h two k", two=2)
            for pp_ in range(2):
                op(S.tt, "dve", OH, IIF[:, pp_].unsqueeze(3).to_broadcast([128, 8, 16, 16]),
                   iota_f[:, 0:16].unsqueeze(1).unsqueeze(1).to_broadcast([128, 8, 16, 16]), ALU.is_equal,
                   [t_II, t_const, t_big], [t_big])
                op(S.tt, "dve", OH, OH, IDXv[:, :, pp_, :].unsqueeze(2).to_broadcast([128, 8, 16, 16]), ALU.mult,
                   [t_big, t_IDX], [t_big])
                op(S.emit, "dve", (lambda e, pp_=pp_, SEL=SEL: e.tensor_reduce(
                    SEL[:, pp_, :], OH.rearrange("p h k i -> p (h k) i"), AX.X, ALU.add)), [t_big], [t_SEL])
            op(S.emit, "dve", (lambda e: e.tensor_reduce(sm[:, :, 0], CS[:], AX.X, ALU.max)), [t_CS], [t_sm])
            op(S.tt, "dve", CS[:], CS[:], sm[:, :, 0:1].to_broadcast([128, 8, 16]), ALU.subtract, [t_CS, t_sm], [t_CS])
            op(S.actf, CS[:], CS[:], AF.Exp, [t_CS], [t_CS])
            op(S.emit, "dve", (lambda e: e.tensor_reduce(sm[:, :, 1], CS[:], AX.X, ALU.add)), [t_CS, t_sm], [t_sm])
            op(S.emit, "dve", (lambda e: e.reciprocal(sm[:, :, 2], sm[:, :, 1])), [t_sm], [t_sm])
            op(S.tt, "dve", SEL[:, 2, :].rearrange("p (h k) -> p h k", k=16), CS[:],
               sm[:, :, 2:3].to_broadcast([128, 8, 16]), ALU.mult, [t_CS, t_sm], [t_SEL])
            for j in range(3):
                op_late(S.tr, pC[1][:, j * 128:(j + 1) * 128], SEL[:, j, :], ident[:], [t_SEL, t_const], [t_pC[1]])
            op_late(S.cp, "dve", SELT[:].rearrange("p a b -> p (a b)"), pC[1][:, 0:384], [t_pC[1]], [t_selt])
        return ops, late

    def gd_build(g, pend):
        qi = 0
        pi = [0]

        def pop(n):
            for _ in range(n):
                if pi[0] < len(pend):
                    pend[pi[0]]()
                    pi[0] += 1
        for tl in range(2):
            ti = g * 2 + tl
            SELT, t_selt = SELTr[ti % 4], t_SELT[ti % 4]
            for q in range(8):
                b2 = qi % 2
                qi += 1
                tq = slice(q * 16, (q + 1) * 16)
                S.tt("dve", ABoh[b2][:], iota_f[:].unsqueeze(1).unsqueeze(1).to_broadcast([128, 2, 16, 128]),
                     SELT[:, 0:2, tq].unsqueeze(3).to_broadcast([128, 2, 16, 128]), ALU.is_equal,
                     [t_const, t_selt], [t_A[b2]])
                S.tt("pool" if q % 4 != 3 else "dve", Boh[b2][:], ABoh[b2][:, 1],
                     SELT[:, 2, tq].unsqueeze(2).to_broadcast([128, 16, 128]), ALU.mult,
                     [t_A[b2], t_selt], [t_B[b2]])
                for tt_ in range(16):
                    S.mm(pA[:, tt_ * 128:(tt_ + 1) * 128], Boh[b2][:, tt_, :], ABoh[b2][:, 0, tt_, :], True, True,
                         [t_A[b2], t_B[b2]], [t_pA], sig=(tt_ == 15))
                c0 = tl * 128 + q * 16
                S.cp("act", Gd[:, c0:c0 + 16, :], pA[:].rearrange("p (t a) -> p t a", a=128), [t_pA], [t_Gd])
                pop(2)
        pop(len(pend))

    SKEW = 2
    hbank = [pB[0], pB[1], pC[0]]

    def dense(g, extra_ops, late_ops):
        gb = g % 2
        xtb, t_xtb = x1Tr[gb], t_x1Tr[gb]
        per = -(-len(extra_ops) // 112) if extra_ops else 0
        pos = 0

        def stage_h(a):
            r4 = a % 4
            r3 = a % 3
            r5 = a % NUV
            S.dma("sp", UV[r5][:], uv_v[:, a, :], (), [t_UV[r5]])
            hp = hbank[r3][:, 0:TG]
            for k in range(8):
                S.mm(hp, UV[r5][:, k * 128:(k + 1) * 128], xtb[:, k, :], k == 0, k == 7,
                     [t_UV[r5], t_xtb], [t_pBh[r3]])
            S.actf(HG[r4][:], hp, AF.Gelu, [t_pBh[r3]], [t_HG[r4]])
            S.tt("pool", WW[r4][:], HG[r4][:], Gd[:, :, a], ALU.mult, [t_HG[r4], t_Gd], [t_WW[r4]])

        def stage_o(a):
            r4 = a % 4
            r5 = a % NUV
            for tl in range(2):
                for hf in range(2):
                    S.mm(pA[:, (tl * 2 + hf) * 512:(tl * 2 + hf + 1) * 512], WW[r4][:, tl * 128:(tl + 1) * 128],
                         UV[r5][:, 1024 + hf * 512:1024 + (hf + 1) * 512], a == 0, a == 127, [t_WW[r4], t_UV[r5]], [t_pA],
                         sig=(tl == 1 and hf == 1))

        for a in range(128 + SKEW):
            if a < 128:
                stage_h(a)
            if a >= SKEW:
                stage_o(a - SKEW)
            for _ in range(per):
                if pos < len(extra_ops):
                    extra_ops[pos]()
                    pos += 1
        while pos < len(extra_ops):
            extra_ops[pos]()
            pos += 1
        for f in late_ops:
            f()

    def final(g):
        gb = g % 2
        th = []
        for tl in range(2):
            ti = g * 2 + tl
            S.dma("sp", ybr[tl][:], x1_s[ti * 128:(ti + 1) * 128, :], (), [t_ybr[tl]])
        for tl in range(2):
            S.stt("dve", ybr[tl][:], ybr[tl][:], ALPHA, pA[:, tl * 1024:(tl + 1) * 1024], ALU.mult, ALU.add,
                  [t_ybr[tl], t_pA], [t_ybr[tl]])
        for tl in range(2):
            ti = g * 2 + tl
            rec = Rec(S)
            layer_norm(rec, ybr[tl], t_ybr[tl], st, t_st, junk, t_junk, lng, lnb, t_ln)
            th.extend(rec.ops)
            th.append(lambda tl=tl, ti=ti: S.dma("sp", out_d[ti * 128:(ti + 1) * 128, :], ybr[tl][:], [t_ybr[tl]], ()))
        return th

    o0, l0 = build_sel(0)
    for f in o0 + l0:
        f()
    pend = []
    for g in range(NG):
        gd_build(g, pend)
        o1, l1 = build_sel(g + 1) if g + 1 < NG else ([], [])
        dense(g, o1, l1)
        pend = final(g)
    for f in pend:
        f()


def host_inputs(inputs, b):
    f = np.float32
    x = np.ascontiguousarray(inputs["x"][b], dtype=f)
    cp = np.stack([inputs["conv_w"][0][0], inputs["conv_w"][0][1], inputs["conv_w"][0][2], inputs["conv_w"][0][3],
                   inputs["conv_b"][0], inputs["lru_ba"][0], inputs["lru_bx"][0], inputs["lru_lambda"][0]], axis=-1)
    cpar = np.ascontiguousarray(cp.reshape(8, 128, 8).transpose(1, 0, 2), dtype=f)
    keys = inputs["peer_keys"][0]
    keysT = np.ascontiguousarray(keys.reshape(16, 128, 128).transpose(2, 0, 1), dtype=f)
    u = inputs["peer_u"][0]
    u_l = np.ascontiguousarray(u.reshape(128, 128, 8, 128).transpose(0, 3, 2, 1), dtype=f).reshape(16384, 1024)
    lnp = np.stack([inputs["ln1_g"][0], inputs["ln1_b"][0], inputs["ln2_g"][0], inputs["ln2_b"][0]], axis=0)
    return {
        "xT": np.ascontiguousarray(x.T),
        "x": x,
        "memT": np.ascontiguousarray(inputs["mem"][b].T, dtype=f),
        "w_in": np.ascontiguousarray(inputs["w_in"][0], dtype=f),
        "cpar": cpar,
        "lru_wa": np.ascontiguousarray(inputs["lru_wa"][0], dtype=f),
        "lru_wx": np.ascontiguousarray(inputs["lru_wx"][0], dtype=f),
        "w_mem_kv": np.ascontiguousarray(inputs["w_mem_kv"][0], dtype=f),
        "w_br_attn": np.ascontiguousarray(inputs["w_br_attn"][0], dtype=f),
        "w_br_lru": np.ascontiguousarray(inputs["w_br_lru"][0], dtype=f),
        "w_br_mem": np.ascontiguousarray(inputs["w_br_mem"][0], dtype=f),
        "w_out": np.ascontiguousarray(inputs["w_out"][0], dtype=f),
        "b_gate": np.ascontiguousarray(inputs["b_gate"][0].reshape(1, 3072), dtype=f),
        "lnp": np.ascontiguousarray(lnp, dtype=f),
        "peer_wqT": np.ascontiguousarray(inputs["peer_wq"][0].T, dtype=f),
        "keysT": keysT,
        "u_l": u_l,
        "peer_v": np.ascontiguousarray(inputs["peer_v"][0], dtype=f),
    }


def kernel(**inputs):
    inputs = {k: np.asarray(v) for k, v in inputs.items()}
    nc = build_program()
    shared = host_inputs(inputs, 0)
    in_maps = []
    for b in range(8):
        m = dict(shared)
        x = np.ascontiguousarray(inputs["x"][b], dtype=np.float32)
        m["x"] = x
        m["xT"] = np.ascontiguousarray(x.T)
        m["memT"] = np.ascontiguousarray(inputs["mem"][b].T, dtype=np.float32)
        in_maps.append(m)
    res = run_bass_kernel_spmd(nc, in_maps, core_ids=list(range(8)))
    out = np.stack([np.asarray(r["out"]) for r in res.results], axis=0)
    return out.astype(np.float32)
```

```python
import math
import os
DBG_P1 = int(os.environ.get('DBG_P1', '99'))
from contextlib import ExitStack

import numpy as np
import concourse.bass as bass
import concourse.mybir as mybir
from concourse.bass_utils import run_bass_kernel_spmd

F32 = mybir.dt.float32
BF16 = mybir.dt.bfloat16
U32 = mybir.dt.uint32
AF = mybir.ActivationFunctionType
ALU = mybir.AluOpType
AX = mybir.AxisListType

S_LEN = 4096
D = 1024
NT = S_LEN // 128
ALPHA = 2.0 ** 0.25
LN_EPS = 1e-5
QOFF, KOFF, VOFF, XROFF, YGOFF, MQOFF, GLOFF = 0, 1536, 3072, 4608, 5632, 6656, 7168
DILS = (1, 4, 16)
ENGS = ("pe", "act", "dve", "pool", "sp")


class Trk:
    __slots__ = ("w", "r")

    def __init__(self):
        self.w = None
        self.r = {}


def trks(n):
    return [Trk() for _ in range(n)]


class Sched:
    def __init__(self, nc, es):
        self.nc = nc
        self.semh = {}
        for e in ("pe", "act", "dve", "pool"):
            self.semh[e] = es.enter_context(nc.semaphore("s_" + e))
        self.dq = {"sp": [], "pool": [], "act": []}
        nd = {"sp": 10, "pool": 8, "act": 2}
        for q, n in nd.items():
            for i in range(n):
                k = "d_%s%d" % (q, i)
                self.semh[k] = es.enter_context(nc.semaphore(k))
                self.dq[q].append(k)
        self.cnt = {k: 0 for k in self.semh}
        self.rr = {q: 0 for q in self.dq}
        self.seen = {e: {} for e in ENGS}
        self.prog = {e: [] for e in ENGS}
        self.pending = {e: [] for e in ENGS}

    def _waits(self, eng, reads, writes, extra=(), strict=False):
        deps = {}

        def need(tok):
            if tok is None:
                return
            k, v = tok
            if deps.get(k, 0) < v:
                deps[k] = v
        for b in reads:
            need(b.w)
        for b in writes:
            if b.w is not None and (strict or b.w[0] != eng):
                need(b.w)
            for k, v in b.r.items():
                if strict or k != eng:
                    need((k, v))
        for tok in extra:
            need(tok)
        out = []
        sn = self.seen[eng]
        for k, v in deps.items():
            if sn.get(k, 0) < v:
                sn[k] = v
                out.append((k, v))
        return out

    def emit(self, eng, fn, reads=(), writes=(), sig=True):
        waits = self._waits(eng, reads, writes, strict=(eng != 'pe'))
        ticket = self.cnt[eng] + 1
        if sig:
            self.cnt[eng] = ticket
        self.prog[eng].append((waits, fn, (eng, 1) if sig else None))
        for b in reads:
            if b.r.get(eng, 0) < ticket:
                b.r[eng] = ticket
        for b in writes:
            b.w = (eng, ticket)
            b.r = {}

    def dma(self, q, out, in_, reads=(), writes=()):
        ks = self.dq[q]
        k = ks[self.rr[q] % len(ks)]
        self.rr[q] += 1
        extra = [(k, self.cnt[k])] if self.cnt[k] > 0 else []
        waits = self._waits(q, reads, writes, extra, strict=True)
        self.cnt[k] += 16
        v = self.cnt[k]
        self.prog[q].append((waits, lambda e: e.dma_start(out=out, in_=in_), (k, 16)))
        for b in reads:
            if b.r.get(k, 0) < v:
                b.r[k] = v
        for b in writes:
            b.w = (k, v)
            b.r = {}

    def barrier(self):
        for e in ENGS:
            waits = []
            for k, v in self.cnt.items():
                if v > 0 and k != e and self.seen[e].get(k, 0) < v:
                    self.seen[e][k] = v
                    waits.append((k, v))
            if waits:
                self.prog[e].append((waits, None, None))

    def flush(self):
        nc = self.nc
        semh = self.semh
        prog = self.prog

        def replay(name, e):
            for waits, fn, inc in prog[name]:
                for k, v in waits:
                    e.wait_ge(semh[k], v)
                if fn is None:
                    continue
                ins = fn(e)
                if inc is not None:
                    ins.then_inc(semh[inc[0]], inc[1])

        with nc.Block() as block:
            @block.tensor
            def _(e):
                replay("pe", e)

            @block.scalar
            def _(e):
                replay("act", e)

            @block.vector
            def _(e):
                replay("dve", e)

            @block.gpsimd
            def _(e):
                replay("pool", e)

            @block.sync
            def _(e):
                replay("sp", e)
        self.prog = {e: [] for e in ENGS}

    def mm(self, out, lhsT, rhs, start, stop, reads=(), writes=(), sig=None):
        if sig is None:
            sig = stop
        self.emit("pe", lambda e: e.matmul(out, lhsT, rhs, start=start, stop=stop), reads, writes, sig)

    def tr(self, out, in_, ident, reads=(), writes=()):
        self.emit("pe", lambda e: e.transpose(out, in_, ident), reads, writes)

    def actf(self, out, in_, func, reads=(), writes=(), bias=0.0, scale=1.0, accum=None, eng="act"):
        if accum is None:
            self.emit("act", lambda e: e.activation(out, in_, func, bias=bias, scale=scale), reads, writes)
        else:
            self.emit("act", lambda e: e.activation(out, in_, func, bias=bias, scale=scale, accum_out=accum),
                      reads, writes)

    def tt(self, eng, out, in0, in1, op, reads=(), writes=()):
        self.emit(eng, lambda e: e.tensor_tensor(out, in0, in1, op), reads, writes)

    def ts(self, eng, out, in0, s1, s2, op0, op1=None, reads=(), writes=()):
        if op1 is None:
            self.emit(eng, lambda e: e.tensor_scalar(out, in0, s1, None, op0), reads, writes)
        else:
            self.emit(eng, lambda e: e.tensor_scalar(out, in0, s1, s2, op0, op1), reads, writes)

    def stt(self, eng, out, in0, scalar, in1, op0, op1, reads=(), writes=()):
        self.emit(eng, lambda e: e.scalar_tensor_tensor(out, in0, scalar, in1, op0, op1), reads, writes)

    def cp(self, eng, out, in_, reads=(), writes=()):
        if eng == "act":
            self.emit("act", lambda e: e.copy(out, in_), reads, writes)
        else:
            self.emit(eng, lambda e: e.tensor_copy(out, in_), reads, writes)

    def memset(self, eng, ap, val, writes=()):
        self.emit(eng, lambda e: e.memset(ap, val), (), writes)


class Rec:
    def __init__(self, S):
        self.S = S
        self.ops = []

    def __getattr__(self, name):
        f = getattr(self.S, name)

        def wrap(*a, **k):
            self.ops.append(lambda: f(*a, **k))
        return wrap


class Ring:
    def __init__(self, items):
        self.items = items
        self.i = 0

    def next(self):
        it = self.items[self.i % len(self.items)]
        self.i += 1
        return it


def build_program(stop_after=99, debug=False):
    nc = bass.Bass("TRN2", target_bir_lowering=False)
    es = ExitStack()

    def din(name, shape, dt=F32):
        return nc.dram_tensor(name, list(shape), dt, kind="ExternalInput").ap()

    skind = "ExternalOutput" if debug else "Internal"
    xT_d = din("xT", [D, S_LEN])
    x_d = din("x", [S_LEN, D])
    memT_d = din("memT", [D, 256])
    w_in_d = din("w_in", [D, 10240])
    cpar_d = din("cpar", [128, 8, 8])
    wa_d = din("lru_wa", [16, 64, 64])
    wx_d = din("lru_wx", [16, 64, 64])
    wkv_d = din("w_mem_kv", [D, 1024])
    wbra_d = din("w_br_attn", [512, D])
    wbrl_d = din("w_br_lru", [1024, D])
    wbrm_d = din("w_br_mem", [512, D])
    wout_d = din("w_out", [D, D])
    bgate_d = din("b_gate", [1, 3072])
    lnp_d = din("lnp", [4, D])
    wqT_d = din("peer_wqT", [2048, D])
    keysT_d = din("keysT", [128, 16, 128])
    ul_d = din("u_l", [16384, 1024])
    v_d = din("peer_v", [16384, 1024])
    out_d = nc.dram_tensor("out", [S_LEN, D], F32, kind="ExternalOutput").ap()
    br_s = nc.dram_tensor("br_s", [2048, S_LEN], BF16, kind=skind).ap()
    x1_s = nc.dram_tensor("x1_s", [S_LEN, D], F32, kind=skind).ap()
    uv_s = nc.dram_tensor("uv_s", [16384, 2048], BF16, kind="Internal").ap()

    S = Sched(nc, es)

    def sb(st, name, shape, dt):
        return st.enter_context(nc.sbuf_tensor("sb_" + name, list(shape), dt))

    def psb(st, name, shape=(128, 512), dt=F32):
        return st.enter_context(nc.psum_tensor("ps_" + name, list(shape), dt))

    ident = sb(es, "ident", [128, 128], F32)
    ones_bf = sb(es, "ones_bf", [128, 128], BF16)
    iota_i = sb(es, "iota_i", [128, 128], mybir.dt.int32)
    iota_f = sb(es, "iota_f", [128, 128], F32)
    iota_p = sb(es, "iota_p", [128, 1], F32)
    t_const = Trk()
    S.emit("pool", lambda e: e.iota(iota_i[:], pattern=[[1, 128]], base=0, channel_multiplier=0), (), [t_const])
    S.cp("dve", iota_f[:], iota_i[:], [t_const], [t_const])
    S.emit("pool", lambda e: e.iota(iota_i[:, 0:1], pattern=[[1, 1]], base=0, channel_multiplier=1), [t_const], [t_const])
    S.cp("dve", iota_p[:], iota_i[:, 0:1], [t_const], [t_const])
    S.ts("dve", ident[:], iota_f[:], iota_p[:, 0:1], None, ALU.is_equal, reads=[t_const], writes=[t_const])
    S.memset("dve", ones_bf[:], 1.0, [t_const])
    iota_b = sb(es, "iota_b", [128, 128], BF16)
    S.cp("dve", iota_b[:], iota_f[:], [t_const], [t_const])

    t_us, t_vs = Trk(), Trk()
    NCH = 32
    rows = 16384 // NCH

    cast_i = [0]

    def next_cast(n=1):
        for _ in range(n):
            i = cast_i[0]
            if stop_after < 4 or i >= 2 * NCH:
                return
            cast_i[0] += 1
            j = i // 2
            if i % 2 == 0:
                S.dma("pool", uv_s[j * rows:(j + 1) * rows, 0:1024], ul_d[j * rows:(j + 1) * rows, :], (), ())
            else:
                S.dma("pool", uv_s[j * rows:(j + 1) * rows, 1024:2048], v_d[j * rows:(j + 1) * rows, :], (), ())

    with ExitStack() as p12:
        xT = sb(p12, "xT", [128, 8, S_LEN], BF16)
        t_xT = trks(8)
        xT_v = xT_d.rearrange("(k p) t -> p k t", p=128)
        for k in range(8):
            S.dma("pool", xT[:, k, :], xT_v[:, k, :], (), [t_xT[k]])
        w_in_v = w_in_d.rearrange("(k p) n -> p k n", p=128)

        with ExitStack() as p1:
            cpar = sb(p1, "cpar", [128, 8, 8], F32)
            cneg = sb(p1, "cneg", [128, 8], F32)
            cneg2 = sb(p1, "cneg2", [128, 8], F32)
            wa_bd = sb(p1, "wa_bd", [128, 8, 128], BF16)
            wx_bd = sb(p1, "wx_bd", [128, 8, 128], BF16)
            t_cp, t_wbd = Trk(), Trk()
            S.dma("sp", cpar[:], cpar_d, (), [t_cp])
            S.actf(cneg[:], cpar[:, :, 7], AF.Exp, [t_cp], [t_cp], scale=-1.0)
            S.actf(cneg[:], cneg[:], AF.Ln, [t_cp], [t_cp], bias=1.0)
            S.ts("dve", cneg2[:], cneg[:], -16.0, None, ALU.mult, reads=[t_cp], writes=[t_cp])
            S.ts("dve", cneg[:], cneg[:], -8.0, None, ALU.mult, reads=[t_cp], writes=[t_cp])
            S.memset("dve", wa_bd[:], 0.0, [t_wbd])
            S.memset("dve", wx_bd[:], 0.0, [t_wbd])
            for (wd, wsb) in ((wa_d, wa_bd), (wx_d, wx_bd)):
                wv = wd.rearrange("(c two) i j -> two i c j", two=2)
                S.dma("pool", wsb[0:64, :, 0:64], wv[0], (), [t_wbd])
                S.dma("pool", wsb[64:128, :, 64:128], wv[1], (), [t_wbd])

            HL = 1024
            NBC = S_LEN // HL
            NSET = 4
            XR = [sb(p1, "XR%d" % i, [128, HL + 4], F32) for i in range(NSET)]
            XC = [sb(p1, "XC%d" % i, [128, HL], F32) for i in range(NSET)]
            XCb = [sb(p1, "XCb%d" % i, [128, HL], BF16) for i in range(NSET)]
            RA = [sb(p1, "RA%d" % i, [128, HL], F32) for i in range(NSET)]
            IB = [sb(p1, "IB%d" % i, [128, HL], F32) for i in range(NSET)]
            MU = [sb(p1, "MU%d" % i, [128, HL], F32) for i in range(NSET)]
            YG = [sb(p1, "YG%d" % i, [128, HL], F32) for i in range(NSET)]
            REC = [sb(p1, "REC%d" % i, [128, HL], BF16) for i in range(NSET)]
            wxy = [sb(p1, "wxy%d" % i, [128, 8, 256], BF16) for i in range(2)]
            t_wxy = trks(2)
            t_XR, t_XC, t_XCb, t_RA, t_IB, t_MU, t_YG, t_REC = [trks(NSET) for _ in range(8)]
            pp = [psb(p1, "p1ps%d" % i) for i in range(8)]
            t_pp = trks(8)
            ppr = Ring(list(zip(pp, t_pp)))

            def block_ops(c, hh, it):
                r1, r2, r3 = Rec(S), Rec(S), Rec(S)
                wt, t_w = wxy[c % 2], t_wxy[c % 2]
                if hh == 0:
                    for cn in ([0, 1] if c == 0 else [c + 1]):
                        if cn < 8:
                            r1.dma("pool", wxy[cn % 2][:, :, 0:128],
                                   w_in_v[:, :, XROFF + cn * 128:XROFF + (cn + 1) * 128], (), [t_wxy[cn % 2]])
                            r1.dma("pool", wxy[cn % 2][:, :, 128:256],
                                   w_in_v[:, :, YGOFF + cn * 128:YGOFF + (cn + 1) * 128], (), [t_wxy[cn % 2]])
                pb = it % NSET
                pv = (it - 1) % NSET
                xr, xc, xcb, ra, ib, mu, yg, rec = XR[pb], XC[pb], XCb[pb], RA[pb], IB[pb], MU[pb], YG[pb], REC[pb]
                for which, (dst, t_dst, off) in enumerate(((xr, t_XR[pb], 4), (yg, t_YG[pb], 0))):
                    for tb in range(HL // 512):
                        ps, t_ps = ppr.next()
                        t0 = hh * HL + tb * 512
                        for k in range(8):
                            r1.mm(ps[:], wt[:, k, which * 128:(which + 1) * 128], xT[:, k, t0:t0 + 512],
                                  k == 0, k == 7, [t_w, t_xT[k]], [t_ps])
                        eng = "act" if tb % 2 == 0 else "dve"
                        r1.cp(eng, dst[:, off + tb * 512: off + (tb + 1) * 512], ps[:], [t_ps], [t_dst])
                if hh == 0:
                    r2.memset("dve", xr[:, 0:4], 0.0, [t_XR[pb]])
                else:
                    r2.cp("dve", xr[:, 0:4], XR[pv][:, HL:HL + 4], [t_XR[pv]], [t_XR[pb]])
                r2.actf(xc[:], xr[:, 4:4 + HL], AF.Identity, [t_XR[pb], t_cp], [t_XC[pb]],
                        bias=cpar[:, c, 4:5], scale=cpar[:, c, 3:4])
                for j in range(3):
                    r2.stt("dve", xc[:], xr[:, 1 + j:1 + j + HL], cpar[:, c, j:j + 1], xc[:],
                           ALU.mult, ALU.add, [t_XR[pb], t_cp, t_XC[pb]], [t_XC[pb]])
                r2.cp("pool", xcb[:], xc[:], [t_XC[pb]], [t_XCb[pb]])
                for (wsb, dst, t_dst, bcol) in ((wa_bd, ra, t_RA[pb], 5), (wx_bd, ib, t_IB[pb], 6)):
                    for tb in range(HL // 512):
                        ps, t_ps = ppr.next()
                        r2.mm(ps[:], wsb[:, c, :], xcb[:, tb * 512:(tb + 1) * 512], True, True,
                              [t_wbd, t_XCb[pb]], [t_ps])
                        r2.actf(dst[:, tb * 512:(tb + 1) * 512], ps[:], AF.Sigmoid, [t_ps, t_cp], [t_dst],
                                bias=cpar[:, c, bcol:bcol + 1])
                r3.actf(mu[:], ra[:], AF.Exp, [t_RA[pb], t_cp], [t_MU[pb]], scale=cneg2[:, c:c + 1])
                r3.actf(ra[:], ra[:], AF.Exp, [t_RA[pb], t_cp], [t_RA[pb]], scale=cneg[:, c:c + 1])
                r3.actf(mu[:], mu[:], AF.Sqrt, [t_MU[pb]], [t_MU[pb]], bias=1.0, scale=-1.0)
                r3.tt("dve", ib[:], ib[:], xc[:], ALU.mult, [t_IB[pb], t_XC[pb]], [t_IB[pb]])
                r3.tt("pool", ib[:], ib[:], mu[:], ALU.mult, [t_IB[pb], t_MU[pb]], [t_IB[pb]])
                if hh == 0:
                    r3.emit("dve", (lambda e, mu=mu, ra=ra, ib=ib: e.tensor_tensor_scan(
                        mu[:], ra[:], ib[:], 0.0, ALU.mult, ALU.add)),
                        [t_RA[pb], t_IB[pb], t_MU[pb]], [t_MU[pb]])
                else:
                    r3.emit("dve", (lambda e, mu=mu, ra=ra, ib=ib, pm=MU[pv]: e.tensor_tensor_scan(
                        mu[:], ra[:], ib[:], pm[:, HL - 1:HL], ALU.mult, ALU.add)),
                        [t_RA[pb], t_IB[pb], t_MU[pb], t_MU[pv]], [t_MU[pb]])
                r3.tt("pool", xc[:], yg[:], yg[:], ALU.mult, [t_YG[pb]], [t_XC[pb]])
                r3.ts("dve", xc[:], xc[:], 0.044715, 1.0, ALU.mult, ALU.add, reads=[t_XC[pb]], writes=[t_XC[pb]])
                r3.tt("pool", xc[:], xc[:], yg[:], ALU.mult, [t_XC[pb], t_YG[pb]], [t_XC[pb]])
                r3.actf(xc[:], xc[:], AF.Sigmoid, [t_XC[pb]], [t_XC[pb]], scale=1.5957691216057308)
                r3.tt("pool", xc[:], xc[:], yg[:], ALU.mult, [t_XC[pb], t_YG[pb]], [t_XC[pb]])
                r3.tt("dve", rec[:], mu[:], xc[:], ALU.mult, [t_MU[pb], t_XC[pb]], [t_REC[pb]])
                r3.dma("sp", br_s[512 + c * 128:512 + (c + 1) * 128, hh * HL:(hh + 1) * HL], rec[:], [t_REC[pb]], ())
                if it % 2 == 0:
                    r3.ops.append(lambda: next_cast(1))
                return r1.ops, r2.ops, r3.ops

            blocks = [(c, hh) for c in range(8 if stop_after >= 1 else 0) for hh in range(NBC)]
            stages = []
            for it in range(len(blocks) + 2):
                if it < len(blocks):
                    stages.append(block_ops(blocks[it][0], blocks[it][1], it))
                lists = []
                if it < len(blocks):
                    lists.append(stages[it][0])
                if 0 <= it - 1 < len(blocks):
                    lists.append(stages[it - 1][1])
                if 0 <= it - 2 < len(blocks):
                    lists.append(stages[it - 2][2])
                n = max(len(l) for l in lists)
                pos = [0] * len(lists)
                for step in range(1, n + 1):
                    for li, l in enumerate(lists):
                        want = (step * len(l)) // n
                        while pos[li] < want:
                            l[pos[li]]()
                            pos[li] += 1
            S.barrier()
            S.flush()

        with ExitStack() as p2:
            QT = sb(p2, "QT", [128, S_LEN], BF16)
            KT = sb(p2, "KT", [128, S_LEN], BF16)
            VV = sb(p2, "VV", [128, 32, 128], BF16)
            ACCN = sb(p2, "ACCN", [128, S_LEN], F32)
            ACCD = sb(p2, "ACCD", [128, S_LEN], F32)
            ATT = sb(p2, "ATT", [128, S_LEN], BF16)
            mask2 = sb(p2, "mask2", [128, 512], BF16)
            mtmp = sb(p2, "mtmp", [128, 128], F32)
            wqkv = [sb(p2, "wqkv%d" % i, [128, 8, 384], BF16) for i in range(2)]
            PT = [sb(p2, "PT%d" % i, [128, 512], BF16) for i in range(3)]
            t_QT, t_KT, t_VV, t_ACCN, t_ACCD, t_ATT, t_mask = trks(7)
            t_wqkv = trks(2)
            t_PT = trks(3)
            pp = [psb(p2, "p2ps%d" % i) for i in range(8)]
            t_pp = trks(8)
            ppr = Ring(list(zip(pp, t_pp)))
            S.ts("dve", mtmp[:], iota_f[:], iota_p[:, 0:1], None, ALU.is_ge, reads=[t_const], writes=[t_mask])
            S.cp("dve", mask2[:, 0:128], mtmp[:], [t_mask], [t_mask])
            S.cp("dve", mask2[:, 256:384], mtmp[:], [t_mask], [t_mask])
            S.ts("dve", mtmp[:], iota_f[:], iota_p[:, 0:1], None, ALU.is_le, reads=[t_const, t_mask], writes=[t_mask])
            S.cp("dve", mask2[:, 128:256], mtmp[:], [t_mask], [t_mask])
            S.cp("dve", mask2[:, 384:512], mtmp[:], [t_mask], [t_mask])
            inv_sqrt = 1.0 / math.sqrt(128.0)
            hi = 0
            for h in range(4 if stop_after >= 2 else 0):
                for g in range(3):
                    d = DILS[g]
                    L = S_LEN // d
                    nb = L // 128
                    wt, t_w = wqkv[hi % 2], t_wqkv[hi % 2]
                    for hn in ([0, 1] if hi == 0 else [hi + 1]):
                        if hn < 12:
                            coln = (hn % 3) * 512 + (hn // 3) * 128
                            for j, off in enumerate((QOFF, KOFF, VOFF)):
                                S.dma("pool", wqkv[hn % 2][:, :, j * 128:(j + 1) * 128],
                                      w_in_v[:, :, off + coln:off + coln + 128], (), [t_wqkv[hn % 2]])
                    hi += 1
                    next_cast(2)
                    col = g * 512 + h * 128
                    for j, (dst, t_dst) in enumerate(((QT, t_QT), (KT, t_KT))):
                        dv = dst[:].rearrange("p (r m) -> p m r", r=d)
                        for tb in range(8):
                            ps, t_ps = ppr.next()
                            for k in range(8):
                                S.mm(ps[:], wt[:, k, j * 128:(j + 1) * 128], xT[:, k, tb * 512:(tb + 1) * 512],
                                     k == 0, k == 7, [t_w, t_xT[k]], [t_ps])
                            m0 = tb * (512 // d)
                            o_ap = dv[:, m0:m0 + 512 // d, :]
                            i_ap = ps[:].rearrange("p (m r) -> p m r", r=d)
                            if j == 0:
                                S.actf(o_ap, i_ap, AF.Copy, [t_ps], [t_dst], scale=inv_sqrt)
                            else:
                                S.cp("dve", o_ap, i_ap, [t_ps], [t_dst])
                    xTg = [xT[:, k, :].rearrange("p (m r) -> p r m", r=d) for k in range(8)]
                    for jb4 in range(8):
                        ps, t_ps = ppr.next()
                        for q4 in range(4):
                            jb = jb4 * 4 + q4
                            r, n = jb // nb, jb % nb
                            for k in range(8):
                                S.mm(ps[:, q4 * 128:(q4 + 1) * 128], xTg[k][:, r, n * 128:(n + 1) * 128],
                                     wt[:, k, 256:384], k == 0, k == 7, [t_w, t_xT[k]], [t_ps],
                                     sig=(k == 7 and q4 == 3))
                        S.cp("act" if jb4 % 2 else "dve", VV[:, jb4 * 4:(jb4 + 1) * 4, :],
                             ps[:].rearrange("p (a b) -> p a b", b=128), [t_ps], [t_VV])
                    accn_v = ACCN[:].rearrange("p (m r) -> p r m", r=d)
                    accd_v = ACCD[:].rearrange("p (m r) -> p r m", r=d)
                    prevPT = None
                    for jp in range(16):
                        ps, t_ps = ppr.next()
                        for u in range(2):
                            jb = jp * 2 + u
                            n = jb % nb
                            ncol = 256 if n < nb - 1 else 128
                            S.mm(ps[:, u * 256:u * 256 + ncol], KT[:, jb * 128:(jb + 1) * 128],
                                 QT[:, jb * 128:jb * 128 + ncol], True, True, [t_KT, t_QT], [t_ps], sig=(u == 1))
                        pt, t_pt = PT[jp % 3], t_PT[jp % 3]
                        wv = 512 if ((jp * 2 + 1) % nb) < nb - 1 else 384
                        S.actf(pt[:, 0:wv], ps[:, 0:wv], AF.Exp, [t_ps], [t_pt])
                        S.tt("pool" if jp % 2 else "dve", pt[:, 0:wv], pt[:, 0:wv], mask2[:, 0:wv], ALU.mult,
                             [t_pt, t_mask], [t_pt])
                        pso, t_pso = ppr.next()
                        for u in range(2):
                            jb = jp * 2 + u
                            n = jb % nb
                            srcs = []
                            if n > 0:
                                if u == 0:
                                    srcs.append((jb - 1, prevPT[0][:, 384:512], prevPT[1]))
                                else:
                                    srcs.append((jb - 1, pt[:, 128:256], t_pt))
                            srcs.append((jb, pt[:, u * 256:u * 256 + 128], t_pt))
                            for half, use_ones in ((0, False), (1, True)):
                                oc = half * 256 + u * 128
                                for si, (kb, p_ap, t_p) in enumerate(srcs):
                                    lhs = ones_bf[:] if use_ones else VV[:, kb, :]
                                    S.mm(pso[:, oc:oc + 128], lhs, p_ap, si == 0, si == len(srcs) - 1,
                                         [t_p, t_VV, t_const], [t_pso], sig=(si == len(srcs) - 1 and half == 1 and u == 1))
                        prevPT = (pt, t_pt)
                        c0 = jp * 256
                        if L >= 256:
                            r0, m0 = c0 // L, c0 % L
                            on = accn_v[:, r0, m0:m0 + 256]
                            od = accd_v[:, r0, m0:m0 + 256]
                            inn, ind = pso[:, 0:256], pso[:, 256:512]
                        else:
                            raise AssertionError
                        if g == 0:
                            S.cp("act", on, inn, [t_pso], [t_ACCN])
                            S.cp("act", od, ind, [t_pso], [t_ACCD])
                        else:
                            S.tt("dve", on, on, inn, ALU.add, [t_pso, t_ACCN], [t_ACCN])
                            S.tt("dve", od, od, ind, ALU.add, [t_pso, t_ACCD], [t_ACCD])
                S.emit("dve", lambda e: e.reciprocal(ACCD[:], ACCD[:]), [t_ACCD], [t_ACCD])
                S.tt("pool", ATT[:], ACCN[:], ACCD[:], ALU.mult, [t_ACCN, t_ACCD], [t_ATT])
                S.dma("sp", br_s[h * 128:(h + 1) * 128, :], ATT[:], [t_ATT], ())

            if stop_after >= 2:
                memT = sb(p2, "memT", [128, 8, 256], BF16)
                wkv = sb(p2, "wkv", [128, 8, 1024], BF16)
                wmq = sb(p2, "wmq", [128, 8, 512], BF16)
                KmT = sb(p2, "KmT", [128, 4, 256], BF16)
                Vm = sb(p2, "Vm", [128, 2, 512], BF16)
                MQ = [sb(p2, "MQ%d" % i, [128, 512], BF16) for i in range(2)]
                PM = [sb(p2, "PM%d" % i, [128, 2, 512], BF16) for i in range(2)]
                MO = [sb(p2, "MO%d" % i, [128, 512], F32) for i in range(2)]
                MOb = [sb(p2, "MOb%d" % i, [128, 512], BF16) for i in range(2)]
                t_memT, t_wkv, t_wmq, t_KmT, t_Vm = trks(5)
                t_MQ, t_PM, t_MO, t_MOb = trks(2), trks(2), trks(2), trks(2)
                S.dma("pool", memT[:], memT_d.rearrange("(k p) m -> p k m", p=128), (), [t_memT])
                S.dma("pool", wkv[:], wkv_d.rearrange("(k p) n -> p k n", p=128), (), [t_wkv])
                S.dma("pool", wmq[:], w_in_v[:, :, MQOFF:MQOFF + 512], (), [t_wmq])
                for hh in range(4):
                    ps, t_ps = ppr.next()
                    for k in range(8):
                        S.mm(ps[:, 0:256], wkv[:, k, hh * 128:(hh + 1) * 128], memT[:, k, :], k == 0, k == 7,
                             [t_wkv, t_memT], [t_ps])
                    S.cp("dve", KmT[:, hh, :], ps[:, 0:256], [t_ps], [t_KmT])
                for mc in range(2):
                    ps, t_ps = ppr.next()
                    for k in range(8):
                        S.mm(ps[:], memT[:, k, mc * 128:(mc + 1) * 128], wkv[:, k, 512:1024], k == 0, k == 7,
                             [t_wkv, t_memT], [t_ps])
                    S.cp("dve", Vm[:, mc, :], ps[:], [t_ps], [t_Vm])
                it = 0
                for tb in range(8):
                    for hh in range(4):
                        b2 = it % 2
                        it += 1
                        ps, t_ps = ppr.next()
                        for k in range(8):
                            S.mm(ps[:], wmq[:, k, hh * 128:(hh + 1) * 128], xT[:, k, tb * 512:(tb + 1) * 512],
                                 k == 0, k == 7, [t_wmq, t_xT[k]], [t_ps])
                        S.actf(MQ[b2][:], ps[:], AF.Copy, [t_ps], [t_MQ[b2]], scale=inv_sqrt)
                        for mc in range(2):
                            ps, t_ps = ppr.next()
                            S.mm(ps[:], KmT[:, hh, mc * 128:(mc + 1) * 128], MQ[b2][:], True, True,
                                 [t_KmT, t_MQ[b2]], [t_ps])
                            S.actf(PM[b2][:, mc, :], ps[:], AF.Exp, [t_ps], [t_PM[b2]])
                        pn, t_pn = ppr.next()
                        pd, t_pd = ppr.next()
                        for mc in range(2):
                            S.mm(pn[:], Vm[:, mc, hh * 128:(hh + 1) * 128], PM[b2][:, mc, :], mc == 0, mc == 1,
                                 [t_Vm, t_PM[b2]], [t_pn])
                        for mc in range(2):
                            S.mm(pd[:], ones_bf[:], PM[b2][:, mc, :], mc == 0, mc == 1, [t_const, t_PM[b2]], [t_pd])
                        S.emit("dve", lambda e, o=MO[b2], i=pd: e.reciprocal(o[:], i[:]), [t_pd], [t_MO[b2]])
                        S.tt("dve", MOb[b2][:], MO[b2][:], pn[:], ALU.mult, [t_MO[b2], t_pn], [t_MOb[b2]])
                        S.dma("sp", br_s[1536 + hh * 128:1536 + (hh + 1) * 128, tb * 512:(tb + 1) * 512], MOb[b2][:],
                              [t_MOb[b2]], ())
            S.barrier()
            S.flush()

    t_br = Trk()

    with ExitStack() as p3:
        wg = sb(p3, "wg", [128, 8, 3072], BF16)
        wbr = sb(p3, "wbr", [128, 16, 1024], BF16)
        wout = sb(p3, "wout", [128, 8, 1024], BF16)
        bg = sb(p3, "bg", [1, 3072], F32)
        ones_f = sb(p3, "ones_f", [1, 128], F32)
        lng = sb(p3, "lng", [128, 1024], F32)
        lnb = sb(p3, "lnb", [128, 1024], F32)
        t_wg, t_wbr, t_wout, t_bg, t_ln = trks(5)
        if stop_after >= 3:
            for k in range(8):
                S.dma("pool", wg[:, k, :], w_in_v[:, k, GLOFF:GLOFF + 3072], (), [t_wg])
            S.dma("pool", wbr[:, 0:4, :], wbra_d.rearrange("(k p) n -> p k n", p=128), (), [t_wbr])
            S.dma("pool", wbr[:, 4:12, :], wbrl_d.rearrange("(k p) n -> p k n", p=128), (), [t_wbr])
            S.dma("pool", wbr[:, 12:16, :], wbrm_d.rearrange("(k p) n -> p k n", p=128), (), [t_wbr])
            S.dma("pool", wout[:], wout_d.rearrange("(k p) n -> p k n", p=128), (), [t_wout])
            S.dma("sp", bg[:], bgate_d, (), [t_bg])
            S.memset("dve", ones_f[:], 1.0, [t_bg])
            S.dma("sp", lng[:], lnp_d[0:1, :].to_broadcast([128, 1024]), (), [t_ln])
            S.dma("sp", lnb[:], lnp_d[1:2, :].to_broadcast([128, 1024]), (), [t_ln])
        xTb = [sb(p3, "xTb%d" % i, [128, 8, 512], BF16) for i in range(2)]
        brT = [sb(p3, "brT%d" % i, [128, 16, 512], BF16) for i in range(2)]
        xres = [sb(p3, "xres%d" % i, [128, 1024], F32) for i in range(2)]
        gate = sb(p3, "gate", [128, 1024], F32)
        macc = sb(p3, "macc", [128, 1024], F32)
        macc2 = sb(p3, "macc2", [128, 1024], F32)
        t_macc2 = Trk()
        mtmp3 = sb(p3, "mtmp3", [128, 1024], F32)
        mT = sb(p3, "mT", [128, 8, 128], BF16)
        yb = [sb(p3, "yb%d" % i, [128, 1024], F32) for i in range(2)]
        junk = sb(p3, "junk3", [128, 1024], F32)
        st = [sb(p3, "st%d" % i, [128, 8], F32) for i in range(2)]
        t_xTb, t_brT, t_xres, t_yb, t_st = trks(2), trks(2), trks(2), trks(2), trks(2)
        t_gate, t_macc, t_mtmp, t_mT, t_junk = trks(5)
        pg = [psb(p3, "p3g%d" % i) for i in range(2)]
        pj = [psb(p3, "p3j%d" % i) for i in range(2)]
        ptr = [psb(p3, "p3t%d" % i) for i in range(2)]
        pm = [psb(p3, "p3m%d" % i) for i in range(2)]
        t_pg, t_pj, t_ptr, t_pm = trks(2), trks(2), trks(2), trks(2)
        xT_v2 = xT_d.rearrange("(k p) t -> p k t", p=128)
        br_v = br_s.rearrange("(k p) t -> p k t", p=128)
        brch = ((0, 4), (4, 12), (12, 16))
        maccr = [macc, macc2]
        t_maccr = [t_macc, t_macc2]

        def tile_ops(ti):
            tb, tl = ti // 4, ti % 4
            b2 = tb % 2
            b1 = ti % 2
            mac, t_mac = maccr[b1], t_maccr[b1]
            RA_, RB_ = Rec(S), Rec(S)
            if tl == 0:
                for tbn in ([0, 1] if tb == 0 else [tb + 1]):
                    if tbn < 8:
                        RA_.dma("pool", xTb[tbn % 2][:], xT_v2[:, :, tbn * 512:(tbn + 1) * 512], (), [t_xTb[tbn % 2]])
                        RA_.dma("sp", brT[tbn % 2][:], br_v[:, :, tbn * 512:(tbn + 1) * 512], [t_br], [t_brT[tbn % 2]])
            tsl = slice(tl * 128, (tl + 1) * 128)
            RA_.dma("sp", xres[b1][:], x_d[ti * 128:(ti + 1) * 128, :], (), [t_xres[b1]])
            RA_.ops.append(lambda: next_cast(1))
            for br in range(3):
                for hf in range(2):
                    cs = br * 1024 + hf * 512
                    RA_.mm(pg[hf][:], ones_f[0:1, :], bg[0:1, cs:cs + 512], True, False, [t_bg], [t_pg[hf]], sig=False)
                    for k in range(8):
                        RA_.mm(pg[hf][:], xTb[b2][:, k, tsl], wg[:, k, cs:cs + 512], False, k == 7,
                               [t_xTb[b2], t_wg], [t_pg[hf]])
                    RA_.actf(gate[:, hf * 512:(hf + 1) * 512], pg[hf][:], AF.Sigmoid, [t_pg[hf]], [t_gate])
                    c0, c1 = brch[br]
                    for c in range(c0, c1):
                        RA_.mm(pj[hf][:], brT[b2][:, c, tsl], wbr[:, c, hf * 512:(hf + 1) * 512], c == c0, c == c1 - 1,
                               [t_brT[b2], t_wbr], [t_pj[hf]])
                    hs = slice(hf * 512, (hf + 1) * 512)
                    if br == 0:
                        RA_.tt("dve", mac[:, hs], gate[:, hs], pj[hf][:], ALU.mult, [t_gate, t_pj[hf]], [t_mac])
                    else:
                        RA_.tt("dve", mtmp3[:, hs], gate[:, hs], pj[hf][:], ALU.mult, [t_gate, t_pj[hf]], [t_mtmp])
                        RA_.tt("pool", mac[:, hs], mac[:, hs], mtmp3[:, hs], ALU.add, [t_mtmp, t_mac], [t_mac])
            for k in range(8):
                RB_.tr(ptr[k // 4][:, (k % 4) * 128:(k % 4 + 1) * 128], mac[:, k * 128:(k + 1) * 128], ident[:],
                       [t_mac, t_const], [t_ptr[k // 4]])
            for hf in range(2):
                RB_.cp("act", mT[:, hf * 4:(hf + 1) * 4, :], ptr[hf][:].rearrange("p (a b) -> p a b", b=128),
                       [t_ptr[hf]], [t_mT])
            for hf in range(2):
                for k in range(8):
                    RB_.mm(pm[hf][:], mT[:, k, :], wout[:, k, hf * 512:(hf + 1) * 512], k == 0, k == 7,
                           [t_mT, t_wout], [t_pm[hf]])
                RB_.stt("dve", yb[b1][:, hf * 512:(hf + 1) * 512], xres[b1][:, hf * 512:(hf + 1) * 512], ALPHA,
                        pm[hf][:], ALU.mult, ALU.add, [t_xres[b1], t_pm[hf]], [t_yb[b1]])
            layer_norm(RB_, yb[b1], t_yb[b1], st[b1], t_st[b1], junk, t_junk, lng, lnb, t_ln)
            RB_.dma("sp", x1_s[ti * 128:(ti + 1) * 128, :], yb[b1][:], [t_yb[b1]], ())
            return RA_.ops, RB_.ops

        prevB3 = []
        for ti in range(32 if stop_after >= 3 else 0):
            A3, B3 = tile_ops(ti)
            nA, nB = len(A3), len(prevB3)
            jb = 0
            for ia, fa in enumerate(A3):
                fa()
                want = ((ia + 1) * nB) // nA
                while jb < want:
                    prevB3[jb]()
                    jb += 1
            while jb < nB:
                prevB3[jb]()
                jb += 1
            prevB3 = B3
        for fb in prevB3:
            fb()
        next_cast(2 * NCH)
        S.barrier()
        S.flush()

    if stop_after >= 4:
        with ExitStack() as p4:
            peer_phase(nc, S, p4, sb, psb, ident, iota_f, iota_b, t_const, x1_s, wqT_d, keysT_d, uv_s, lnp_d, out_d)
            S.barrier()
            S.flush()
    else:
        S.barrier()
        S.flush()
    es.close()
    return nc


def layer_norm(S, y, t_y, st, t_st, junk, t_junk, lng, lnb, t_ln):
    S.emit("dve", lambda e: e.tensor_reduce(st[:, 0:1], y[:], AX.X, ALU.add), [t_y], [t_st])
    S.ts("dve", st[:, 1:2], st[:, 0:1], -1.0 / D, None, ALU.mult, reads=[t_st], writes=[t_st])
    S.actf(y[:], y[:], AF.Identity, [t_y, t_st], [t_y], bias=st[:, 1:2])
    S.tt("pool", junk[:], y[:], y[:], ALU.mult, [t_y], [t_junk])
    S.emit("dve", lambda e: e.tensor_reduce(st[:, 2:3], junk[:], AX.X, ALU.add), [t_junk, t_st], [t_st])
    S.ts("dve", st[:, 3:4], st[:, 2:3], 1.0 / D, LN_EPS, ALU.mult, ALU.add, reads=[t_st], writes=[t_st])
    S.actf(st[:, 3:4], st[:, 3:4], AF.Sqrt, [t_st], [t_st])
    S.emit("dve", lambda e: e.reciprocal(st[:, 3:4], st[:, 3:4]), [t_st], [t_st])
    S.actf(y[:], y[:], AF.Copy, [t_y, t_st], [t_y], scale=st[:, 3:4])
    S.tt("pool", y[:], y[:], lng[:], ALU.mult, [t_y, t_ln], [t_y])
    S.tt("pool", y[:], y[:], lnb[:], ALU.add, [t_y, t_ln], [t_y])


def peer_phase(nc, S, p4, sb, psb, ident, iota_f, iota_b, t_const, x1_s, wqT_d, keysT_d, uv_s, lnp_d, out_d):
    TG = 256
    NG = S_LEN // TG
    Wsc = sb(p4, "Wsc", [128, 8, 2048], BF16)
    lng = sb(p4, "lng2", [128, 1024], F32)
    lnb = sb(p4, "lnb2", [128, 1024], F32)
    t_Wsc, t_ln = trks(2)
    S.dma("sp", lng[:], lnp_d[2:3, :].to_broadcast([128, 1024]), (), [t_ln])
    S.dma("sp", lnb[:], lnp_d[3:4, :].to_broadcast([128, 1024]), (), [t_ln])
    with ExitStack() as pre:
        wqT = sb(pre, "wqT", [128, 16, 1024], F32)
        keysT = sb(pre, "keysTf", [128, 16, 128], F32)
        t_keys = Trk()
        t_wqT4 = trks(4)
        wqT_v = wqT_d.rearrange("(e c) d -> c e d", c=128)
        S.dma("sp", keysT[:], keysT_d, (), [t_keys])
        for e4 in range(4):
            S.dma("sp", wqT[:, e4 * 4:(e4 + 1) * 4, :], wqT_v[:, e4 * 4:(e4 + 1) * 4, :], (), [t_wqT4[e4]])
        pw = [psb(pre, "p4w%d" % i) for i in range(4)]
        t_pw = trks(4)
        i = 0
        for e4 in range(4):
            for k in range(8):
                ps, t_ps = pw[i % 4], t_pw[i % 4]
                i += 1
                for ee in range(4):
                    e_ = e4 * 4 + ee
                    S.mm(ps[:, ee * 128:(ee + 1) * 128], wqT[:, e_, k * 128:(k + 1) * 128], keysT[:, e_, :], True, True,
                         [t_wqT4[e4], t_keys], [t_ps], sig=(ee == 3))
                S.cp("act" if i % 2 else "dve", Wsc[:, k, e4 * 512:(e4 + 1) * 512], ps[:], [t_ps], [t_Wsc])
        S.barrier()
        S.flush()

    Gd = sb(p4, "Gd", [128, TG, 128], BF16)
    x1r = [sb(p4, "x1r%d" % i, [128, 1024], F32) for i in range(2)]
    x1Tr = [sb(p4, "x1Tr%d" % i, [128, 8, TG], BF16) for i in range(2)]
    SELTr = [sb(p4, "SELT%d" % i, [128, 3, 128], F32) for i in range(2)]
    BIG = sb(p4, "BIG", [128, 2048], F32)
    SC = BIG[:].rearrange("p (a b) -> p a b", b=128)
    CAND = BIG[:].rearrange("p (h c) -> p h c", c=256)
    OH = BIG[:].rearrange("p (h k i) -> p h k i", k=16, i=16)
    SC2 = sb(p4, "SC2", [128, 256], F32)
    M1 = sb(p4, "M1", [128, 16, 16], F32)
    IDX = sb(p4, "IDX", [128, 16, 16], U32)
    IDXF = sb(p4, "IDXF", [128, 16, 16], F32)
    CS = sb(p4, "CS", [128, 8, 16], F32)
    CI = sb(p4, "CI", [128, 8, 16], U32)
    II = sb(p4, "II", [128, 8, 16], U32)
    IIF = sb(p4, "IIF", [128, 2, 8, 16], F32)
    SELr = [sb(p4, "SEL%d" % i, [128, 3, 128], F32) for i in range(2)]
    sm = sb(p4, "sm", [128, 8, 4], F32)
    ABoh = [sb(p4, "ABoh%d" % i, [128, 2, 16, 128], BF16) for i in range(2)]
    Boh = [sb(p4, "Boh%d" % i, [128, 16, 128], BF16) for i in range(2)]
    NUV = 7
    UV = [sb(p4, "UV%d" % i, [128, 2048], BF16) for i in range(NUV)]
    HG = [sb(p4, "HG%d" % i, [128, TG], BF16) for i in range(4)]
    WW = [sb(p4, "WW%d" % i, [128, TG], BF16) for i in range(4)]
    ybr = [sb(p4, "yb4_%d" % i, [128, 1024], F32) for i in range(2)]
    st = sb(p4, "st4", [128, 8], F32)
    (t_Gd, t_big, t_SC2, t_M1, t_IDX, t_CS, t_CI, t_II, t_sm, t_st) = trks(10)
    junk, t_junk = BIG[:, 0:1024], t_big
    t_ybr = trks(2)
    t_SELr = trks(2)
    t_x1r, t_x1Tr, t_SELT = trks(2), trks(2), trks(4)
    t_A, t_B, t_Bt = trks(2), trks(2), trks(2)
    t_UV, t_HG, t_WW = trks(7), trks(4), trks(4)
    t_pBh = trks(4)
    pA = psb(p4, "p4A", [128, 2048])
    pB = [psb(p4, "p4B%d" % i) for i in range(2)]
    pC = [psb(p4, "p4C%d" % i) for i in range(2)]
    t_pA, = trks(1)
    t_pB, t_pC = trks(2), trks(2)
    uv_v = uv_s.rearrange("(a p) f -> p a f", p=128)

    def build_sel(g):
        ops = []
        late = []

        def op(f, *a, **k):
            ops.append(lambda: f(*a, **k))

        def op_late(f, *a, **k):
            late.append(lambda: f(*a, **k))
        gb = g % 2
        xtb, t_xtb = x1Tr[gb], t_x1Tr[gb]
        for tl in range(2):
            ti = g * 2 + tl
            xb, t_xb = x1r[tl], t_x1r[tl]
            SELT, t_selt = SELTr[ti % 2], t_SELT[ti % 2]
            SEL, t_SEL = SELr[tl], t_SELr[tl]
            op(S.dma, "sp", xb[:], x1_s[ti * 128:(ti + 1) * 128, :], (), [t_xb])
            for hf in range(2):
                for k4 in range(4):
                    k = hf * 4 + k4
                    op(S.tr, pC[1][:, k4 * 128:(k4 + 1) * 128], xb[:, k * 128:(k + 1) * 128], ident[:],
                       [t_xb, t_const], [t_pC[1]])
                op(S.cp, "dve", xtb[:, hf * 4:(hf + 1) * 4, tl * 128:(tl + 1) * 128],
                   pC[1][:].rearrange("p (a b) -> p a b", b=128), [t_pC[1]], [t_xtb])
            for pc in range(4):
                for k in range(8):
                    op(S.mm, pC[1][:], xtb[:, k, tl * 128:(tl + 1) * 128], Wsc[:, k, pc * 512:(pc + 1) * 512],
                       k == 0, k == 7, [t_xtb, t_Wsc], [t_pC[1]])
                op(S.cp, "dve", BIG[:, pc * 512:(pc + 1) * 512], pC[1][:], [t_pC[1]], [t_big])
            for e_ in range(16):
                op(S.emit, "dve", (lambda e, e_=e_: e.max(M1[:, e_, 0:8], SC[:, e_, :])), [t_big], [t_M1])
                op(S.emit, "dve", (lambda e, e_=e_: e.match_replace(SC2[:, 0:128], M1[:, e_, 0:8], SC[:, e_, :], -1e30)),
                   [t_big, t_M1], [t_SC2])
                op(S.emit, "dve", (lambda e, e_=e_: e.max(M1[:, e_, 8:16], SC2[:, 0:128])), [t_SC2], [t_M1])
                op(S.emit, "dve", (lambda e, e_=e_: e.max_index(IDX[:, e_, 0:8], M1[:, e_, 0:8], SC[:, e_, :])),
                   [t_big, t_M1], [t_IDX])
                op(S.emit, "dve", (lambda e, e_=e_: e.max_index(IDX[:, e_, 8:16], M1[:, e_, 8:16], SC[:, e_, :])),
                   [t_big, t_M1], [t_IDX])
            op(S.cp, "dve", IDXF[:], IDX[:], [t_IDX], [t_IDX])
            M1v = M1[:].rearrange("p (h two) k -> p h two k", two=2)
            op(S.tt, "dve", CAND.rearrange("p h (i j) -> p h i j", j=16),
               M1v[:, :, 0, :].unsqueeze(3).to_broadcast([128, 8, 16, 16]),
               M1v[:, :, 1, :].unsqueeze(2).to_broadcast([128, 8, 16, 16]), ALU.add, [t_M1], [t_big])
            for h in range(8):
                op(S.emit, "dve", (lambda e, h=h: e.max(CS[:, h, 0:8], CAND[:, h, :])), [t_big], [t_CS])
                op(S.emit, "dve", (lambda e, h=h: e.match_replace(SC2[:], CS[:, h, 0:8], CAND[:, h, :], -1e30)),
                   [t_big, t_CS], [t_SC2])
                op(S.emit, "dve", (lambda e, h=h: e.max(CS[:, h, 8:16], SC2[:])), [t_SC2], [t_CS])
                op(S.emit, "dve", (lambda e, h=h: e.max_index(CI[:, h, 0:8], CS[:, h, 0:8], CAND[:, h, :])),
                   [t_big, t_CS], [t_CI])
                op(S.emit, "dve", (lambda e, h=h: e.max_index(CI[:, h, 8:16], CS[:, h, 8:16], CAND[:, h, :])),
                   [t_big, t_CS], [t_CI])
            op(S.emit, "dve", (lambda e: e.tensor_single_scalar(II[:], CI[:], 4, ALU.logical_shift_right)), [t_CI], [t_II])
            op(S.cp, "dve", IIF[:, 0], II[:], [t_II], [t_II])
            op(S.emit, "dve", (lambda e: e.tensor_single_scalar(II[:], CI[:], 15, ALU.bitwise_and)), [t_CI, t_II], [t_II])
            op(S.cp, "dve", IIF[:, 1], II[:], [t_II], [t_II])
            IDXv = IDXF[:].rearrange("p (h two) k -> p h two k", two=2)
            for pp_ in range(2):
                op(S.tt, "dve", OH, IIF[:, pp_].unsqueeze(3).to_broadcast([128, 8, 16, 16]),
                   iota_f[:, 0:16].unsqueeze(1).unsqueeze(1).to_broadcast([128, 8, 16, 16]), ALU.is_equal,
                   [t_II, t_const, t_big], [t_big])
                op(S.tt, "dve", OH, OH, IDXv[:, :, pp_, :].unsqueeze(2).to_broadcast([128, 8, 16, 16]), ALU.mult,
                   [t_big, t_IDX], [t_big])
                op(S.emit, "dve", (lambda e, pp_=pp_, SEL=SEL: e.tensor_reduce(
                    SEL[:, pp_, :], OH.rearrange("p h k i -> p (h k) i"), AX.X, ALU.add)), [t_big], [t_SEL])
            op(S.emit, "dve", (lambda e: e.tensor_reduce(sm[:, :, 0], CS[:], AX.X, ALU.max)), [t_CS], [t_sm])
            op(S.tt, "dve", CS[:], CS[:], sm[:, :, 0:1].to_broadcast([128, 8, 16]), ALU.subtract, [t_CS, t_sm], [t_CS])
            op(S.actf, CS[:], CS[:], AF.Exp, [t_CS], [t_CS])
            op(S.emit, "dve", (lambda e: e.tensor_reduce(sm[:, :, 1], CS[:], AX.X, ALU.add)), [t_CS, t_sm], [t_sm])
            op(S.emit, "dve", (lambda e: e.reciprocal(sm[:, :, 2], sm[:, :, 1])), [t_sm], [t_sm])
            op(S.tt, "dve", SEL[:, 2, :].rearrange("p (h k) -> p h k", k=16), CS[:],
               sm[:, :, 2:3].to_broadcast([128, 8, 16]), ALU.mult, [t_CS, t_sm], [t_SEL])
            for j in range(3):
                op_late(S.tr, pC[1][:, j * 128:(j + 1) * 128], SEL[:, j, :], ident[:], [t_SEL, t_const], [t_pC[1]])
            op_late(S.cp, "dve", SELT[:].rearrange("p a b -> p (a b)"), pC[1][:, 0:384], [t_pC[1]], [t_selt])
        return ops, late

    def gd_build(g, pend):
        qi = 0
        pi = [0]

        def pop(n):
            for _ in range(n):
                if pi[0] < len(pend):
                    pend[pi[0]]()
                    pi[0] += 1
        for tl in range(2):
            ti = g * 2 + tl
            SELT, t_selt = SELTr[ti % 2], t_SELT[ti % 2]
            for q in range(8):
                b2 = qi % 2
                qi += 1
                tq = slice(q * 16, (q + 1) * 16)
                S.tt("dve", ABoh[b2][:], iota_f[:].unsqueeze(1).unsqueeze(1).to_broadcast([128, 2, 16, 128]),
                     SELT[:, 0:2, tq].unsqueeze(3).to_broadcast([128, 2, 16, 128]), ALU.is_equal,
                     [t_const, t_selt], [t_A[b2]])
                S.tt("pool" if q % 4 != 3 else "dve", Boh[b2][:], ABoh[b2][:, 1],
                     SELT[:, 2, tq].unsqueeze(2).to_broadcast([128, 16, 128]), ALU.mult,
                     [t_A[b2], t_selt], [t_B[b2]])
                for tt_ in range(16):
                    S.mm(pA[:, tt_ * 128:(tt_ + 1) * 128], Boh[b2][:, tt_, :], ABoh[b2][:, 0, tt_, :], True, True,
                         [t_A[b2], t_B[b2]], [t_pA], sig=(tt_ == 15))
                c0 = tl * 128 + q * 16
                S.cp("act", Gd[:, c0:c0 + 16, :], pA[:].rearrange("p (t a) -> p t a", a=128), [t_pA], [t_Gd])
                pop(2)
        pop(len(pend))

    SKEW = 2
    hbank = [pB[0], pB[1], pC[0]]

    def dense(g, extra_ops, late_ops):
        gb = g % 2
        xtb, t_xtb = x1Tr[gb], t_x1Tr[gb]
        per = -(-len(extra_ops) // 112) if extra_ops else 0
        pos = 0

        def stage_h(a):
            r4 = a % 4
            r3 = a % 3
            r5 = a % NUV
            S.dma("sp", UV[r5][:], uv_v[:, a, :], (), [t_UV[r5]])
            hp = hbank[r3][:, 0:TG]
            for k in range(8):
                S.mm(hp, UV[r5][:, k * 128:(k + 1) * 128], xtb[:, k, :], k == 0, k == 7,
                     [t_UV[r5], t_xtb], [t_pBh[r3]])
            S.actf(HG[r4][:], hp, AF.Gelu, [t_pBh[r3]], [t_HG[r4]])
            S.tt("pool", WW[r4][:], HG[r4][:], Gd[:, :, a], ALU.mult, [t_HG[r4], t_Gd], [t_WW[r4]])

        def stage_o(a):
            r4 = a % 4
            r5 = a % NUV
            for tl in range(2):
                for hf in range(2):
                    S.mm(pA[:, (tl * 2 + hf) * 512:(tl * 2 + hf + 1) * 512], WW[r4][:, tl * 128:(tl + 1) * 128],
                         UV[r5][:, 1024 + hf * 512:1024 + (hf + 1) * 512], a == 0, a == 127, [t_WW[r4], t_UV[r5]], [t_pA],
                         sig=(tl == 1 and hf == 1))

        for a in range(128 + SKEW):
            if a < 128:
                stage_h(a)
            if a >= SKEW:
                stage_o(a - SKEW)
            for _ in range(per):
                if pos < len(extra_ops):
                    extra_ops[pos]()
                    pos += 1
        while pos < len(extra_ops):
            extra_ops[pos]()
            pos += 1
        for f in late_ops:
            f()

    def final(g):
        gb = g % 2
        th = []
        for tl in range(2):
            ti = g * 2 + tl
            S.dma("sp", ybr[tl][:], x1_s[ti * 128:(ti + 1) * 128, :], (), [t_ybr[tl]])
        for tl in range(2):
            S.stt("dve", ybr[tl][:], ybr[tl][:], ALPHA, pA[:, tl * 1024:(tl + 1) * 1024], ALU.mult, ALU.add,
                  [t_ybr[tl], t_pA], [t_ybr[tl]])
        for tl in range(2):
            ti = g * 2 + tl
            rec = Rec(S)
            layer_norm(rec, ybr[tl], t_ybr[tl], st, t_st, junk, t_junk, lng, lnb, t_ln)
            th.extend(rec.ops)
            th.append(lambda tl=tl, ti=ti: S.dma("sp", out_d[ti * 128:(ti + 1) * 128, :], ybr[tl][:], [t_ybr[tl]], ()))
        return th

    o0, l0 = build_sel(0)
    for f in o0 + l0:
        f()
    pend = []
    for g in range(NG):
        gd_build(g, pend)
        o1, l1 = build_sel(g + 1) if g + 1 < NG else ([], [])
        dense(g, o1, l1)
        pend = final(g)
    for f in pend:
        f()


def host_inputs(inputs, b):
    f = np.float32
    x = np.ascontiguousarray(inputs["x"][b], dtype=f)
    cp = np.stack([inputs["conv_w"][0][0], inputs["conv_w"][0][1], inputs["conv_w"][0][2], inputs["conv_w"][0][3],
                   inputs["conv_b"][0], inputs["lru_ba"][0], inputs["lru_bx"][0], inputs["lru_lambda"][0]], axis=-1)
    cpar = np.ascontiguousarray(cp.reshape(8, 128, 8).transpose(1, 0, 2), dtype=f)
    keys = inputs["peer_keys"][0]
    keysT = np.ascontiguousarray(keys.reshape(16, 128, 128).transpose(2, 0, 1), dtype=f)
    u = inputs["peer_u"][0]
    u_l = np.ascontiguousarray(u.reshape(128, 128, 8, 128).transpose(0, 3, 2, 1), dtype=f).reshape(16384, 1024)
    lnp = np.stack([inputs["ln1_g"][0], inputs["ln1_b"][0], inputs["ln2_g"][0], inputs["ln2_b"][0]], axis=0)
    return {
        "xT": np.ascontiguousarray(x.T),
        "x": x,
        "memT": np.ascontiguousarray(inputs["mem"][b].T, dtype=f),
        "w_in": np.ascontiguousarray(inputs["w_in"][0], dtype=f),
        "cpar": cpar,
        "lru_wa": np.ascontiguousarray(inputs["lru_wa"][0], dtype=f),
        "lru_wx": np.ascontiguousarray(inputs["lru_wx"][0], dtype=f),
        "w_mem_kv": np.ascontiguousarray(inputs["w_mem_kv"][0], dtype=f),
        "w_br_attn": np.ascontiguousarray(inputs["w_br_attn"][0], dtype=f),
        "w_br_lru": np.ascontiguousarray(inputs["w_br_lru"][0], dtype=f),
        "w_br_mem": np.ascontiguousarray(inputs["w_br_mem"][0], dtype=f),
        "w_out": np.ascontiguousarray(inputs["w_out"][0], dtype=f),
        "b_gate": np.ascontiguousarray(inputs["b_gate"][0].reshape(1, 3072), dtype=f),
        "lnp": np.ascontiguousarray(lnp, dtype=f),
        "peer_wqT": np.ascontiguousarray(inputs["peer_wq"][0].T, dtype=f),
        "keysT": keysT,
        "u_l": u_l,
        "peer_v": np.ascontiguousarray(inputs["peer_v"][0], dtype=f),
    }


def kernel(**inputs):
    inputs = {k: np.asarray(v) for k, v in inputs.items()}
    nc = build_program()
    shared = host_inputs(inputs, 0)
    in_maps = []
    for b in range(8):
        m = dict(shared)
        x = np.ascontiguousarray(inputs["x"][b], dtype=np.float32)
        m["x"] = x
        m["xT"] = np.ascontiguousarray(x.T)
        m["memT"] = np.ascontiguousarray(inputs["mem"][b].T, dtype=np.float32)
        in_maps.append(m)
    res = run_bass_kernel_spmd(nc, in_maps, core_ids=list(range(8)))
    out = np.stack([np.asarray(r["out"]) for r in res.results], axis=0)
    return out.astype(np.float32)
```
